# Optimizing a Trainium2 kernel written in Bass

```python
import jax, jax.numpy as jnp
from jax import lax
import numpy as np

D_MODEL = 1024
BATCH = 32
SEQ = 2048
DEPTH = 2
DEC_BATCH = 8
DEC_SEQ = 8192
PAST_LEN = 128

N_META = 16
GRID_W = 64
NA_HEADS = 16
NA_HEAD_DIM = 64
NA_WIDTH = NA_HEADS * NA_HEAD_DIM
NA_KH_MAX = 8
NA_KW = 16
NA_QBLK = 16
NA_KBLK = NA_QBLK + NA_KW
LRU_WIDTH = 1024
LRU_BLOCKS = 16
LRU_BLOCK_DIM = LRU_WIDTH // LRU_BLOCKS
CONV_W = 4
CONV_LEFT = 2
LRU_C = 8.0
D_FF = 2816
RMS_EPS = 1e-6
MASK_VALUE = -1e30
IN_WIDTH = 3 * NA_WIDTH + 2 * LRU_WIDTH + 2 * D_MODEL
IN_SPLITS = [NA_WIDTH, 2 * NA_WIDTH, 3 * NA_WIDTH, 3 * NA_WIDTH + LRU_WIDTH,
             3 * NA_WIDTH + 2 * LRU_WIDTH, 3 * NA_WIDTH + 2 * LRU_WIDTH + D_MODEL]

kernel_name = "hybrid_natten_rglru_macaron_encoder"


def rmsnorm(x, g):
    xf = x.astype(jnp.float32)
    inv = lax.rsqrt(jnp.mean(xf * xf, axis=-1, keepdims=True) + RMS_EPS)
    return (xf * inv).astype(x.dtype) * g


def swiglu(x, wg, wu, wd):
    return (jax.nn.silu(x @ wg) * (x @ wu)) @ wd


def neighbourhood_attention(q, k, v, rel_bias):
    B, L, H, dh = q.shape
    T = L - N_META
    rows = T // GRID_W
    kh = min(NA_KH_MAX, rows)
    q = q * (dh ** -0.5)
    qm, km, vm = q[:, :N_META], k[:, :N_META], v[:, :N_META]
    qg = q[:, N_META:].reshape(B, rows, GRID_W, H, dh)
    kg = k[:, N_META:].reshape(B, rows, GRID_W, H, dh)
    vg = v[:, N_META:].reshape(B, rows, GRID_W, H, dh)

    s_mm = jnp.einsum('bqhd,bkhd->bhqk', qm, km).astype(jnp.float32)
    p_mm = jax.nn.softmax(s_mm, axis=-1).astype(v.dtype)
    out_meta = jnp.einsum('bhqk,bkhd->bqhd', p_mm, vm)

    n_cb = GRID_W // NA_QBLK
    qcol = np.arange(GRID_W).reshape(n_cb, NA_QBLK)
    cs = np.clip(qcol - NA_KW // 2, 0, GRID_W - NA_KW)
    base = np.minimum(cs[:, 0], GRID_W - NA_KBLK)
    kcol = base[:, None] + np.arange(NA_KBLK)
    col_ok = (kcol[:, None, :] >= cs[:, :, None]) & (kcol[:, None, :] < cs[:, :, None] + NA_KW)
    dcol_idx = np.clip(kcol[:, None, :] - qcol[:, :, None] + NA_KW - 1, 0, 2 * NA_KW - 2)
    col_mask = col_ok[:, :, None, :]

    def row_step(r):
        rs = jnp.clip(r - kh // 2, 0, rows - kh)
        k_band = lax.dynamic_slice_in_dim(kg, rs, kh, axis=1)
        v_band = lax.dynamic_slice_in_dim(vg, rs, kh, axis=1)
        k_blk = k_band[:, :, kcol]
        v_blk = v_band[:, :, kcol]
        q_row = lax.dynamic_index_in_dim(qg, r, axis=1, keepdims=False)
        q_row = q_row.reshape(B, n_cb, NA_QBLK, H, dh)
        s_loc = jnp.einsum('bnqhd,binwhd->bhnqiw', q_row, k_blk).astype(jnp.float32)
        drow = rs + jnp.arange(kh) - r + NA_KH_MAX - 1
        bias = rel_bias[:, drow[None, None, :, None], dcol_idx[:, :, None, :]]
        s_loc = jnp.where(col_mask, s_loc + bias[None].astype(jnp.float32), MASK_VALUE)
        s_met = jnp.einsum('bnqhd,bkhd->bhnqk', q_row, km).astype(jnp.float32)
        s_all = jnp.concatenate([s_loc.reshape(B, H, n_cb, NA_QBLK, kh * NA_KBLK), s_met], axis=-1)
        p = jax.nn.softmax(s_all, axis=-1).astype(v.dtype)
        p_loc = p[..., :kh * NA_KBLK].reshape(B, H, n_cb, NA_QBLK, kh, NA_KBLK)
        p_met = p[..., kh * NA_KBLK:]
        o = (jnp.einsum('bhnqiw,binwhd->bnqhd', p_loc, v_blk)
             + jnp.einsum('bhnqk,bkhd->bnqhd', p_met, vm))
        return o.reshape(B, GRID_W, H, dh)

    out_grid = lax.map(row_step, jnp.arange(rows))
    out_grid = jnp.moveaxis(out_grid, 0, 1).reshape(B, T, H, dh)
    return jnp.concatenate([out_meta, out_grid], axis=1)


def centred_dwconv(x, w, b):
    L = x.shape[1]
    xp = jnp.pad(x, ((0, 0), (CONV_LEFT, CONV_W - 1 - CONV_LEFT), (0, 0)))
    y = b
    for j in range(CONV_W):
        y = y + xp[:, j:j + L] * w[j]
    return y


def rg_lru(x, wa, ba, wx, bx, lam, reverse):
    B, L, _ = x.shape
    xb = x.reshape(B, L, LRU_BLOCKS, LRU_BLOCK_DIM)
    r = jax.nn.sigmoid(jnp.einsum('btgi,gij->btgj', xb, wa).reshape(B, L, LRU_WIDTH) + ba)
    i = jax.nn.sigmoid(jnp.einsum('btgi,gij->btgj', xb, wx).reshape(B, L, LRU_WIDTH) + bx)
    log_a = -LRU_C * r.astype(jnp.float32) * jax.nn.softplus(-lam.astype(jnp.float32))
    a = jnp.exp(log_a)
    u = jnp.sqrt(-jnp.expm1(2.0 * log_a)) * (i * x).astype(jnp.float32)

    def combine(left, right):
        a1, b1 = left
        a2, b2 = right
        return a1 * a2, a2 * b1 + b2

    _, h = lax.associative_scan(combine, (a, u), axis=1, reverse=reverse)
    return h.astype(x.dtype)


def hybrid_mixer(h, w_in, rel_bias, conv_w, conv_b, lru_wa, lru_ba, lru_wx, lru_bx, lru_lambda,
                 w_na_proj, w_lru_proj, w_out):
    B, L, _ = h.shape
    z = h @ w_in
    q, k, v, xr, yr, g_na, g_lru = jnp.split(z, IN_SPLITS, axis=-1)
    na = neighbourhood_attention(q.reshape(B, L, NA_HEADS, NA_HEAD_DIM),
                                 k.reshape(B, L, NA_HEADS, NA_HEAD_DIM),
                                 v.reshape(B, L, NA_HEADS, NA_HEAD_DIM), rel_bias)
    na = na.reshape(B, L, NA_WIDTH) @ w_na_proj
    xc = centred_dwconv(xr, conv_w, conv_b)
    hr = (rg_lru(xc, lru_wa[0], lru_ba[0], lru_wx[0], lru_bx[0], lru_lambda[0], False)
          + rg_lru(xc, lru_wa[1], lru_ba[1], lru_wx[1], lru_bx[1], lru_lambda[1], True))
    lru = (jax.nn.gelu(yr) * hr) @ w_lru_proj
    merged = jax.nn.sigmoid(g_na) * na + jax.nn.sigmoid(g_lru) * lru
    return merged @ w_out


def encode(x, meta_tokens, norm_ffn1, ffn1_w_gate, ffn1_w_up, ffn1_w_down, norm_mix, w_in,
           na_rel_bias, conv_w, conv_b, lru_wa, lru_ba, lru_wx, lru_bx, lru_lambda,
           w_na_proj, w_lru_proj, w_out, norm_ffn2, ffn2_w_gate, ffn2_w_up, ffn2_w_down, final_norm):
    B = x.shape[0]
    meta = jnp.broadcast_to(meta_tokens.astype(x.dtype)[None], (B, N_META, x.shape[-1]))
    h = jnp.concatenate([meta, x], axis=1)
    for l in range(DEPTH):
        h = h + 0.5 * swiglu(rmsnorm(h, norm_ffn1[l]), ffn1_w_gate[l], ffn1_w_up[l], ffn1_w_down[l])
        h = h + hybrid_mixer(rmsnorm(h, norm_mix[l]), w_in[l], na_rel_bias[l], conv_w[l], conv_b[l],
                             lru_wa[l], lru_ba[l], lru_wx[l], lru_bx[l], lru_lambda[l],
                             w_na_proj[l], w_lru_proj[l], w_out[l])
        h = h + 0.5 * swiglu(rmsnorm(h, norm_ffn2[l]), ffn2_w_gate[l], ffn2_w_up[l], ffn2_w_down[l])
    h = rmsnorm(h, final_norm)
    return h[:, N_META:]


def setup_inputs(seed: int = 0) -> dict:
    key = jax.random.key(seed)
    ks = jax.random.split(key, 32)
    f32 = jnp.float32

    def nrm(k, shape, scale):
        return jax.random.normal(k, shape, f32) * scale

    u = jax.random.uniform(ks[20], (DEPTH, 2, LRU_WIDTH), f32, 0.9, 0.999)
    s = u ** (1.0 / LRU_C)
    lru_lambda = jnp.log(s) - jnp.log1p(-s)
    return {
        "x_prompt": nrm(ks[0], (BATCH, SEQ, D_MODEL), 1.0),
        "x_sample": nrm(ks[1], (DEC_BATCH, DEC_SEQ, D_MODEL), 1.0),
        "meta_tokens": nrm(ks[2], (N_META, D_MODEL), 1.0),
        "norm_ffn1": 1.0 + nrm(ks[3], (DEPTH, D_MODEL), 0.02),
        "ffn1_w_gate": nrm(ks[4], (DEPTH, D_MODEL, D_FF), D_MODEL ** -0.5),
        "ffn1_w_up": nrm(ks[5], (DEPTH, D_MODEL, D_FF), D_MODEL ** -0.5),
        "ffn1_w_down": nrm(ks[6], (DEPTH, D_FF, D_MODEL), D_FF ** -0.5),
        "norm_mix": 1.0 + nrm(ks[7], (DEPTH, D_MODEL), 0.02),
        "w_in": nrm(ks[8], (DEPTH, D_MODEL, IN_WIDTH), D_MODEL ** -0.5),
        "na_rel_bias": nrm(ks[9], (DEPTH, NA_HEADS, 2 * NA_KH_MAX - 1, 2 * NA_KW - 1), 0.1),
        "conv_w": nrm(ks[10], (DEPTH, CONV_W, LRU_WIDTH), CONV_W ** -0.5),
        "conv_b": nrm(ks[11], (DEPTH, LRU_WIDTH), 0.01),
        "lru_wa": nrm(ks[12], (DEPTH, 2, LRU_BLOCKS, LRU_BLOCK_DIM, LRU_BLOCK_DIM), LRU_BLOCK_DIM ** -0.5),
        "lru_ba": nrm(ks[13], (DEPTH, 2, LRU_WIDTH), 0.01),
        "lru_wx": nrm(ks[14], (DEPTH, 2, LRU_BLOCKS, LRU_BLOCK_DIM, LRU_BLOCK_DIM), LRU_BLOCK_DIM ** -0.5),
        "lru_bx": nrm(ks[15], (DEPTH, 2, LRU_WIDTH), 0.01),
        "lru_lambda": lru_lambda,
        "w_na_proj": nrm(ks[16], (DEPTH, NA_WIDTH, D_MODEL), NA_WIDTH ** -0.5),
        "w_lru_proj": nrm(ks[17], (DEPTH, LRU_WIDTH, D_MODEL), LRU_WIDTH ** -0.5),
        "w_out": nrm(ks[18], (DEPTH, D_MODEL, D_MODEL), D_MODEL ** -0.5),
        "norm_ffn2": 1.0 + nrm(ks[19], (DEPTH, D_MODEL), 0.02),
        "ffn2_w_gate": nrm(ks[21], (DEPTH, D_MODEL, D_FF), D_MODEL ** -0.5),
        "ffn2_w_up": nrm(ks[22], (DEPTH, D_MODEL, D_FF), D_MODEL ** -0.5),
        "ffn2_w_down": nrm(ks[23], (DEPTH, D_FF, D_MODEL), D_FF ** -0.5),
        "final_norm": 1.0 + nrm(ks[24], (D_MODEL,), 0.02),
    }


def reference(x_prompt, x_sample, meta_tokens, norm_ffn1, ffn1_w_gate, ffn1_w_up, ffn1_w_down,
              norm_mix, w_in, na_rel_bias, conv_w, conv_b, lru_wa, lru_ba, lru_wx, lru_bx,
              lru_lambda, w_na_proj, w_lru_proj, w_out, norm_ffn2, ffn2_w_gate, ffn2_w_up,
              ffn2_w_down, final_norm):
    y_prompt = encode(x_prompt, meta_tokens, norm_ffn1, ffn1_w_gate, ffn1_w_up, ffn1_w_down, norm_mix,
                      w_in, na_rel_bias, conv_w, conv_b, lru_wa, lru_ba, lru_wx, lru_bx, lru_lambda,
                      w_na_proj, w_lru_proj, w_out, norm_ffn2, ffn2_w_gate, ffn2_w_up, ffn2_w_down,
                      final_norm)
    y_sample = encode(x_sample, meta_tokens, norm_ffn1, ffn1_w_gate, ffn1_w_up, ffn1_w_down, norm_mix,
                      w_in, na_rel_bias, conv_w, conv_b, lru_wa, lru_ba, lru_wx, lru_bx, lru_lambda,
                      w_na_proj, w_lru_proj, w_out, norm_ffn2, ffn2_w_gate, ffn2_w_up, ffn2_w_down,
                      final_norm)
    return (y_prompt, y_sample)
```

```python
import contextlib
import numpy as np
import concourse.bass as bass
import concourse.mybir as mybir
from concourse.bass_utils import run_bass_kernel_spmd

F32 = mybir.dt.float32
BF16 = mybir.dt.bfloat16
AF = mybir.ActivationFunctionType
ALU = mybir.AluOpType

D = 1024
DFF = 2816
NFF = DFF // 128
DEPTH = 2
NMETA = 16
GW = 64
INW = 7168
EPS = 1e-6
NCORES = 8
TT = 512

V_G1, V_GM, V_G2, V_CW, V_CB, V_BA, V_BX, V_LAM, V_GF = 0, 8, 16, 24, 56, 64, 80, 96, 112
NV = 120


class Ev:
    __slots__ = ("ds", "val", "eng")

    def __init__(self, ds, val, eng):
        self.ds, self.val, self.eng = ds, val, eng

    def value(self):
        return self.val if self.val is not None else 16 * self.ds.cnt


class DSem:
    def __init__(self, sem):
        self.sem, self.cnt = sem, 0


class Res:
    def __init__(self, name=""):
        self.name = name
        self.w = None
        self.rs = {}


class Sched:
    ENGS = ("sp", "act", "dve", "pool", "pe")

    def __init__(self, nc, es, tag):
        self.nc, self.es, self.tag = nc, es, tag
        self.q = {e: [] for e in self.ENGS}
        self.sems = []
        self.esem = {e: DSem(self._sem(f"{tag}_s_{e}")) for e in ("act", "dve", "pool", "pe")}
        self.dsems = []
        self.nalloc = 0

    def _sem(self, name):
        h = self.nc.alloc_semaphore(name=name)
        self.sems.append(h)
        return h

    def dsem(self):
        self.nalloc += 1
        d = DSem(self._sem(f"{self.tag}_d{self.nalloc}"))
        self.dsems.append(d)
        return d

    def sb(self, name, shape, dt):
        return self.es.enter_context(self.nc.sbuf_tensor(f"{self.tag}_{name}", shape, dt))

    def ps(self, name, shape, dt=F32):
        return self.es.enter_context(self.nc.psum_tensor(f"{self.tag}_{name}", shape, dt))

    def _deps(self, eng, rd, wr):
        waits = []
        for r in rd:
            if r.w is not None:
                waits.append((r.w, True))
        for r in wr:
            if r.w is not None:
                waits.append((r.w, False))
            for ev in r.rs.values():
                waits.append((ev, False))
        return waits

    def _commit(self, ev, rd, wr):
        for r in rd:
            r.rs[id(ev.ds)] = ev
        for r in wr:
            r.w = ev
            r.rs = {}

    def op(self, eng, fn, rd=(), wr=()):
        waits = self._deps(eng, rd, wr)
        ds = self.esem[eng]
        ds.cnt += 1
        ev = Ev(ds, ds.cnt, eng)
        self.q[eng].append((fn, waits, ev, 1))
        self._commit(ev, rd, wr)
        return ev

    def dma(self, q, out, in_, ds, rd=(), wr=(), bulk=False, **kw):
        waits = self._deps(q, rd, wr)
        ds.cnt += 1
        ev = Ev(ds, None if bulk else 16 * ds.cnt, "dma")
        self.q[q].append((lambda e: e.dma_start(out=out, in_=in_, **kw), waits, ev, 16))
        self._commit(ev, rd, wr)
        return ev

    def replay(self):
        nc = self.nc
        finals = [(d.sem, 16 * d.cnt) for d in self.dsems if d.cnt]
        with nc.Block() as block:
            for eng, deco in (("sp", block.sync), ("act", block.scalar), ("dve", block.vector),
                              ("pool", block.gpsimd), ("pe", block.tensor)):
                q = self.q[eng]

                def body(e, q=q, eng=eng):
                    mw = {}
                    for fn, waits, ev, inc in q:
                        for wev, raw in waits:
                            key = id(wev.ds)
                            v = wev.value()
                            if mw.get(key, 0) >= v:
                                continue
                            e.wait_ge(wev.ds.sem, v)
                            mw[key] = v
                        ins = fn(e)
                        ins.then_inc(ev.ds.sem, inc)
                    if eng == "sp":
                        for sem, v in finals:
                            e.wait_ge(sem, v)
                        for en2 in ("act", "dve", "pool", "pe"):
                            d = self.esem[en2]
                            if d.cnt:
                                e.wait_ge(d.sem, d.cnt)

                deco(body)
        nc.all_engine_barrier()
        nc.clear_and_free_semaphores(self.sems)
        nc.all_engine_barrier()


def col_tiles(n, step=512):
    return [(a, min(a + step, n)) for a in range(0, n, step)]


class Builder:
    def __init__(self, n_p, t_p, n_s, t_s, debug=False):
        self.cfg = (n_p, t_p, n_s, t_s)
        self.debug = debug
        self.smax = 2064
        nc = self.nc = bass.Bass("TRN2", target_bir_lowering=False)
        self.seqs = []
        base = 0
        for i in range(n_p):
            self.seqs.append(("p", i, t_p, base))
            base += NMETA + t_p
        for i in range(n_s):
            self.seqs.append(("s", i, t_s, base))
            base += NMETA + t_s
        self.NT = NT = base
        di = lambda name, shape, dt=F32: nc.dram_tensor(name, shape, dt, kind="ExternalInput").ap()
        do = lambda name, shape, dt=F32: nc.dram_tensor(name, shape, dt, kind="ExternalOutput").ap()
        sk = "ExternalOutput" if debug else "Internal"
        dsr = lambda name, shape, dt: nc.dram_tensor(name, shape, dt, kind=sk).ap()
        self.xp = di("xp", [n_p, t_p, D])
        self.xs = di("xs", [n_s, t_s, D])
        self.meta = di("meta", [NMETA, D])
        self.ident = di("ident", [128, 128])
        self.vec = di("vec", [DEPTH, 128, NV])
        self.fbias = di("fbias", [DEPTH, 16, 15, 31])
        self.w = {}
        for nm, shp in (("f1g", [D, DFF]), ("f1u", [D, DFF]), ("f1d", [DFF, D]), ("win", [D, INW]),
                        ("wna", [D, D]), ("wlru", [D, D]), ("wout", [D, D]),
                        ("f2g", [D, DFF]), ("f2u", [D, DFF]), ("f2d", [DFF, D]),
                        ("lwa", [2, 16, 64, 64]), ("lwx", [2, 16, 64, 64])):
            self.w[nm] = di(nm, [DEPTH] + shp)
        self.yp = do("yp", [n_p, t_p, D])
        self.ys = do("ys", [n_s, t_s, D])
        self.H = dsr("H", [D, NT], F32)
        self.QT = dsr("QT", [D, NT], BF16)
        self.KT = dsr("KT", [D, NT], BF16)
        self.V = dsr("V", [NT, D], BF16)
        self.XR = dsr("XR", [D, NT], F32)
        self.GY = dsr("GY", [D, NT], BF16)
        self.SGN = dsr("SGN", [D, NT], BF16)
        self.SGL = dsr("SGL", [D, NT], BF16)
        self.NAT = dsr("NAT", [D, NT], BF16)
        self.LT = dsr("LT", [D, NT], BF16)
        self.HF = dsr("HF", [D, NT], F32)
        self.XC = nc.dram_tensor("XC", [D, NT], F32, kind="Internal").ap()
        self.MSK = nc.dram_tensor("MSK", [DEPTH, 2, 128, 16, 1024], BF16, kind="Internal").ap()
        self.dbg = {}
        if debug:
            for nm in ("H0", "H1", "H2", "H3"):
                self.dbg[nm] = do("dbg_" + nm, [D, NT])

    @staticmethod
    def fm(ap):
        return ap.rearrange("(c p) t -> p c t", p=128)

    def tok_tiles(self):
        return col_tiles(self.NT, TT)

    def norm(self, S, x, R_x, n, gcol, vecs, R_vec, ones, sqs, R_sq, k, ssq, R_ssq, sd, R_sd, rstd, R_rstd, xn, R_xn):
        for c in range(8):
            sq, Rq = sqs[(k * 8 + c) % 2], R_sq[(k * 8 + c) % 2]
            S.op("act", lambda e, c=c, sq=sq: e.activation(out=sq[:, :n], in_=x[:, c, :n], func=AF.Square),
                 rd=[R_x], wr=[Rq])
            S.op("pe", lambda e, c=c, sq=sq: e.matmul(ssq[:, :n], ones[:, :], sq[:, :n], start=(c == 0), stop=(c == 7)),
                 rd=[Rq, self.R_ones], wr=[R_ssq])
        S.op("act", lambda e: e.activation(out=sd[:, :n], in_=ssq[:, :n], func=AF.Sqrt, scale=1.0 / D, bias=self.eps_ap),
             rd=[R_ssq, self.R_ones], wr=[R_sd])
        S.op("dve", lambda e: e.reciprocal(out=rstd[:, :n], in_=sd[:, :n]), rd=[R_sd], wr=[R_rstd])
        for c in range(8):
            S.op("dve", lambda e, c=c: e.scalar_tensor_tensor(out=xn[:, c, :n], in0=x[:, c, :n],
                                                              scalar=vecs[:, gcol + c:gcol + c + 1], in1=rstd[:, :n],
                                                              op0=ALU.mult, op1=ALU.mult),
                 rd=[R_x, R_rstd, R_vec], wr=[R_xn])

    def consts(self, S, l):
        ones = S.sb("ones", [128, 128], BF16)
        vecs = S.sb("vecs", [128, NV], F32)
        eps = S.sb("eps", [128, 1], F32)
        R_ones, R_vec = Res(), Res()
        S.op("pool", lambda e: e.memset(ones[:, :], 1.0), wr=[R_ones])
        S.op("pool", lambda e: e.memset(eps[:, :], EPS), wr=[R_ones])
        S.dma("sp", vecs[:, :], self.vec[l], S.dsem(), wr=[R_vec])
        self.eps_ap = eps[:, 0:1]
        self.R_ones = R_ones
        return ones, R_ones, vecs, R_vec

    def load_w(self, S, dst, src, ds, R, nk, ncols, cstep=1024):
        srcv = src.rearrange("(c p) f -> p c f", p=128)
        for c in range(nk):
            for a, b in col_tiles(ncols, cstep):
                S.dma("pool", dst[:, c, a:b], srcv[:, c, a:b], ds, wr=[], bulk=True)
        R.w = Ev(ds, None, "dma")

    def build_masks(self, S, q):
        rf32 = S.sb("rf32", [128, 16, 15, 64], F32)
        rfull4 = S.sb("rfull", [128, 16, 16, 64], BF16)
        rint4 = S.sb("rint", [128, 16, 16, 64], BF16)
        rfull = rfull4.rearrange("p h j q -> p h (j q)")
        rint = rint4.rearrange("p h j q -> p h (j q)")
        R_rf32, R_rfull, R_rint = Res(), Res(), Res()
        d_st = [S.dsem(), S.dsem()]
        qs = np.arange(GW)
        cs = np.clip(qs - 8, 0, GW - 16)
        rfv = rf32.rearrange("p h j q -> p (h j) q")
        for l in range(DEPTH):
            dmk = S.dsem()
            S.op("pool", lambda e: e.memset(rf32[:, :, :, :], -30000.0), wr=[R_rf32])
            S.op("pool", lambda e: e.memset(rfull4[:, :, :, :], 0.0), wr=[R_rfull])
            first = True
            for kp in range(2):
                for kcol in range(GW):
                    valid = np.nonzero((cs <= kcol) & (kcol < cs + 16))[0]
                    qlo, qhi = int(valid[0]), int(valid[-1])
                    assert len(valid) == qhi - qlo + 1
                    nq = qhi - qlo + 1
                    b0 = 15 - kcol + qlo
                    assert 0 <= b0 and b0 + nq <= 31
                    pp = kp * 64 + kcol
                    S.dma(q, rfv[pp:pp + 1, :, qlo:qhi + 1],
                          self.fbias[l].rearrange("h a b -> (h a) b")[:, b0:b0 + nq].rearrange("(o r) b -> o r b", o=1),
                          dmk, rd=[R_rf32] if first else [], bulk=True)
                    first = False
            R_rf32.w = Ev(dmk, None, "dma")
            S.op("act", lambda e: e.activation(out=rfull4[0:64, :, 0:15, :], in_=rf32[0:64, :, :, :], func=AF.Exp),
                 rd=[R_rf32, R_rfull], wr=[R_rfull])
            S.op("act", lambda e: e.activation(out=rfull4[64:128, :, 1:16, :], in_=rf32[64:128, :, :, :], func=AF.Exp),
                 rd=[R_rf32, R_rfull], wr=[R_rfull])
            S.op("pool", lambda e: e.tensor_copy(out=rint[:, :, :], in_=rfull[:, :, :]), rd=[R_rfull], wr=[R_rint])
            S.op("pool", lambda e: e.memset(rint[0:64, :, 0:4 * 64], 0.0), wr=[R_rint])
            S.op("pool", lambda e: e.memset(rint[0:64, :, 12 * 64:16 * 64], 0.0), wr=[R_rint])
            S.op("pool", lambda e: e.memset(rint[64:128, :, 0:5 * 64], 0.0), wr=[R_rint])
            S.op("pool", lambda e: e.memset(rint[64:128, :, 13 * 64:16 * 64], 0.0), wr=[R_rint])
            S.dma("pool", self.MSK[l, 0], rfull[:, :, :], d_st[0], rd=[R_rfull])
            S.dma("pool", self.MSK[l, 1], rint[:, :, :], d_st[1], rd=[R_rint])

    def phase_init(self):
        nc = self.nc
        with contextlib.ExitStack() as es:
            S = Sched(nc, es, "in")
            self.build_masks(S, "act")
            idt = S.sb("idt", [128, 128], F32)
            R_id = Res()
            S.dma("sp", idt[:, :], self.ident, S.dsem(), wr=[R_id])
            xin = [S.sb(f"xin{i}", [128, 4, D], F32) for i in range(2)]
            xfm = [S.sb(f"xfm{i}", [128, 8, TT], F32) for i in range(2)]
            R_xin = [Res(), Res()]
            R_xfm = [Res(), Res()]
            d_xin = [S.dsem(), S.dsem()]
            d_xfm = [S.dsem(), S.dsem()]
            pst = [S.ps(f"pt{i}", [128, TT]) for i in range(4)]
            R_ps = [Res() for _ in range(4)]
            jobs = []
            for kind, idx, T, base in self.seqs:
                jobs.append((self.meta, base, NMETA))
                src = self.xp if kind == "p" else self.xs
                for a, b in col_tiles(T, TT):
                    jobs.append((src[idx, a:b, :], base + NMETA + a, b - a))
            Hv = self.fm(self.H)
            pc = 0
            for k, (src, t0, n) in enumerate(jobs):
                b = k % 2
                ng = (n + 127) // 128
                if n >= 128:
                    S.dma("sp", xin[b][:, :ng, :], src.rearrange("(g p) d -> p g d", p=128), d_xin[b], wr=[R_xin[b]])
                else:
                    S.dma("sp", xin[b][:n, 0, :], src, d_xin[b], wr=[R_xin[b]])
                for c in range(8):
                    pi = pc % 4
                    pc += 1
                    ps = pst[pi]

                    def tr(e, c=c, ps=ps, b=b, n=n, ng=ng):
                        ins = None
                        for g in range(ng):
                            m = min(128, n - g * 128)
                            ins = e.transpose(ps[:, g * 128:g * 128 + m], xin[b][:m, g, c * 128:(c + 1) * 128], idt[:m, :m])
                        return ins
                    S.op("pe", tr, rd=[R_xin[b], R_id], wr=[R_ps[pi]])
                    eng = "dve"
                    if eng == "dve":
                        S.op("dve", lambda e, c=c, ps=ps, b=b, n=n: e.tensor_copy(out=xfm[b][:, c, :n], in_=ps[:, :n]),
                             rd=[R_ps[pi]], wr=[R_xfm[b]])
                    else:
                        S.op("act", lambda e, c=c, ps=ps, b=b, n=n: e.activation(out=xfm[b][:, c, :n], in_=ps[:, :n], func=AF.Copy),
                             rd=[R_ps[pi]], wr=[R_xfm[b]])
                S.dma("sp", Hv[:, :, t0:t0 + n], xfm[b][:, :, :n], d_xfm[b], rd=[R_xfm[b]])
            S.replay()

    def phase_ffn(self, l, which):
        nc = self.nc
        pre = "f1" if which == 1 else "f2"
        gcol = V_G1 if which == 1 else V_G2
        with contextlib.ExitStack() as es:
            S = Sched(nc, es, f"{pre}{l}")
            ones, R_ones, vecs, R_vec = self.consts(S, l)
            wg = S.sb("wg", [128, 8, DFF], BF16)
            wu = S.sb("wu", [128, 8, DFF], BF16)
            wd = S.sb("wd", [128, NFF, D], BF16)
            R_wd = Res()
            dd = S.dsem()
            blocks = col_tiles(DFF, 1024)
            R_wgu = [Res() for _ in blocks]
            dgu = [S.dsem() for _ in blocks]
            gv = self.w[pre + "g"][l].rearrange("(c p) f -> p c f", p=128)
            uv = self.w[pre + "u"][l].rearrange("(c p) f -> p c f", p=128)
            for bi, (a, b) in enumerate(blocks):
                for c in range(8):
                    S.dma("pool", wg[:, c, a:b], gv[:, c, a:b], dgu[bi], bulk=True)
                    S.dma("pool", wu[:, c, a:b], uv[:, c, a:b], dgu[bi], bulk=True)
                R_wgu[bi].w = Ev(dgu[bi], None, "dma")
            self.load_w(S, wd, self.w[pre + "d"][l], dd, R_wd, NFF, D)
            x = [S.sb(f"x{i}", [128, 8, TT], F32) for i in range(2)]
            R_x = [Res(), Res()]
            d_x = [S.dsem(), S.dsem()]
            d_xs = [S.dsem(), S.dsem()]
            xn = S.sb("xn", [128, 8, TT], BF16)
            R_xn = Res()
            hT = S.sb("hT", [128, NFF, TT], BF16)
            R_h = [Res() for _ in range(NFF)]
            sqs = [S.sb(f"sq{i}", [128, TT], BF16) for i in range(2)]
            R_sq = [Res(), Res()]
            sg = [S.sb(f"sg{i}", [128, TT], F32) for i in range(2)]
            R_sg = [Res(), Res()]
            sd = S.sb("sd", [128, TT], F32)
            rstd = S.sb("rstd", [128, TT], F32)
            R_sd, R_rstd = Res(), Res()
            ssq = S.ps("ssq", [128, TT])
            R_ssq = Res()
            gps = [S.ps(f"g{i}", [128, TT]) for i in range(2)]
            ups = [S.ps(f"u{i}", [128, TT]) for i in range(2)]
            ops = [S.ps(f"o{i}", [128, TT]) for i in range(2)]
            R_g, R_u, R_o = [Res(), Res()], [Res(), Res()], [Res(), Res()]
            Hv = self.fm(self.H)
            tiles = self.tok_tiles()

            def stage_a(k):
                t0, t1 = tiles[k]
                n = t1 - t0
                b = k % 2
                S.dma("sp", x[b][:, :, :n], Hv[:, :, t0:t1], d_x[b], wr=[R_x[b]])
                self.norm(S, x[b], R_x[b], n, gcol, vecs, R_vec, ones, sqs, R_sq, k, ssq, R_ssq, sd, R_sd,
                          rstd, R_rstd, xn, R_xn)

            stage_a(0)
            for k, (t0, t1) in enumerate(tiles):
                n = t1 - t0
                b = k % 2
                for f in range(NFF):
                    pb = f % 2

                    def mmg(e, f=f, pb=pb, n=n):
                        ins = None
                        for c in range(8):
                            ins = e.matmul(gps[pb][:, :n], wg[:, c, f * 128:(f + 1) * 128], xn[:, c, :n],
                                           start=(c == 0), stop=(c == 7))
                        return ins

                    def mmu(e, f=f, pb=pb, n=n):
                        ins = None
                        for c in range(8):
                            ins = e.matmul(ups[pb][:, :n], wu[:, c, f * 128:(f + 1) * 128], xn[:, c, :n],
                                           start=(c == 0), stop=(c == 7))
                        return ins
                    S.op("pe", mmg, rd=[R_wgu[f * 128 // 1024], R_xn], wr=[R_g[pb]])
                    S.op("pe", mmu, rd=[R_wgu[f * 128 // 1024], R_xn], wr=[R_u[pb]])
                    S.op("act", lambda e, pb=pb, n=n: e.activation(out=sg[pb][:, :n], in_=gps[pb][:, :n], func=AF.Silu),
                         rd=[R_g[pb]], wr=[R_sg[pb]])
                    S.op("dve", lambda e, f=f, pb=pb, n=n: e.tensor_tensor(out=hT[:, f, :n], in0=ups[pb][:, :n],
                                                                          in1=sg[pb][:, :n], op=ALU.mult),
                         rd=[R_u[pb], R_sg[pb]], wr=[R_h[f]])
                if k + 1 < len(tiles):
                    stage_a(k + 1)
                for c in range(8):
                    pb = c % 2

                    def mmd(e, c=c, pb=pb, n=n):
                        ins = None
                        for f in range(NFF):
                            ins = e.matmul(ops[pb][:, :n], wd[:, f, c * 128:(c + 1) * 128], hT[:, f, :n],
                                           start=(f == 0), stop=(f == NFF - 1))
                        return ins
                    S.op("pe", mmd, rd=[R_wd] + R_h, wr=[R_o[pb]])
                    S.op("dve", lambda e, c=c, pb=pb, n=n, b=b: e.scalar_tensor_tensor(
                        out=x[b][:, c, :n], in0=ops[pb][:, :n], scalar=0.5, in1=x[b][:, c, :n],
                        op0=ALU.mult, op1=ALU.add), rd=[R_o[pb], R_x[b]], wr=[R_x[b]])
                S.dma("pool", Hv[:, :, t0:t1], x[b][:, :, :n], d_xs[b], rd=[R_x[b]])
            S.replay()

    def phase_mixin(self, l):
        nc = self.nc
        with contextlib.ExitStack() as es:
            S = Sched(nc, es, f"mi{l}")
            ones, R_ones, vecs, R_vec = self.consts(S, l)
            win = S.sb("win", [128, 8, INW], BF16)
            R_wb = [Res() for _ in range(7)]
            winv = self.w["win"][l].rearrange("(c p) f -> p c f", p=128)
            for bi in range(7):
                dwb = S.dsem()
                for c in range(8):
                    S.dma("pool", win[:, c, bi * 1024:(bi + 1) * 1024], winv[:, c, bi * 1024:(bi + 1) * 1024], dwb, bulk=True)
                R_wb[bi].w = Ev(dwb, None, "dma")
            x = S.sb("x", [128, 8, TT], F32)
            R_x, d_x = Res(), S.dsem()
            xn = [S.sb(f"xn{i}", [128, 8, TT], BF16) for i in range(2)]
            R_xn = [Res(), Res()]
            sqs = [S.sb(f"sq{i}", [128, TT], BF16) for i in range(2)]
            R_sq = [Res(), Res()]
            sd = S.sb("sd", [128, TT], F32)
            rstd = S.sb("rstd", [128, TT], F32)
            R_sd, R_rstd = Res(), Res()
            stg = [S.sb(f"stg{i}", [128, 8, TT], BF16) for i in range(3)]
            R_stg = [Res() for _ in range(3)]
            d_stg = [S.dsem() for _ in range(3)]
            vst = S.sb("vst", [128, 4, D], BF16)
            R_vst, d_vst = Res(), S.dsem()
            xrs = S.sb("xrs", [128, 8, TT], F32)
            R_xrs, d_xrs = Res(), S.dsem()
            ssq = S.ps("ssq", [128, TT])
            R_ssq = Res()
            zps = [S.ps(f"z{i}", [128, TT]) for i in range(6)]
            R_z = [Res() for _ in range(6)]
            Hv = self.fm(self.H)
            tiles = self.tok_tiles()

            def load_x(k):
                t0, t1 = tiles[k]
                S.dma("sp", x[:, :, :t1 - t0], Hv[:, :, t0:t1], d_x, wr=[R_x])

            def stage_a(k):
                t0, t1 = tiles[k]
                n = t1 - t0
                self.norm(S, x, R_x, n, V_GM, vecs, R_vec, ones, sqs, R_sq, k, ssq, R_ssq, sd, R_sd,
                          rstd, R_rstd, xn[k % 2], R_xn[k % 2])

            load_x(0)
            stage_a(0)
            zc = 0
            sc = 0
            kinds = [("q", 0, self.QT), ("k", 1024, self.KT), ("xr", 3072, self.XR), ("yr", 4096, self.GY),
                     ("gn", 5120, self.SGN), ("gl", 6144, self.SGL)]
            for k, (t0, t1) in enumerate(tiles):
                n = t1 - t0
                xb, Rb = xn[k % 2], R_xn[k % 2]
                if k + 1 < len(tiles):
                    load_x(k + 1)
                for name, off, dst in kinds:
                    if name == "xr":
                        buf, Rbuf, dbuf = xrs, R_xrs, d_xrs
                    else:
                        si = sc % 3
                        sc += 1
                        buf, Rbuf, dbuf = stg[si], R_stg[si], d_stg[si]
                    for c in range(8):
                        zi = zc % 6
                        zc += 1
                        zp = zps[zi]

                        def mm(e, c=c, zp=zp, n=n, off=off, xb=xb):
                            ins = None
                            for kk in range(8):
                                ins = e.matmul(zp[:, :n], win[:, kk, off + c * 128:off + (c + 1) * 128], xb[:, kk, :n],
                                               start=(kk == 0), stop=(kk == 7))
                            return ins
                        S.op("pe", mm, rd=[R_wb[off // 1024], Rb], wr=[R_z[zi]])
                        if name == "q":
                            S.op("dve", lambda e, c=c, zp=zp, n=n, buf=buf: e.tensor_scalar(
                                out=buf[:, c, :n], in0=zp[:, :n], scalar1=0.125, scalar2=None, op0=ALU.mult),
                                rd=[R_z[zi]], wr=[Rbuf])
                        elif name in ("k", "xr"):
                            S.op("dve", lambda e, c=c, zp=zp, n=n, buf=buf: e.tensor_copy(out=buf[:, c, :n], in_=zp[:, :n]),
                                 rd=[R_z[zi]], wr=[Rbuf])
                        else:
                            fn = AF.Gelu if name == "yr" else AF.Sigmoid
                            S.op("act", lambda e, c=c, zp=zp, n=n, buf=buf, fn=fn: e.activation(
                                out=buf[:, c, :n], in_=zp[:, :n], func=fn), rd=[R_z[zi]], wr=[Rbuf])
                    S.dma("pool", self.fm(dst)[:, :, t0:t1], buf[:, :, :n], dbuf, rd=[Rbuf])
                    if name == "k" and k + 1 < len(tiles):
                        stage_a(k + 1)
                ng = (n + 127) // 128
                for g in range(ng):
                    m = min(128, n - g * 128)
                    for hf in range(2):
                        zi = zc % 6
                        zc += 1
                        zp = zps[zi]

                        def mmv(e, g=g, m=m, hf=hf, zp=zp, xb=xb):
                            ins = None
                            for kk in range(8):
                                ins = e.matmul(zp[:m, :], xb[:, kk, g * 128:g * 128 + m],
                                               win[:, kk, 2048 + hf * 512:2048 + (hf + 1) * 512],
                                               start=(kk == 0), stop=(kk == 7))
                            return ins
                        S.op("pe", mmv, rd=[R_wb[2], Rb], wr=[R_z[zi]])
                        S.op("dve", lambda e, g=g, m=m, hf=hf, zp=zp: e.tensor_copy(
                            out=vst[:m, g, hf * 512:(hf + 1) * 512], in_=zp[:m, :]), rd=[R_z[zi]], wr=[R_vst])
                if n % 128 == 0:
                    S.dma("pool", self.V[t0:t1, :].rearrange("(g p) d -> p g d", p=128), vst[:, :ng, :], d_vst, rd=[R_vst])
                else:
                    S.dma("pool", self.V[t0:t1, :], vst[:n, 0, :], d_vst, rd=[R_vst])
            S.replay()

    def phase_attn(self, l):
        nc = self.nc
        with contextlib.ExitStack() as es:
            S = Sched(nc, es, f"at{l}")
            ones = S.sb("ones", [128, 128], BF16)
            R_ones = Res()
            S.op("pool", lambda e: e.memset(ones[:, :], 1.0), wr=[R_ones])
            rfull = S.sb("rfull", [128, 16, 1024], BF16)
            rint = S.sb("rint", [128, 16, 1024], BF16)
            R_rfull, R_rint = Res(), Res()
            S.dma("sp", rfull[:, :, :], self.MSK[l, 0], S.dsem(), wr=[R_rfull])
            S.dma("sp", rint[:, :, :], self.MSK[l, 1], S.dsem(), wr=[R_rint])

            NS = 8
            kt = [S.sb(f"kt{i}", [128, 8, 128], BF16) for i in range(NS)]
            vt = [S.sb(f"vt{i}", [128, D], BF16) for i in range(NS)]
            R_kv = [Res() for _ in range(NS)]
            d_kv = [S.dsem() for _ in range(NS)]
            ktm = S.sb("ktm", [128, 8, NMETA], BF16)
            vm = S.sb("vm", [NMETA, D], BF16)
            R_m, d_m = Res(), S.dsem()
            qt = [S.sb(f"qt{i}", [128, 8, 256], BF16) for i in range(2)]
            R_q = [Res(), Res()]
            d_q = [S.dsem(), S.dsem()]
            for i in range(2):
                S.op("pool", lambda e, i=i: e.memset(qt[i][:, :, :], 0.0), wr=[R_q[i]])
            ot = [S.sb(f"ot{i}", [128, 8, 128], BF16) for i in range(2)]
            R_ot = [Res(), Res()]
            d_ot = [S.dsem(), S.dsem()]
            eb = [S.sb(f"eb{i}", [128, 6, 256], BF16) for i in range(3)]
            R_eb = [Res() for _ in range(3)]
            rc = [S.sb(f"rc{i}", [128, 256], F32) for i in range(2)]
            R_rc = [Res(), Res()]
            es1 = [S.sb(f"es1{i}", [128, 256], BF16) for i in range(3)]
            es2 = [S.sb(f"es2{i}", [128, 256], BF16) for i in range(3)]
            R_es1 = [Res() for _ in range(3)]
            R_es2 = [Res() for _ in range(3)]
            stp = [S.ps(f"st{i}", [128, 6, 256]) for i in range(2)]
            R_st = [Res(), Res()]
            odp = [S.ps(f"od{i}", [128, 512]) for i in range(2)]
            R_od = [Res(), Res()]
            QTv, KTv, NATv = self.fm(self.QT), self.fm(self.KT), self.fm(self.NAT)
            cnt = dict(q=0, h=0)

            jobs = []
            load_meta_cur = [None]

            def do_queries(qtok0, nq, kcs, p, slot_of, table, R_table, pre=None):
                qi = cnt["q"] % 2
                cnt["q"] += 1
                nk = len(kcs)
                ma = mb = 0
                if nk:
                    d_hi, d_lo = kcs[0] - p, kcs[-1] - p
                    ma, mb = (7 - 2 * d_hi) * 64, (9 - 2 * d_lo) * 64
                    assert mb - ma == nk * 128 and ma >= 0 and mb <= 1024
                for hp in range(8):
                    hi = cnt["h"]
                    cnt["h"] += 1
                    st, Rst = stp[hi % 2], R_st[hi % 2]
                    e_, Re = eb[hi % 3], R_eb[hi % 3]
                    od, Rod = odp[hi % 2], R_od[hi % 2]
                    rcb, Rrc = rc[hi % 2], R_rc[hi % 2]

                    s1, Rs1, s2, Rs2 = es1[hi % 3], R_es1[hi % 3], es2[hi % 3], R_es2[hi % 3]
                    assert nk in (0, 4, 5)

                    def stage_a(hp=hp, st=st, Rst=Rst, e_=e_, Re=Re, s1=s1, Rs1=Rs1, s2=s2, Rs2=Rs2):
                        if hp == 0:
                            if pre is not None:
                                pre()
                            S.dma("sp", qt[qi][0:64, :, 0:nq], QTv[0:64, :, qtok0:qtok0 + nq], d_q[qi], wr=[R_q[qi]])
                            S.dma("sp", qt[qi][64:128, :, 128:128 + nq], QTv[64:128, :, qtok0:qtok0 + nq], d_q[qi], wr=[R_q[qi]])

                        def qk(e):
                            for i, kc in enumerate(kcs):
                                e.matmul(st[:, i, :], kt[slot_of(kc)][:, hp, :], qt[qi][:, hp, :], start=True, stop=True)
                            return e.matmul(st[:NMETA, nk, :], ktm[:, hp, :], qt[qi][:, hp, :], start=True, stop=True)
                        S.op("pe", qk, rd=[R_q[qi], R_m] + [R_kv[slot_of(kc)] for kc in kcs], wr=[Rst])
                        if nk:
                            S.op("act", lambda e: e.activation(out=e_[:, :nk, :], in_=st[:, :nk, :], func=AF.Exp),
                                 rd=[Rst], wr=[Re])
                        S.op("act", lambda e: e.activation(out=e_[:NMETA, nk, :], in_=st[:NMETA, nk, :], func=AF.Exp),
                             rd=[Rst], wr=[Re])
                        if nk:
                            ev = e_[:, :nk, :].rearrange("p i (h q) -> p i h q", h=2)
                            tv = table[:, 2 * hp:2 * hp + 2, ma:mb].rearrange("p h (i q) -> p i h q", q=128)
                            S.op("dve", lambda e: e.tensor_tensor(out=ev, in0=ev, in1=tv, op=ALU.mult),
                                 rd=[Re, R_table], wr=[Re])
                            S.op("pool", lambda e: e.tensor_tensor(out=s1[:, :], in0=e_[:, 0, :], in1=e_[:, 1, :], op=ALU.add),
                                 rd=[Re], wr=[Rs1])
                            S.op("pool", lambda e: e.tensor_tensor(out=s2[:, :], in0=e_[:, 2, :], in1=e_[:, 3, :], op=ALU.add),
                                 rd=[Re], wr=[Rs2])
                            S.op("dve", lambda e: e.tensor_tensor(out=s1[:, :], in0=s1[:, :], in1=s2[:, :], op=ALU.add),
                                 rd=[Rs1, Rs2], wr=[Rs1])
                            if nk == 5:
                                S.op("dve", lambda e: e.tensor_tensor(out=s1[:, :], in0=s1[:, :], in1=e_[:, 4, :], op=ALU.add),
                                     rd=[Rs1, Re], wr=[Rs1])

                    def stage_b(hp=hp, e_=e_, Re=Re, od=od, Rod=Rod, rcb=rcb, Rrc=Rrc, s1=s1, Rs1=Rs1):
                        def pv(e):
                            for i, kc in enumerate(kcs):
                                e.matmul(od[:, 0:256], vt[slot_of(kc)][:, hp * 128:(hp + 1) * 128], e_[:, i, :],
                                         start=(i == 0), stop=False)
                            e.matmul(od[:, 0:256], vm[:, hp * 128:(hp + 1) * 128], e_[:NMETA, nk, :],
                                     start=(nk == 0), stop=True)
                            if nk:
                                e.matmul(od[:, 256:512], ones[:, :], s1[:, :], start=True, stop=False)
                            return e.matmul(od[:, 256:512], ones[:NMETA, :], e_[:NMETA, nk, :], start=(nk == 0), stop=True)
                        S.op("pe", pv, rd=[Re, R_m, R_ones] + ([Rs1] if nk else []) + [R_kv[slot_of(kc)] for kc in kcs], wr=[Rod])
                        S.op("act", lambda e: e.activation(out=rcb[:, :], in_=od[:, 256:512], func=AF.Ln),
                             rd=[Rod], wr=[Rrc])
                        S.op("act", lambda e: e.activation(out=rcb[:, :], in_=rcb[:, :], func=AF.Exp, scale=-1.0),
                             rd=[Rrc], wr=[Rrc])
                        S.op("dve", lambda e: e.tensor_tensor(out=ot[qi][0:64, hp, :nq], in0=od[0:64, 0:nq], in1=rcb[0:64, 0:nq],
                                                              op=ALU.mult),
                             rd=[Rod, Rrc], wr=[R_ot[qi]])
                        S.op("dve", lambda e: e.tensor_tensor(out=ot[qi][64:128, hp, :nq], in0=od[64:128, 128:128 + nq],
                                                              in1=rcb[64:128, 128:128 + nq], op=ALU.mult),
                             rd=[Rod, Rrc], wr=[R_ot[qi]])
                        if hp == 7:
                            S.dma("pool", NATv[:, :, qtok0:qtok0 + nq], ot[qi][:, :, :nq], d_ot[qi], rd=[R_ot[qi]])
                    jobs.append((stage_a, stage_b, pre is load_meta_cur[0] and hp == 0))

            gslot = [0]
            for kind, idx, T, base in self.seqs:
                P = T // 128
                assert P >= 4
                s0 = gslot[0]
                slot_of = lambda kc, s0=s0: (s0 + kc) % NS
                gslot[0] += P

                def load_meta(base=base):
                    S.dma("sp", ktm[:, :, :], KTv[:, :, base:base + NMETA], d_m, wr=[R_m])
                    S.dma("sp", vm[:, :], self.V[base:base + NMETA, :], d_m, wr=[R_m])

                def load_kv(kc, base=base, slot_of=slot_of):
                    sl = slot_of(kc)
                    tk = base + NMETA + kc * 128
                    S.dma("sp", kt[sl][:, :, :], KTv[:, :, tk:tk + 128], d_kv[sl], wr=[R_kv[sl]])
                    S.dma("sp", vt[sl][:, :], self.V[tk:tk + 128, :], d_kv[sl], wr=[R_kv[sl]])
                load_meta_cur[0] = load_meta
                do_queries(base, NMETA, [], 0, slot_of, rfull, R_rfull, pre=load_meta)
                loaded = 0
                for p in range(P):
                    if p < 2:
                        kcs = [3, 2, 1, 0]
                        table, Rt = rfull, R_rfull
                    elif p >= P - 2:
                        kcs = [P - 1, P - 2, P - 3, P - 4]
                        table, Rt = rfull, R_rfull
                    else:
                        kcs = [p + 2, p + 1, p, p - 1, p - 2]
                        table, Rt = rint, R_rint
                    need = min(P, max(kcs) + 3)
                    lst = list(range(loaded, need))
                    loaded = max(loaded, need)
                    pre = (lambda lst=lst, load_kv=load_kv: [load_kv(kc) for kc in lst]) if lst else None
                    do_queries(base + NMETA + p * 128, 128, kcs, p, slot_of, table, Rt, pre=pre)
            pend = None
            for i in range(len(jobs)):
                if jobs[i][2] and pend is not None:
                    pend()
                    pend = None
                jobs[i][0]()
                if pend is not None:
                    pend()
                pend = jobs[i][1]
            pend()
            S.replay()

    def phase_lru(self, l):
        nc = self.nc
        with contextlib.ExitStack() as es:
            S = Sched(nc, es, f"lr{l}")
            vecs = S.sb("vecs", [128, NV], F32)
            R_vec = Res()
            S.dma("sp", vecs[:, :], self.vec[l], S.dsem(), wr=[R_vec])
            wst = S.sb("wst", [128, 4, 8, 128], F32)
            R_wst = Res()
            dws = S.dsem()
            S.op("pool", lambda e: e.memset(wst[:, :, :, :], 0.0), wr=[R_wst])
            first = True
            for d in range(2):
                for kd, nm in enumerate(("lwa", "lwx")):
                    src = self.w[nm][l, d].rearrange("(c h) i j -> h i c j", h=2)
                    for half in range(2):
                        S.dma("sp", wst[half * 64:(half + 1) * 64, d * 2 + kd, :, half * 64:(half + 1) * 64], src[half],
                              dws, rd=[R_wst] if first else [], bulk=True)
                        first = False
            R_wst.w = Ev(dws, None, "dma")
            one = S.sb("one", [128, 1], F32)
            S.op("pool", lambda e: e.memset(one[:, :], 1.0), wr=[R_vec])
            self.one_ap = one[:, 0:1]
            nsp = S.sb("nsp", [128, 16], F32)
            R_nsp = Res()
            S.op("act", lambda e: e.activation(out=nsp[:, :], in_=vecs[:, V_LAM:V_LAM + 16], func=AF.Exp, scale=-1.0),
                 rd=[R_vec], wr=[R_nsp])
            S.op("act", lambda e: e.activation(out=nsp[:, :], in_=nsp[:, :], func=AF.Ln, bias=self.one_ap),
                 rd=[R_nsp, R_vec], wr=[R_nsp])
            S.op("dve", lambda e: e.tensor_scalar(out=nsp[:, :], in0=nsp[:, :], scalar1=-8.0, scalar2=None, op0=ALU.mult),
                 rd=[R_nsp], wr=[R_nsp])

            hb2 = S.sb("hb2", [128, 32], F32)
            nsph = S.sb("nsph", [128, 16], F32)
            quart = S.sb("quart", [128, 1], F32)
            R_hb2 = Res()
            S.op("dve", lambda e: e.tensor_scalar(out=hb2[:, :], in0=vecs[:, V_BA:V_BA + 32], scalar1=0.5, scalar2=None,
                                                  op0=ALU.mult), rd=[R_vec], wr=[R_hb2])
            S.op("dve", lambda e: e.tensor_scalar(out=nsph[:, :], in0=nsp[:, :], scalar1=0.5, scalar2=None, op0=ALU.mult),
                 rd=[R_nsp], wr=[R_hb2])
            S.op("pool", lambda e: e.memset(quart[:, :], 0.25), wr=[R_hb2])
            SMAX = self.smax
            NB = 2
            NX = 3
            W = SMAX + 4
            mk = lambda nm, dt, n=NB: [S.sb(f"{nm}{i}", [128, W], dt) for i in range(n)]
            xrb, rb, ib, tb, hb, hfin = (mk(n_, F32) for n_ in ("xrb", "rb", "ib", "tb", "hb", "hfin"))
            xc = mk("xc", F32, NX)
            gy, ob = (mk(n_, BF16) for n_ in ("gy", "ob"))
            mkr = lambda n=NB: [Res() for _ in range(n)]
            R_xrb, R_rb, R_ib, R_tb, R_hb, R_hfin, R_gy, R_ob = (mkr() for _ in range(8))
            R_xc = mkr(NX)
            d_xrb, d_hb, d_hfin, d_gy, d_ob = ([S.dsem() for _ in range(NB)] for _ in range(5))
            carry = [S.sb(f"carry{i}", [128, 1], F32) for i in range(2)]
            R_carry = [Res(), Res()]
            rps = [S.ps(f"r{i}", [128, 512]) for i in range(2)]
            ips = [S.ps(f"i{i}", [128, 512]) for i in range(2)]
            R_rp, R_ip = [Res(), Res()], [Res(), Res()]
            XRv, HFv, GYv, LTv = self.fm(self.XR), self.fm(self.HF), self.fm(self.GY), self.fm(self.LT)
            pc = [0]
            xci = [0]
            jobs = []

            XCv = self.fm(self.XC)
            d_xcs = [S.dsem() for _ in range(NX)]
            d_xcl = [S.dsem() for _ in range(NX)]

            def make_job(ji, base, L, c, s0, s1, d, has_carry, reuse_xc, hfres, xcres=None, xmode="compute"):
                k = ji % NB
                Sg = s1 - s0
                cidx = d
                if not reuse_xc:
                    xci[0] += 1
                kx = xci[0] % NX
                cw = lambda j: vecs[:, V_CW + j * 8 + c:V_CW + j * 8 + c + 1]
                hba = hb2[:, d * 8 + c:d * 8 + c + 1]
                hbx = hb2[:, 16 + d * 8 + c:16 + d * 8 + c + 1]

                def ldf():
                    if xmode == "load":
                        S.dma("sp", xc[kx][:, :Sg], XCv[:, c, base + s0:base + s1], d_xcl[kx], rd=[xcres], wr=[R_xc[kx]])
                    elif not reuse_xc:
                        lo, hi = max(0, s0 - 2), min(L, s1 + 1)
                        if s0 - 2 < 0:
                            S.op("pool", lambda e: e.memset(xrb[k][:, 0:2], 0.0), wr=[R_xrb[k]])
                        if s1 + 1 > L:
                            S.op("pool", lambda e: e.memset(xrb[k][:, Sg + 2:Sg + 3], 0.0), wr=[R_xrb[k]])
                        S.dma("sp", xrb[k][:, lo - (s0 - 2):hi - (s0 - 2)], XRv[:, c, base + lo:base + hi], d_xrb[k],
                              wr=[R_xrb[k]])

                def s1f():
                    if not reuse_xc and xmode == "compute":
                        S.op("dve", lambda e: e.tensor_scalar(out=xc[kx][:, :Sg], in0=xrb[k][:, 0:Sg], scalar1=cw(0),
                                                              scalar2=vecs[:, V_CB + c:V_CB + c + 1], op0=ALU.mult, op1=ALU.add),
                             rd=[R_xrb[k], R_vec], wr=[R_xc[kx]])
                        for j in range(1, 4):
                            S.op("dve", lambda e, j=j: e.scalar_tensor_tensor(out=xc[kx][:, :Sg], in0=xrb[k][:, j:j + Sg],
                                                                              scalar=cw(j), in1=xc[kx][:, :Sg],
                                                                              op0=ALU.mult, op1=ALU.add),
                                 rd=[R_xrb[k], R_xc[kx], R_vec], wr=[R_xc[kx]])
                        if xcres is not None:
                            S.dma("pool", XCv[:, c, base + s0:base + s1], xc[kx][:, :Sg], d_xcs[kx], rd=[R_xc[kx]], wr=[xcres])
                    for a_, b_ in col_tiles(Sg, 512):
                        pb = pc[0] % 2
                        pc[0] += 1
                        S.op("pe", lambda e, a_=a_, b_=b_, pb=pb: e.matmul(rps[pb][:, :b_ - a_], wst[:, d * 2, c, :],
                                                                            xc[kx][:, a_:b_], start=True, stop=True),
                             rd=[R_wst, R_xc[kx]], wr=[R_rp[pb]])
                        S.op("pe", lambda e, a_=a_, b_=b_, pb=pb: e.matmul(ips[pb][:, :b_ - a_], wst[:, d * 2 + 1, c, :],
                                                                            xc[kx][:, a_:b_], start=True, stop=True),
                             rd=[R_wst, R_xc[kx]], wr=[R_ip[pb]])
                        S.op("act", lambda e, a_=a_, b_=b_, pb=pb: e.activation(out=rb[k][:, a_:b_], in_=rps[pb][:, :b_ - a_],
                                                                                 func=AF.Tanh, scale=0.5, bias=hba),
                             rd=[R_rp[pb], R_hb2], wr=[R_rb[k]])
                        S.op("act", lambda e, a_=a_, b_=b_, pb=pb: e.activation(out=ib[k][:, a_:b_], in_=ips[pb][:, :b_ - a_],
                                                                                 func=AF.Tanh, scale=0.5, bias=hbx),
                             rd=[R_ip[pb], R_hb2], wr=[R_ib[k]])
                    S.op("act", lambda e: e.activation(out=rb[k][:, :Sg], in_=rb[k][:, :Sg], func=AF.Exp,
                                                       scale=nsph[:, d * 8 + c:d * 8 + c + 1],
                                                       bias=nsph[:, d * 8 + c:d * 8 + c + 1]),
                         rd=[R_rb[k], R_hb2], wr=[R_rb[k]])
                    S.op("act", lambda e: e.activation(out=tb[k][:, :Sg], in_=rb[k][:, :Sg], func=AF.Square),
                         rd=[R_rb[k]], wr=[R_tb[k]])
                    S.op("act", lambda e: e.activation(out=tb[k][:, :Sg], in_=tb[k][:, :Sg], func=AF.Sqrt, scale=-0.25,
                                                       bias=quart[:, 0:1]),
                         rd=[R_tb[k], R_hb2], wr=[R_tb[k]])

                def pref():
                    if d == 1:
                        S.dma("sp", hfin[k][:, :Sg], HFv[:, c, base + s0:base + s1], d_hfin[k], rd=[hfres], wr=[R_hfin[k]])
                        S.dma("sp", gy[k][:, :Sg], GYv[:, c, base + s0:base + s1], d_gy[k], wr=[R_gy[k]])

                def s2f():
                    S.op("dve", lambda e: e.scalar_tensor_tensor(out=ib[k][:, :Sg], in0=ib[k][:, :Sg], scalar=1.0,
                                                                 in1=xc[kx][:, :Sg], op0=ALU.add, op1=ALU.mult),
                         rd=[R_ib[k], R_xc[kx]], wr=[R_ib[k]])
                    S.op("dve", lambda e: e.tensor_tensor(out=ib[k][:, :Sg], in0=ib[k][:, :Sg], in1=tb[k][:, :Sg], op=ALU.mult),
                         rd=[R_ib[k], R_tb[k]], wr=[R_ib[k]])
                    init = carry[cidx][:, 0:1] if has_carry else 0.0
                    rdc = [R_carry[cidx]] if has_carry else []
                    if d == 0:
                        S.op("dve", lambda e: e.tensor_tensor_scan(out=hb[k][:, :Sg], data0=rb[k][:, :Sg], data1=ib[k][:, :Sg],
                                                                   initial=init, op0=ALU.mult, op1=ALU.add),
                             rd=[R_rb[k], R_ib[k]] + rdc, wr=[R_hb[k]])
                        S.op("dve", lambda e: e.tensor_copy(out=carry[cidx][:, 0:1], in_=hb[k][:, Sg - 1:Sg]),
                             rd=[R_hb[k]], wr=[R_carry[cidx]])
                        S.dma("pool", HFv[:, c, base + s0:base + s1], hb[k][:, :Sg], d_hb[k], rd=[R_hb[k]], wr=[hfres])
                    else:
                        S.op("dve", lambda e: e.tensor_tensor_scan(out=hb[k][:, :Sg][:, ::-1], data0=rb[k][:, :Sg][:, ::-1],
                                                                   data1=ib[k][:, :Sg][:, ::-1], initial=init,
                                                                   op0=ALU.mult, op1=ALU.add),
                             rd=[R_rb[k], R_ib[k]] + rdc, wr=[R_hb[k]])
                        S.op("dve", lambda e: e.tensor_copy(out=carry[cidx][:, 0:1], in_=hb[k][:, 0:1]),
                             rd=[R_hb[k]], wr=[R_carry[cidx]])
                        S.op("pool", lambda e: e.tensor_tensor(out=hfin[k][:, :Sg], in0=hfin[k][:, :Sg], in1=hb[k][:, :Sg],
                                                               op=ALU.add),
                             rd=[R_hfin[k], R_hb[k]], wr=[R_hfin[k]])
                        S.op("pool", lambda e: e.tensor_tensor(out=ob[k][:, :Sg], in0=hfin[k][:, :Sg], in1=gy[k][:, :Sg],
                                                               op=ALU.mult),
                             rd=[R_hfin[k], R_gy[k]], wr=[R_ob[k]])
                        S.dma("pool", LTv[:, c, base + s0:base + s1], ob[k][:, :Sg], d_ob[k], rd=[R_ob[k]])
                jobs.append(dict(ld=ldf, s1=s1f, s2=s2f, pre=pref))

            for kind, idx, T, base in self.seqs:
                L = NMETA + T
                nseg = (L + SMAX - 1) // SMAX
                assert L % nseg == 0
                Sg = L // nseg
                segs = [(i * Sg, (i + 1) * Sg) for i in range(nseg)]
                for c in range(8):
                    hfres = Res()
                    xcres = Res() if nseg > 1 else None
                    for i, (s0, s1) in enumerate(segs):
                        make_job(len(jobs), base, L, c, s0, s1, 0, i != 0, False, hfres,
                                 xcres=xcres if i != nseg - 1 else None)
                    for i, (s0, s1) in reversed(list(enumerate(segs))):
                        if i == nseg - 1:
                            make_job(len(jobs), base, L, c, s0, s1, 1, False, True, hfres)
                        else:
                            make_job(len(jobs), base, L, c, s0, s1, 1, True, False, hfres, xcres=xcres, xmode="load")
            n = len(jobs)
            jobs[0]["ld"]()
            if n > 1:
                jobs[1]["ld"]()
            jobs[0]["s1"]()
            for i in range(n):
                if i + 2 < n:
                    jobs[i + 2]["ld"]()
                if i + 1 < n:
                    jobs[i + 1]["s1"]()
                if i == 0:
                    jobs[0]["pre"]()
                jobs[i]["s2"]()
                if i + 1 < n:
                    jobs[i + 1]["pre"]()
            S.replay()

    def phase_mixout(self, l):
        nc = self.nc
        with contextlib.ExitStack() as es:
            S = Sched(nc, es, f"mo{l}")
            wna = S.sb("wna", [128, 8, D], BF16)
            wlr = S.sb("wlr", [128, 8, D], BF16)
            wou = S.sb("wou", [128, 8, D], BF16)
            R_w, R_w2 = Res(), Res()
            dw, dw2 = S.dsem(), S.dsem()
            for dst, nm in ((wna, "wna"), (wlr, "wlru")):
                self.load_w(S, dst, self.w[nm][l], dw, R_w, 8, D)
            self.load_w(S, wou, self.w["wout"][l], dw2, R_w2, 8, D)
            NB = 2
            mk = lambda nm, dt: [S.sb(f"{nm}{i}", [128, 8, TT], dt) for i in range(NB)]
            x, nat, lt, sgn, sgl = mk("x", F32), mk("nat", BF16), mk("lt", BF16), mk("sgn", BF16), mk("sgl", BF16)
            R_x, R_nat, R_lt, R_sgn, R_sgl = ([Res() for _ in range(NB)] for _ in range(5))
            d_x, d_nat, d_lt, d_sgn, d_sgl, d_xs = ([S.dsem() for _ in range(NB)] for _ in range(6))
            mg = S.sb("mg", [128, 8, TT], BF16)
            R_mg = [Res() for _ in range(8)]
            t1 = [S.sb(f"t1{i}", [128, TT], F32) for i in range(2)]
            t2 = [S.sb(f"t2{i}", [128, TT], F32) for i in range(2)]
            R_t1, R_t2 = [Res(), Res()], [Res(), Res()]
            nps = [S.ps(f"n{i}", [128, TT]) for i in range(2)]
            lps = [S.ps(f"l{i}", [128, TT]) for i in range(2)]
            ops = [S.ps(f"o{i}", [128, TT]) for i in range(2)]
            R_n, R_l, R_o = [Res(), Res()], [Res(), Res()], [Res(), Res()]
            Hv = self.fm(self.H)
            tiles = self.tok_tiles()

            def loads(k):
                t0, t1_ = tiles[k]
                n = t1_ - t0
                b = k % NB
                for buf, R, ds, src in ((nat, R_nat, d_nat, self.NAT), (lt, R_lt, d_lt, self.LT),
                                        (sgn, R_sgn, d_sgn, self.SGN), (sgl, R_sgl, d_sgl, self.SGL), (x, R_x, d_x, self.H)):
                    S.dma("sp", buf[b][:, :, :n], self.fm(src)[:, :, t0:t1_], ds[b], wr=[R[b]])

            loads(0)
            for k, (t0, t1_) in enumerate(tiles):
                n = t1_ - t0
                b = k % NB
                if k + 1 < len(tiles):
                    loads(k + 1)
                for c in range(8):
                    pb = c % 2

                    def mmn(e, c=c, pb=pb, n=n, b=b):
                        ins = None
                        for kk in range(8):
                            ins = e.matmul(nps[pb][:, :n], wna[:, kk, c * 128:(c + 1) * 128], nat[b][:, kk, :n],
                                           start=(kk == 0), stop=(kk == 7))
                        return ins

                    def mml(e, c=c, pb=pb, n=n, b=b):
                        ins = None
                        for kk in range(8):
                            ins = e.matmul(lps[pb][:, :n], wlr[:, kk, c * 128:(c + 1) * 128], lt[b][:, kk, :n],
                                           start=(kk == 0), stop=(kk == 7))
                        return ins
                    S.op("pe", mmn, rd=[R_w, R_nat[b]], wr=[R_n[pb]])
                    S.op("pe", mml, rd=[R_w, R_lt[b]], wr=[R_l[pb]])
                    S.op("dve", lambda e, c=c, pb=pb, n=n, b=b: e.tensor_tensor(out=t1[pb][:, :n], in0=nps[pb][:, :n],
                                                                                in1=sgn[b][:, c, :n], op=ALU.mult),
                         rd=[R_n[pb], R_sgn[b]], wr=[R_t1[pb]])
                    S.op("dve", lambda e, c=c, pb=pb, n=n, b=b: e.tensor_tensor(out=t2[pb][:, :n], in0=lps[pb][:, :n],
                                                                                in1=sgl[b][:, c, :n], op=ALU.mult),
                         rd=[R_l[pb], R_sgl[b]], wr=[R_t2[pb]])
                    S.op("pool", lambda e, c=c, pb=pb, n=n: e.tensor_tensor(out=mg[:, c, :n], in0=t1[pb][:, :n],
                                                                            in1=t2[pb][:, :n], op=ALU.add),
                         rd=[R_t1[pb], R_t2[pb]], wr=[R_mg[c]])
                for c in range(8):
                    pb = c % 2

                    def mmo(e, c=c, pb=pb, n=n):
                        ins = None
                        for kk in range(8):
                            ins = e.matmul(ops[pb][:, :n], wou[:, kk, c * 128:(c + 1) * 128], mg[:, kk, :n],
                                           start=(kk == 0), stop=(kk == 7))
                        return ins
                    S.op("pe", mmo, rd=[R_w2] + R_mg, wr=[R_o[pb]])
                    S.op("dve", lambda e, c=c, pb=pb, n=n, b=b: e.tensor_tensor(out=x[b][:, c, :n], in0=ops[pb][:, :n],
                                                                                in1=x[b][:, c, :n], op=ALU.add),
                         rd=[R_o[pb], R_x[b]], wr=[R_x[b]])
                S.dma("pool", Hv[:, :, t0:t1_], x[b][:, :, :n], d_xs[b], rd=[R_x[b]])
            S.replay()

    def phase_final(self):
        nc = self.nc
        with contextlib.ExitStack() as es:
            S = Sched(nc, es, "fin")
            ones, R_ones, vecs, R_vec = self.consts(S, 0)
            idt = S.sb("idt", [128, 128], F32)
            R_id = Res()
            S.dma("sp", idt[:, :], self.ident, S.dsem(), wr=[R_id])
            x = [S.sb(f"x{i}", [128, 8, TT], F32) for i in range(2)]
            R_x, d_x = [Res(), Res()], [S.dsem(), S.dsem()]
            xn = [S.sb(f"xn{i}", [128, 8, TT], F32) for i in range(2)]
            R_xn = [Res(), Res()]
            yt = [S.sb(f"yt{i}", [128, 4, D], F32) for i in range(2)]
            R_yt, d_yt = [Res(), Res()], [S.dsem(), S.dsem()]
            sqs = [S.sb(f"sq{i}", [128, TT], BF16) for i in range(2)]
            R_sq = [Res(), Res()]
            sd = S.sb("sd", [128, TT], F32)
            rstd = S.sb("rstd", [128, TT], F32)
            R_sd, R_rstd = Res(), Res()
            ssq = S.ps("ssq", [128, TT])
            R_ssq = Res()
            tps = [S.ps(f"t{i}", [128, 512]) for i in range(4)]
            R_tp = [Res() for _ in range(4)]
            Hv = self.fm(self.H)
            jobs = []
            for kind, idx, T, base in self.seqs:
                dst = self.yp if kind == "p" else self.ys
                for a, b in col_tiles(T, TT):
                    jobs.append((dst[idx, a:b, :], base + NMETA + a, b - a))
            pc = 0
            pcc = [0]

            def ld(k):
                dst, t0, n = jobs[k]
                S.dma("sp", x[k % 2][:, :, :n], Hv[:, :, t0:t0 + n], d_x[k % 2], wr=[R_x[k % 2]])

            def st_a(k):
                dst, t0, n = jobs[k]
                b = k % 2
                self.norm(S, x[b], R_x[b], n, V_GF, vecs, R_vec, ones, sqs, R_sq, k, ssq, R_ssq, sd, R_sd,
                          rstd, R_rstd, xn[b], R_xn[b])

            def st_b(k):
                dst, t0, n = jobs[k]
                b = k % 2
                ng = n // 128
                for g in range(ng):
                    for hf in range(2):
                        pi = pcc[0] % 4
                        pcc[0] += 1
                        tp = tps[pi]

                        def tr(e, g=g, hf=hf, tp=tp, b=b):
                            ins = None
                            for j in range(4):
                                ins = e.transpose(tp[:, j * 128:(j + 1) * 128], xn[b][:, hf * 4 + j, g * 128:(g + 1) * 128], idt[:, :])
                            return ins
                        S.op("pe", tr, rd=[R_xn[b], R_id], wr=[R_tp[pi]])
                        if hf == 0:
                            S.op("dve", lambda e, g=g, hf=hf, tp=tp, b=b: e.tensor_copy(out=yt[b][:, g, hf * 512:(hf + 1) * 512], in_=tp[:, :]),
                                 rd=[R_tp[pi]], wr=[R_yt[b]])
                        else:
                            S.op("act", lambda e, g=g, hf=hf, tp=tp, b=b: e.activation(out=yt[b][:, g, hf * 512:(hf + 1) * 512], in_=tp[:, :], func=AF.Copy),
                                 rd=[R_tp[pi]], wr=[R_yt[b]])
                S.dma("pool", dst.rearrange("(g p) d -> p g d", p=128), yt[b][:, :ng, :], d_yt[b], rd=[R_yt[b]])

            nj = len(jobs)
            ld(0)
            st_a(0)
            for k in range(nj):
                if k + 1 < nj:
                    ld(k + 1)
                    st_a(k + 1)
                st_b(k)
            S.replay()

    def dump(self, name):
        if not self.debug:
            return
        nc = self.nc
        with contextlib.ExitStack() as es:
            S = Sched(nc, es, "dump" + name)
            S.dma("sp", self.dbg[name], self.H, S.dsem())
            S.replay()

    def build(self, stop_after=None):
        self.phase_init()
        self.dump("H0")
        for l in range(DEPTH):
            self.phase_ffn(l, 1)
            if l == 0:
                self.dump("H1")
            self.phase_mixin(l)
            self.phase_attn(l)
            self.phase_lru(l)
            self.phase_mixout(l)
            if l == 0:
                self.dump("H2")
            self.phase_ffn(l, 2)
            if l == 0:
                self.dump("H3")
        self.phase_final()
        return self.nc


def pack_vec(inp, l):
    v = np.zeros((128, NV), np.float32)
    pc = lambda a: np.ascontiguousarray(a.reshape(8, 128).T)
    v[:, V_G1:V_G1 + 8] = pc(inp["norm_ffn1"][l])
    v[:, V_GM:V_GM + 8] = pc(inp["norm_mix"][l])
    v[:, V_G2:V_G2 + 8] = pc(inp["norm_ffn2"][l])
    for j in range(4):
        v[:, V_CW + j * 8:V_CW + j * 8 + 8] = pc(inp["conv_w"][l, j])
    v[:, V_CB:V_CB + 8] = pc(inp["conv_b"][l])
    for d in range(2):
        v[:, V_BA + d * 8:V_BA + d * 8 + 8] = pc(inp["lru_ba"][l, d])
        v[:, V_BX + d * 8:V_BX + d * 8 + 8] = pc(inp["lru_bx"][l, d])
        v[:, V_LAM + d * 8:V_LAM + d * 8 + 8] = pc(inp["lru_lambda"][l, d])
    v[:, V_GF:V_GF + 8] = pc(inp["final_norm"])
    return v


def shared_inputs(inp):
    f = lambda a: np.ascontiguousarray(np.asarray(a, dtype=np.float32))
    m = dict(
        meta=f(inp["meta_tokens"]), ident=np.eye(128, dtype=np.float32),
        vec=np.stack([pack_vec(inp, l) for l in range(DEPTH)]),
        fbias=f(np.asarray(inp["na_rel_bias"])[:, :, ::-1, ::-1]),
        f1g=f(inp["ffn1_w_gate"]), f1u=f(inp["ffn1_w_up"]), f1d=f(inp["ffn1_w_down"]),
        win=f(inp["w_in"]), wna=f(inp["w_na_proj"]), wlru=f(inp["w_lru_proj"]), wout=f(inp["w_out"]),
        f2g=f(inp["ffn2_w_gate"]), f2u=f(inp["ffn2_w_up"]), f2d=f(inp["ffn2_w_down"]),
        lwa=f(inp["lru_wa"]), lwx=f(inp["lru_wx"]),
    )
    return m


def run(inp, n_cores, debug=False):
    inp = {k: np.asarray(v) for k, v in inp.items()}
    xp, xs = inp["x_prompt"], inp["x_sample"]
    n_p, t_p = xp.shape[0] // n_cores, xp.shape[1]
    n_s, t_s = xs.shape[0] // n_cores, xs.shape[1]
    bld = Builder(n_p, t_p, n_s, t_s, debug=debug)
    nc = bld.build()
    sh = shared_inputs(inp)
    in_maps = []
    for i in range(n_cores):
        m = dict(sh)
        m["xp"] = np.ascontiguousarray(xp[i * n_p:(i + 1) * n_p])
        m["xs"] = np.ascontiguousarray(xs[i * n_s:(i + 1) * n_s])
        in_maps.append(m)
    res = run_bass_kernel_spmd(nc, in_maps, core_ids=list(range(n_cores)))
    yp = np.concatenate([r["yp"] for r in res.results], axis=0)
    ys = np.concatenate([r["ys"] for r in res.results], axis=0)
    return (yp, ys), res, bld


def kernel(**inputs):
    (yp, ys), _, _ = run(inputs, NCORES)
    return (yp.astype(np.float32), ys.astype(np.float32))
```

```python
import contextlib
import numpy as np
import concourse.bass as bass
import concourse.mybir as mybir
from concourse.bass_utils import run_bass_kernel_spmd

F32 = mybir.dt.float32
BF16 = mybir.dt.bfloat16
AF = mybir.ActivationFunctionType
ALU = mybir.AluOpType

D = 1024
DFF = 2816
NFF = DFF // 128
DEPTH = 2
NMETA = 16
GW = 64
INW = 7168
EPS = 1e-6
NCORES = 8
TT = 512

V_G1, V_GM, V_G2, V_CW, V_CB, V_BA, V_BX, V_LAM, V_GF = 0, 8, 16, 24, 56, 64, 80, 96, 112
NV = 120


class Ev:
    __slots__ = ("ds", "val", "eng")

    def __init__(self, ds, val, eng):
        self.ds, self.val, self.eng = ds, val, eng

    def value(self):
        return self.val if self.val is not None else 16 * self.ds.cnt


class DSem:
    def __init__(self, sem):
        self.sem, self.cnt = sem, 0


class Res:
    def __init__(self, name=""):
        self.name = name
        self.w = None
        self.rs = {}


class Sched:
    ENGS = ("sp", "act", "dve", "pool", "pe")

    def __init__(self, nc, es, tag):
        self.nc, self.es, self.tag = nc, es, tag
        self.q = {e: [] for e in self.ENGS}
        self.sems = []
        self.esem = {e: DSem(self._sem(f"{tag}_s_{e}")) for e in ("act", "dve", "pool", "pe")}
        self.dsems = []
        self.nalloc = 0

    def _sem(self, name):
        h = self.nc.alloc_semaphore(name=name)
        self.sems.append(h)
        return h

    def dsem(self):
        self.nalloc += 1
        d = DSem(self._sem(f"{self.tag}_d{self.nalloc}"))
        self.dsems.append(d)
        return d

    def sb(self, name, shape, dt):
        return self.es.enter_context(self.nc.sbuf_tensor(f"{self.tag}_{name}", shape, dt))

    def ps(self, name, shape, dt=F32):
        return self.es.enter_context(self.nc.psum_tensor(f"{self.tag}_{name}", shape, dt))

    def _deps(self, eng, rd, wr):
        waits = []
        for r in rd:
            if r.w is not None:
                waits.append((r.w, True))
        for r in wr:
            if r.w is not None:
                waits.append((r.w, False))
            for ev in r.rs.values():
                waits.append((ev, False))
        return waits

    def _commit(self, ev, rd, wr):
        for r in rd:
            r.rs[id(ev.ds)] = ev
        for r in wr:
            r.w = ev
            r.rs = {}

    def op(self, eng, fn, rd=(), wr=()):
        waits = self._deps(eng, rd, wr)
        ds = self.esem[eng]
        ds.cnt += 1
        ev = Ev(ds, ds.cnt, eng)
        self.q[eng].append((fn, waits, ev, 1))
        self._commit(ev, rd, wr)
        return ev

    def dma(self, q, out, in_, ds, rd=(), wr=(), bulk=False, **kw):
        waits = self._deps(q, rd, wr)
        ds.cnt += 1
        ev = Ev(ds, None if bulk else 16 * ds.cnt, "dma")
        self.q[q].append((lambda e: e.dma_start(out=out, in_=in_, **kw), waits, ev, 16))
        self._commit(ev, rd, wr)
        return ev

    def replay(self):
        nc = self.nc
        finals = [(d.sem, 16 * d.cnt) for d in self.dsems if d.cnt]
        with nc.Block() as block:
            for eng, deco in (("sp", block.sync), ("act", block.scalar), ("dve", block.vector),
                              ("pool", block.gpsimd), ("pe", block.tensor)):
                q = self.q[eng]

                def body(e, q=q, eng=eng):
                    mw = {}
                    for fn, waits, ev, inc in q:
                        for wev, raw in waits:
                            key = id(wev.ds)
                            v = wev.value()
                            if mw.get(key, 0) >= v:
                                continue
                            e.wait_ge(wev.ds.sem, v)
                            mw[key] = v
                        ins = fn(e)
                        ins.then_inc(ev.ds.sem, inc)
                    if eng == "sp":
                        for sem, v in finals:
                            e.wait_ge(sem, v)
                        for en2 in ("act", "dve", "pool", "pe"):
                            d = self.esem[en2]
                            if d.cnt:
                                e.wait_ge(d.sem, d.cnt)

                deco(body)
        nc.all_engine_barrier()
        nc.clear_and_free_semaphores(self.sems)
        nc.all_engine_barrier()


def col_tiles(n, step=512):
    return [(a, min(a + step, n)) for a in range(0, n, step)]


class Builder:
    def __init__(self, n_p, t_p, n_s, t_s, debug=False):
        self.cfg = (n_p, t_p, n_s, t_s)
        self.debug = debug
        self.smax = 2064
        nc = self.nc = bass.Bass("TRN2", target_bir_lowering=False)
        self.seqs = []
        base = 0
        for i in range(n_p):
            self.seqs.append(("p", i, t_p, base))
            base += NMETA + t_p
        for i in range(n_s):
            self.seqs.append(("s", i, t_s, base))
            base += NMETA + t_s
        self.NT = NT = base
        di = lambda name, shape, dt=F32: nc.dram_tensor(name, shape, dt, kind="ExternalInput").ap()
        do = lambda name, shape, dt=F32: nc.dram_tensor(name, shape, dt, kind="ExternalOutput").ap()
        sk = "ExternalOutput" if debug else "Internal"
        dsr = lambda name, shape, dt: nc.dram_tensor(name, shape, dt, kind=sk).ap()
        self.xp = di("xp", [n_p, t_p, D])
        self.xs = di("xs", [n_s, t_s, D])
        self.meta = di("meta", [NMETA, D])
        self.ident = di("ident", [128, 128])
        self.vec = di("vec", [DEPTH, 128, NV])
        self.fbias = di("fbias", [DEPTH, 16, 15, 31])
        self.w = {}
        for nm, shp in (("f1g", [D, DFF]), ("f1u", [D, DFF]), ("f1d", [DFF, D]), ("win", [D, INW]),
                        ("wna", [D, D]), ("wlru", [D, D]), ("wout", [D, D]),
                        ("f2g", [D, DFF]), ("f2u", [D, DFF]), ("f2d", [DFF, D]),
                        ("lwa", [2, 16, 64, 64]), ("lwx", [2, 16, 64, 64])):
            self.w[nm] = di(nm, [DEPTH] + shp)
        self.yp = do("yp", [n_p, t_p, D])
        self.ys = do("ys", [n_s, t_s, D])
        self.H = dsr("H", [D, NT], F32)
        self.QT = dsr("QT", [D, NT], BF16)
        self.KT = dsr("KT", [D, NT], BF16)
        self.V = dsr("V", [NT, D], BF16)
        self.XR = dsr("XR", [D, NT], F32)
        self.GY = dsr("GY", [D, NT], BF16)
        self.SGN = dsr("SGN", [D, NT], BF16)
        self.SGL = dsr("SGL", [D, NT], BF16)
        self.NAT = dsr("NAT", [D, NT], BF16)
        self.LT = dsr("LT", [D, NT], BF16)
        self.HF = dsr("HF", [D, NT], F32)
        self.XC = nc.dram_tensor("XC", [D, NT], F32, kind="Internal").ap()
        self.MSK = nc.dram_tensor("MSK", [DEPTH, 2, 128, 16, 1024], BF16, kind="Internal").ap()
        self.dbg = {}
        if debug:
            for nm in ("H0", "H1", "H2", "H3"):
                self.dbg[nm] = do("dbg_" + nm, [D, NT])

    @staticmethod
    def fm(ap):
        return ap.rearrange("(c p) t -> p c t", p=128)

    def tok_tiles(self):
        return col_tiles(self.NT, TT)

    def norm(self, S, x, R_x, n, gcol, vecs, R_vec, ones, sqs, R_sq, k, ssq, R_ssq, sd, R_sd, rstd, R_rstd, xn, R_xn):
        for c in range(8):
            sq, Rq = sqs[(k * 8 + c) % 2], R_sq[(k * 8 + c) % 2]
            S.op("act", lambda e, c=c, sq=sq: e.activation(out=sq[:, :n], in_=x[:, c, :n], func=AF.Square),
                 rd=[R_x], wr=[Rq])
            S.op("pe", lambda e, c=c, sq=sq: e.matmul(ssq[:, :n], ones[:, :], sq[:, :n], start=(c == 0), stop=(c == 7)),
                 rd=[Rq, self.R_ones], wr=[R_ssq])
        S.op("act", lambda e: e.activation(out=sd[:, :n], in_=ssq[:, :n], func=AF.Sqrt, scale=1.0 / D, bias=self.eps_ap),
             rd=[R_ssq, self.R_ones], wr=[R_sd])
        S.op("dve", lambda e: e.reciprocal(out=rstd[:, :n], in_=sd[:, :n]), rd=[R_sd], wr=[R_rstd])
        for c in range(8):
            S.op("dve", lambda e, c=c: e.scalar_tensor_tensor(out=xn[:, c, :n], in0=x[:, c, :n],
                                                              scalar=vecs[:, gcol + c:gcol + c + 1], in1=rstd[:, :n],
                                                              op0=ALU.mult, op1=ALU.mult),
                 rd=[R_x, R_rstd, R_vec], wr=[R_xn])

    def consts(self, S, l):
        ones = S.sb("ones", [128, 128], BF16)
        vecs = S.sb("vecs", [128, NV], F32)
        eps = S.sb("eps", [128, 1], F32)
        R_ones, R_vec = Res(), Res()
        S.op("pool", lambda e: e.memset(ones[:, :], 1.0), wr=[R_ones])
        S.op("pool", lambda e: e.memset(eps[:, :], EPS), wr=[R_ones])
        S.dma("sp", vecs[:, :], self.vec[l], S.dsem(), wr=[R_vec])
        self.eps_ap = eps[:, 0:1]
        self.R_ones = R_ones
        return ones, R_ones, vecs, R_vec

    def load_w(self, S, dst, src, ds, R, nk, ncols, cstep=1024):
        srcv = src.rearrange("(c p) f -> p c f", p=128)
        for c in range(nk):
            for a, b in col_tiles(ncols, cstep):
                S.dma("pool", dst[:, c, a:b], srcv[:, c, a:b], ds, wr=[], bulk=True)
        R.w = Ev(ds, None, "dma")

    def build_masks(self, S, q):
        rf32 = S.sb("rf32", [128, 16, 15, 64], F32)
        rfull4 = S.sb("rfull", [128, 16, 16, 64], BF16)
        rint4 = S.sb("rint", [128, 16, 16, 64], BF16)
        rfull = rfull4.rearrange("p h j q -> p h (j q)")
        rint = rint4.rearrange("p h j q -> p h (j q)")
        R_rf32, R_rfull, R_rint = Res(), Res(), Res()
        d_st = [S.dsem(), S.dsem()]
        qs = np.arange(GW)
        cs = np.clip(qs - 8, 0, GW - 16)
        rfv = rf32.rearrange("p h j q -> p (h j) q")
        for l in range(DEPTH):
            dmk = S.dsem()
            S.op("pool", lambda e: e.memset(rf32[:, :, :, :], -30000.0), wr=[R_rf32])
            S.op("pool", lambda e: e.memset(rfull4[:, :, :, :], 0.0), wr=[R_rfull])
            first = True
            for kp in range(2):
                for kcol in range(GW):
                    valid = np.nonzero((cs <= kcol) & (kcol < cs + 16))[0]
                    qlo, qhi = int(valid[0]), int(valid[-1])
                    assert len(valid) == qhi - qlo + 1
                    nq = qhi - qlo + 1
                    b0 = 15 - kcol + qlo
                    assert 0 <= b0 and b0 + nq <= 31
                    pp = kp * 64 + kcol
                    S.dma(q, rfv[pp:pp + 1, :, qlo:qhi + 1],
                          self.fbias[l].rearrange("h a b -> (h a) b")[:, b0:b0 + nq].rearrange("(o r) b -> o r b", o=1),
                          dmk, rd=[R_rf32] if first else [], bulk=True)
                    first = False
            R_rf32.w = Ev(dmk, None, "dma")
            S.op("act", lambda e: e.activation(out=rfull4[0:64, :, 0:15, :], in_=rf32[0:64, :, :, :], func=AF.Exp),
                 rd=[R_rf32, R_rfull], wr=[R_rfull])
            S.op("act", lambda e: e.activation(out=rfull4[64:128, :, 1:16, :], in_=rf32[64:128, :, :, :], func=AF.Exp),
                 rd=[R_rf32, R_rfull], wr=[R_rfull])
            S.op("pool", lambda e: e.tensor_copy(out=rint[:, :, :], in_=rfull[:, :, :]), rd=[R_rfull], wr=[R_rint])
            S.op("pool", lambda e: e.memset(rint[0:64, :, 0:4 * 64], 0.0), wr=[R_rint])
            S.op("pool", lambda e: e.memset(rint[0:64, :, 12 * 64:16 * 64], 0.0), wr=[R_rint])
            S.op("pool", lambda e: e.memset(rint[64:128, :, 0:5 * 64], 0.0), wr=[R_rint])
            S.op("pool", lambda e: e.memset(rint[64:128, :, 13 * 64:16 * 64], 0.0), wr=[R_rint])
            S.dma("pool", self.MSK[l, 0], rfull[:, :, :], d_st[0], rd=[R_rfull])
            S.dma("pool", self.MSK[l, 1], rint[:, :, :], d_st[1], rd=[R_rint])

    def phase_init(self):
        nc = self.nc
        with contextlib.ExitStack() as es:
            S = Sched(nc, es, "in")
            self.build_masks(S, "act")
            idt = S.sb("idt", [128, 128], F32)
            R_id = Res()
            S.dma("sp", idt[:, :], self.ident, S.dsem(), wr=[R_id])
            xin = [S.sb(f"xin{i}", [128, 4, D], F32) for i in range(2)]
            xfm = [S.sb(f"xfm{i}", [128, 8, TT], F32) for i in range(2)]
            R_xin = [Res(), Res()]
            R_xfm = [Res(), Res()]
            d_xin = [S.dsem(), S.dsem()]
            d_xfm = [S.dsem(), S.dsem()]
            pst = [S.ps(f"pt{i}", [128, TT]) for i in range(4)]
            R_ps = [Res() for _ in range(4)]
            jobs = []
            for kind, idx, T, base in self.seqs:
                jobs.append((self.meta, base, NMETA))
                src = self.xp if kind == "p" else self.xs
                for a, b in col_tiles(T, TT):
                    jobs.append((src[idx, a:b, :], base + NMETA + a, b - a))
            Hv = self.fm(self.H)
            pc = 0
            for k, (src, t0, n) in enumerate(jobs):
                b = k % 2
                ng = (n + 127) // 128
                if n >= 128:
                    S.dma("sp", xin[b][:, :ng, :], src.rearrange("(g p) d -> p g d", p=128), d_xin[b], wr=[R_xin[b]])
                else:
                    S.dma("sp", xin[b][:n, 0, :], src, d_xin[b], wr=[R_xin[b]])
                for c in range(8):
                    pi = pc % 4
                    pc += 1
                    ps = pst[pi]

                    def tr(e, c=c, ps=ps, b=b, n=n, ng=ng):
                        ins = None
                        for g in range(ng):
                            m = min(128, n - g * 128)
                            ins = e.transpose(ps[:, g * 128:g * 128 + m], xin[b][:m, g, c * 128:(c + 1) * 128], idt[:m, :m])
                        return ins
                    S.op("pe", tr, rd=[R_xin[b], R_id], wr=[R_ps[pi]])
                    eng = "dve"
                    if eng == "dve":
                        S.op("dve", lambda e, c=c, ps=ps, b=b, n=n: e.tensor_copy(out=xfm[b][:, c, :n], in_=ps[:, :n]),
                             rd=[R_ps[pi]], wr=[R_xfm[b]])
                    else:
                        S.op("act", lambda e, c=c, ps=ps, b=b, n=n: e.activation(out=xfm[b][:, c, :n], in_=ps[:, :n], func=AF.Copy),
                             rd=[R_ps[pi]], wr=[R_xfm[b]])
                S.dma("sp", Hv[:, :, t0:t0 + n], xfm[b][:, :, :n], d_xfm[b], rd=[R_xfm[b]])
            S.replay()

    def phase_ffn(self, l, which):
        nc = self.nc
        pre = "f1" if which == 1 else "f2"
        gcol = V_G1 if which == 1 else V_G2
        with contextlib.ExitStack() as es:
            S = Sched(nc, es, f"{pre}{l}")
            ones, R_ones, vecs, R_vec = self.consts(S, l)
            wg = S.sb("wg", [128, 8, DFF], BF16)
            wu = S.sb("wu", [128, 8, DFF], BF16)
            wd = S.sb("wd", [128, NFF, D], BF16)
            R_wd = Res()
            dd = S.dsem()
            blocks = col_tiles(DFF, 1024)
            R_wgu = [Res() for _ in blocks]
            dgu = [S.dsem() for _ in blocks]
            gv = self.w[pre + "g"][l].rearrange("(c p) f -> p c f", p=128)
            uv = self.w[pre + "u"][l].rearrange("(c p) f -> p c f", p=128)
            for bi, (a, b) in enumerate(blocks):
                for c in range(8):
                    S.dma("pool", wg[:, c, a:b], gv[:, c, a:b], dgu[bi], bulk=True)
                    S.dma("pool", wu[:, c, a:b], uv[:, c, a:b], dgu[bi], bulk=True)
                R_wgu[bi].w = Ev(dgu[bi], None, "dma")
            self.load_w(S, wd, self.w[pre + "d"][l], dd, R_wd, NFF, D)
            x = [S.sb(f"x{i}", [128, 8, TT], F32) for i in range(2)]
            R_x = [Res(), Res()]
            d_x = [S.dsem(), S.dsem()]
            d_xs = [S.dsem(), S.dsem()]
            xn = S.sb("xn", [128, 8, TT], BF16)
            R_xn = Res()
            hT = S.sb("hT", [128, NFF, TT], BF16)
            R_h = [Res() for _ in range(NFF)]
            sqs = [S.sb(f"sq{i}", [128, TT], BF16) for i in range(2)]
            R_sq = [Res(), Res()]
            sg = [S.sb(f"sg{i}", [128, TT], F32) for i in range(2)]
            R_sg = [Res(), Res()]
            sd = S.sb("sd", [128, TT], F32)
            rstd = S.sb("rstd", [128, TT], F32)
            R_sd, R_rstd = Res(), Res()
            ssq = S.ps("ssq", [128, TT])
            R_ssq = Res()
            gps = [S.ps(f"g{i}", [128, TT]) for i in range(2)]
            ups = [S.ps(f"u{i}", [128, TT]) for i in range(2)]
            ops = [S.ps(f"o{i}", [128, TT]) for i in range(2)]
            R_g, R_u, R_o = [Res(), Res()], [Res(), Res()], [Res(), Res()]
            Hv = self.fm(self.H)
            tiles = self.tok_tiles()

            def stage_a(k):
                t0, t1 = tiles[k]
                n = t1 - t0
                b = k % 2
                S.dma("sp", x[b][:, :, :n], Hv[:, :, t0:t1], d_x[b], wr=[R_x[b]])
                self.norm(S, x[b], R_x[b], n, gcol, vecs, R_vec, ones, sqs, R_sq, k, ssq, R_ssq, sd, R_sd,
                          rstd, R_rstd, xn, R_xn)

            stage_a(0)
            for k, (t0, t1) in enumerate(tiles):
                n = t1 - t0
                b = k % 2
                for f in range(NFF):
                    pb = f % 2

                    def mmg(e, f=f, pb=pb, n=n):
                        ins = None
                        for c in range(8):
                            ins = e.matmul(gps[pb][:, :n], wg[:, c, f * 128:(f + 1) * 128], xn[:, c, :n],
                                           start=(c == 0), stop=(c == 7))
                        return ins

                    def mmu(e, f=f, pb=pb, n=n):
                        ins = None
                        for c in range(8):
                            ins = e.matmul(ups[pb][:, :n], wu[:, c, f * 128:(f + 1) * 128], xn[:, c, :n],
                                           start=(c == 0), stop=(c == 7))
                        return ins
                    S.op("pe", mmg, rd=[R_wgu[f * 128 // 1024], R_xn], wr=[R_g[pb]])
                    S.op("pe", mmu, rd=[R_wgu[f * 128 // 1024], R_xn], wr=[R_u[pb]])
                    S.op("act", lambda e, pb=pb, n=n: e.activation(out=sg[pb][:, :n], in_=gps[pb][:, :n], func=AF.Silu),
                         rd=[R_g[pb]], wr=[R_sg[pb]])
                    S.op("dve", lambda e, f=f, pb=pb, n=n: e.tensor_tensor(out=hT[:, f, :n], in0=ups[pb][:, :n],
                                                                          in1=sg[pb][:, :n], op=ALU.mult),
                         rd=[R_u[pb], R_sg[pb]], wr=[R_h[f]])
                if k + 1 < len(tiles):
                    stage_a(k + 1)
                for c in range(8):
                    pb = c % 2

                    def mmd(e, c=c, pb=pb, n=n):
                        ins = None
                        for f in range(NFF):
                            ins = e.matmul(ops[pb][:, :n], wd[:, f, c * 128:(c + 1) * 128], hT[:, f, :n],
                                           start=(f == 0), stop=(f == NFF - 1))
                        return ins
                    S.op("pe", mmd, rd=[R_wd] + R_h, wr=[R_o[pb]])
                    S.op("dve", lambda e, c=c, pb=pb, n=n, b=b: e.scalar_tensor_tensor(
                        out=x[b][:, c, :n], in0=ops[pb][:, :n], scalar=0.5, in1=x[b][:, c, :n],
                        op0=ALU.mult, op1=ALU.add), rd=[R_o[pb], R_x[b]], wr=[R_x[b]])
                S.dma("pool", Hv[:, :, t0:t1], x[b][:, :, :n], d_xs[b], rd=[R_x[b]])
            S.replay()

    def phase_mixin(self, l):
        nc = self.nc
        with contextlib.ExitStack() as es:
            S = Sched(nc, es, f"mi{l}")
            ones, R_ones, vecs, R_vec = self.consts(S, l)
            win = S.sb("win", [128, 8, INW], BF16)
            R_wb = [Res() for _ in range(7)]
            winv = self.w["win"][l].rearrange("(c p) f -> p c f", p=128)
            for bi in range(7):
                dwb = S.dsem()
                for c in range(8):
                    S.dma("pool", win[:, c, bi * 1024:(bi + 1) * 1024], winv[:, c, bi * 1024:(bi + 1) * 1024], dwb, bulk=True)
                R_wb[bi].w = Ev(dwb, None, "dma")
            x = S.sb("x", [128, 8, TT], F32)
            R_x, d_x = Res(), S.dsem()
            xn = [S.sb(f"xn{i}", [128, 8, TT], BF16) for i in range(2)]
            R_xn = [Res(), Res()]
            sqs = [S.sb(f"sq{i}", [128, TT], BF16) for i in range(2)]
            R_sq = [Res(), Res()]
            sd = S.sb("sd", [128, TT], F32)
            rstd = S.sb("rstd", [128, TT], F32)
            R_sd, R_rstd = Res(), Res()
            stg = [S.sb(f"stg{i}", [128, 8, TT], BF16) for i in range(3)]
            R_stg = [Res() for _ in range(3)]
            d_stg = [S.dsem() for _ in range(3)]
            vst = S.sb("vst", [128, 4, D], BF16)
            R_vst, d_vst = Res(), S.dsem()
            xrs = S.sb("xrs", [128, 8, TT], F32)
            R_xrs, d_xrs = Res(), S.dsem()
            ssq = S.ps("ssq", [128, TT])
            R_ssq = Res()
            zps = [S.ps(f"z{i}", [128, TT]) for i in range(6)]
            R_z = [Res() for _ in range(6)]
            Hv = self.fm(self.H)
            tiles = self.tok_tiles()

            def load_x(k):
                t0, t1 = tiles[k]
                S.dma("sp", x[:, :, :t1 - t0], Hv[:, :, t0:t1], d_x, wr=[R_x])

            def stage_a(k):
                t0, t1 = tiles[k]
                n = t1 - t0
                self.norm(S, x, R_x, n, V_GM, vecs, R_vec, ones, sqs, R_sq, k, ssq, R_ssq, sd, R_sd,
                          rstd, R_rstd, xn[k % 2], R_xn[k % 2])

            load_x(0)
            stage_a(0)
            zc = 0
            sc = 0
            kinds = [("q", 0, self.QT), ("k", 1024, self.KT), ("xr", 3072, self.XR), ("yr", 4096, self.GY),
                     ("gn", 5120, self.SGN), ("gl", 6144, self.SGL)]
            for k, (t0, t1) in enumerate(tiles):
                n = t1 - t0
                xb, Rb = xn[k % 2], R_xn[k % 2]
                if k + 1 < len(tiles):
                    load_x(k + 1)
                for name, off, dst in kinds:
                    if name == "xr":
                        buf, Rbuf, dbuf = xrs, R_xrs, d_xrs
                    else:
                        si = sc % 3
                        sc += 1
                        buf, Rbuf, dbuf = stg[si], R_stg[si], d_stg[si]
                    for c in range(8):
                        zi = zc % 6
                        zc += 1
                        zp = zps[zi]

                        def mm(e, c=c, zp=zp, n=n, off=off, xb=xb):
                            ins = None
                            for kk in range(8):
                                ins = e.matmul(zp[:, :n], win[:, kk, off + c * 128:off + (c + 1) * 128], xb[:, kk, :n],
                                               start=(kk == 0), stop=(kk == 7))
                            return ins
                        S.op("pe", mm, rd=[R_wb[off // 1024], Rb], wr=[R_z[zi]])
                        if name == "q":
                            S.op("dve", lambda e, c=c, zp=zp, n=n, buf=buf: e.tensor_scalar(
                                out=buf[:, c, :n], in0=zp[:, :n], scalar1=0.125, scalar2=None, op0=ALU.mult),
                                rd=[R_z[zi]], wr=[Rbuf])
                        elif name in ("k", "xr"):
                            S.op("dve", lambda e, c=c, zp=zp, n=n, buf=buf: e.tensor_copy(out=buf[:, c, :n], in_=zp[:, :n]),
                                 rd=[R_z[zi]], wr=[Rbuf])
                        else:
                            fn = AF.Gelu if name == "yr" else AF.Sigmoid
                            S.op("act", lambda e, c=c, zp=zp, n=n, buf=buf, fn=fn: e.activation(
                                out=buf[:, c, :n], in_=zp[:, :n], func=fn), rd=[R_z[zi]], wr=[Rbuf])
                    S.dma("pool", self.fm(dst)[:, :, t0:t1], buf[:, :, :n], dbuf, rd=[Rbuf])
                    if name == "k" and k + 1 < len(tiles):
                        stage_a(k + 1)
                ng = (n + 127) // 128
                for g in range(ng):
                    m = min(128, n - g * 128)
                    for hf in range(2):
                        zi = zc % 6
                        zc += 1
                        zp = zps[zi]

                        def mmv(e, g=g, m=m, hf=hf, zp=zp, xb=xb):
                            ins = None
                            for kk in range(8):
                                ins = e.matmul(zp[:m, :], xb[:, kk, g * 128:g * 128 + m],
                                               win[:, kk, 2048 + hf * 512:2048 + (hf + 1) * 512],
                                               start=(kk == 0), stop=(kk == 7))
                            return ins
                        S.op("pe", mmv, rd=[R_wb[2], Rb], wr=[R_z[zi]])
                        S.op("dve", lambda e, g=g, m=m, hf=hf, zp=zp: e.tensor_copy(
                            out=vst[:m, g, hf * 512:(hf + 1) * 512], in_=zp[:m, :]), rd=[R_z[zi]], wr=[R_vst])
                if n % 128 == 0:
                    S.dma("pool", self.V[t0:t1, :].rearrange("(g p) d -> p g d", p=128), vst[:, :ng, :], d_vst, rd=[R_vst])
                else:
                    S.dma("pool", self.V[t0:t1, :], vst[:n, 0, :], d_vst, rd=[R_vst])
            S.replay()

    def phase_attn(self, l):
        nc = self.nc
        with contextlib.ExitStack() as es:
            S = Sched(nc, es, f"at{l}")
            ones = S.sb("ones", [128, 128], BF16)
            R_ones = Res()
            S.op("pool", lambda e: e.memset(ones[:, :], 1.0), wr=[R_ones])
            rfull = S.sb("rfull", [128, 16, 1024], BF16)
            rint = S.sb("rint", [128, 16, 1024], BF16)
            R_rfull, R_rint = Res(), Res()
            S.dma("sp", rfull[:, :, :], self.MSK[l, 0], S.dsem(), wr=[R_rfull])
            S.dma("sp", rint[:, :, :], self.MSK[l, 1], S.dsem(), wr=[R_rint])

            NS = 8
            kt = [S.sb(f"kt{i}", [128, 8, 128], BF16) for i in range(NS)]
            vt = [S.sb(f"vt{i}", [128, D], BF16) for i in range(NS)]
            R_kv = [Res() for _ in range(NS)]
            d_kv = [S.dsem() for _ in range(NS)]
            ktm = S.sb("ktm", [128, 8, NMETA], BF16)
            vm = S.sb("vm", [NMETA, D], BF16)
            R_m, d_m = Res(), S.dsem()
            qt = [S.sb(f"qt{i}", [128, 8, 256], BF16) for i in range(2)]
            R_q = [Res(), Res()]
            d_q = [S.dsem(), S.dsem()]
            for i in range(2):
                S.op("pool", lambda e, i=i: e.memset(qt[i][:, :, :], 0.0), wr=[R_q[i]])
            ot = [S.sb(f"ot{i}", [128, 8, 128], BF16) for i in range(2)]
            R_ot = [Res(), Res()]
            d_ot = [S.dsem(), S.dsem()]
            eb = [S.sb(f"eb{i}", [128, 6, 256], BF16) for i in range(3)]
            R_eb = [Res() for _ in range(3)]
            rc = [S.sb(f"rc{i}", [128, 256], F32) for i in range(2)]
            R_rc = [Res(), Res()]
            stp = [S.ps(f"st{i}", [128, 6, 256]) for i in range(2)]
            R_st = [Res(), Res()]
            odp = [S.ps(f"od{i}", [128, 512]) for i in range(2)]
            R_od = [Res(), Res()]
            QTv, KTv, NATv = self.fm(self.QT), self.fm(self.KT), self.fm(self.NAT)
            cnt = dict(q=0, h=0)

            jobs = []
            load_meta_cur = [None]

            def do_queries(qtok0, nq, kcs, p, slot_of, table, R_table, pre=None):
                qi = cnt["q"] % 2
                cnt["q"] += 1
                nk = len(kcs)
                ma = mb = 0
                if nk:
                    d_hi, d_lo = kcs[0] - p, kcs[-1] - p
                    ma, mb = (7 - 2 * d_hi) * 64, (9 - 2 * d_lo) * 64
                    assert mb - ma == nk * 128 and ma >= 0 and mb <= 1024
                for hp in range(8):
                    hi = cnt["h"]
                    cnt["h"] += 1
                    st, Rst = stp[hi % 2], R_st[hi % 2]
                    e_, Re = eb[hi % 3], R_eb[hi % 3]
                    od, Rod = odp[hi % 2], R_od[hi % 2]
                    rcb, Rrc = rc[hi % 2], R_rc[hi % 2]

                    def stage_a(hp=hp, st=st, Rst=Rst, e_=e_, Re=Re):
                        if hp == 0:
                            if pre is not None:
                                pre()
                            S.dma("sp", qt[qi][0:64, :, 0:nq], QTv[0:64, :, qtok0:qtok0 + nq], d_q[qi], wr=[R_q[qi]])
                            S.dma("sp", qt[qi][64:128, :, 128:128 + nq], QTv[64:128, :, qtok0:qtok0 + nq], d_q[qi], wr=[R_q[qi]])

                        def qk(e):
                            for i, kc in enumerate(kcs):
                                e.matmul(st[:, i, :], kt[slot_of(kc)][:, hp, :], qt[qi][:, hp, :], start=True, stop=True)
                            return e.matmul(st[:NMETA, nk, :], ktm[:, hp, :], qt[qi][:, hp, :], start=True, stop=True)
                        S.op("pe", qk, rd=[R_q[qi], R_m] + [R_kv[slot_of(kc)] for kc in kcs], wr=[Rst])
                        if nk:
                            S.op("act", lambda e: e.activation(out=e_[:, :nk, :], in_=st[:, :nk, :], func=AF.Exp),
                                 rd=[Rst], wr=[Re])
                        S.op("act", lambda e: e.activation(out=e_[:NMETA, nk, :], in_=st[:NMETA, nk, :], func=AF.Exp),
                             rd=[Rst], wr=[Re])
                        if nk:
                            ev = e_[:, :nk, :].rearrange("p i (h q) -> p i h q", h=2)
                            tv = table[:, 2 * hp:2 * hp + 2, ma:mb].rearrange("p h (i q) -> p i h q", q=128)
                            S.op("dve", lambda e: e.tensor_tensor(out=ev, in0=ev, in1=tv, op=ALU.mult),
                                 rd=[Re, R_table], wr=[Re])

                    def stage_b(hp=hp, e_=e_, Re=Re, od=od, Rod=Rod, rcb=rcb, Rrc=Rrc):
                        def pv(e):
                            for i, kc in enumerate(kcs):
                                e.matmul(od[:, 0:256], vt[slot_of(kc)][:, hp * 128:(hp + 1) * 128], e_[:, i, :],
                                         start=(i == 0), stop=False)
                            e.matmul(od[:, 0:256], vm[:, hp * 128:(hp + 1) * 128], e_[:NMETA, nk, :],
                                     start=(nk == 0), stop=True)
                            for i, kc in enumerate(kcs):
                                e.matmul(od[:, 256:512], ones[:, :], e_[:, i, :], start=(i == 0), stop=False)
                            return e.matmul(od[:, 256:512], ones[:NMETA, :], e_[:NMETA, nk, :], start=(nk == 0), stop=True)
                        S.op("pe", pv, rd=[Re, R_m, R_ones] + [R_kv[slot_of(kc)] for kc in kcs], wr=[Rod])
                        S.op("act", lambda e: e.activation(out=rcb[:, :], in_=od[:, 256:512], func=AF.Ln),
                             rd=[Rod], wr=[Rrc])
                        S.op("act", lambda e: e.activation(out=rcb[:, :], in_=rcb[:, :], func=AF.Exp, scale=-1.0),
                             rd=[Rrc], wr=[Rrc])
                        S.op("dve", lambda e: e.tensor_tensor(out=ot[qi][0:64, hp, :nq], in0=od[0:64, 0:nq], in1=rcb[0:64, 0:nq],
                                                              op=ALU.mult),
                             rd=[Rod, Rrc], wr=[R_ot[qi]])
                        S.op("dve", lambda e: e.tensor_tensor(out=ot[qi][64:128, hp, :nq], in0=od[64:128, 128:128 + nq],
                                                              in1=rcb[64:128, 128:128 + nq], op=ALU.mult),
                             rd=[Rod, Rrc], wr=[R_ot[qi]])
                        if hp == 7:
                            S.dma("pool", NATv[:, :, qtok0:qtok0 + nq], ot[qi][:, :, :nq], d_ot[qi], rd=[R_ot[qi]])
                    jobs.append((stage_a, stage_b, pre is load_meta_cur[0] and hp == 0))

            gslot = [0]
            for kind, idx, T, base in self.seqs:
                P = T // 128
                assert P >= 4
                s0 = gslot[0]
                slot_of = lambda kc, s0=s0: (s0 + kc) % NS
                gslot[0] += P

                def load_meta(base=base):
                    S.dma("sp", ktm[:, :, :], KTv[:, :, base:base + NMETA], d_m, wr=[R_m])
                    S.dma("sp", vm[:, :], self.V[base:base + NMETA, :], d_m, wr=[R_m])

                def load_kv(kc, base=base, slot_of=slot_of):
                    sl = slot_of(kc)
                    tk = base + NMETA + kc * 128
                    S.dma("sp", kt[sl][:, :, :], KTv[:, :, tk:tk + 128], d_kv[sl], wr=[R_kv[sl]])
                    S.dma("sp", vt[sl][:, :], self.V[tk:tk + 128, :], d_kv[sl], wr=[R_kv[sl]])
                load_meta_cur[0] = load_meta
                do_queries(base, NMETA, [], 0, slot_of, rfull, R_rfull, pre=load_meta)
                loaded = 0
                for p in range(P):
                    if p < 2:
                        kcs = [3, 2, 1, 0]
                        table, Rt = rfull, R_rfull
                    elif p >= P - 2:
                        kcs = [P - 1, P - 2, P - 3, P - 4]
                        table, Rt = rfull, R_rfull
                    else:
                        kcs = [p + 2, p + 1, p, p - 1, p - 2]
                        table, Rt = rint, R_rint
                    need = min(P, max(kcs) + 3)
                    lst = list(range(loaded, need))
                    loaded = max(loaded, need)
                    pre = (lambda lst=lst, load_kv=load_kv: [load_kv(kc) for kc in lst]) if lst else None
                    do_queries(base + NMETA + p * 128, 128, kcs, p, slot_of, table, Rt, pre=pre)
            pend = None
            for i in range(len(jobs)):
                if jobs[i][2] and pend is not None:
                    pend()
                    pend = None
                jobs[i][0]()
                if pend is not None:
                    pend()
                pend = jobs[i][1]
            pend()
            S.replay()

    def phase_lru(self, l):
        nc = self.nc
        with contextlib.ExitStack() as es:
            S = Sched(nc, es, f"lr{l}")
            vecs = S.sb("vecs", [128, NV], F32)
            R_vec = Res()
            S.dma("sp", vecs[:, :], self.vec[l], S.dsem(), wr=[R_vec])
            wst = S.sb("wst", [128, 4, 8, 128], F32)
            R_wst = Res()
            dws = S.dsem()
            S.op("pool", lambda e: e.memset(wst[:, :, :, :], 0.0), wr=[R_wst])
            first = True
            for d in range(2):
                for kd, nm in enumerate(("lwa", "lwx")):
                    src = self.w[nm][l, d].rearrange("(c h) i j -> h i c j", h=2)
                    for half in range(2):
                        S.dma("sp", wst[half * 64:(half + 1) * 64, d * 2 + kd, :, half * 64:(half + 1) * 64], src[half],
                              dws, rd=[R_wst] if first else [], bulk=True)
                        first = False
            R_wst.w = Ev(dws, None, "dma")
            one = S.sb("one", [128, 1], F32)
            S.op("pool", lambda e: e.memset(one[:, :], 1.0), wr=[R_vec])
            self.one_ap = one[:, 0:1]
            nsp = S.sb("nsp", [128, 16], F32)
            R_nsp = Res()
            S.op("act", lambda e: e.activation(out=nsp[:, :], in_=vecs[:, V_LAM:V_LAM + 16], func=AF.Exp, scale=-1.0),
                 rd=[R_vec], wr=[R_nsp])
            S.op("act", lambda e: e.activation(out=nsp[:, :], in_=nsp[:, :], func=AF.Ln, bias=self.one_ap),
                 rd=[R_nsp, R_vec], wr=[R_nsp])
            S.op("dve", lambda e: e.tensor_scalar(out=nsp[:, :], in0=nsp[:, :], scalar1=-8.0, scalar2=None, op0=ALU.mult),
                 rd=[R_nsp], wr=[R_nsp])

            hb2 = S.sb("hb2", [128, 32], F32)
            nsph = S.sb("nsph", [128, 16], F32)
            quart = S.sb("quart", [128, 1], F32)
            R_hb2 = Res()
            S.op("dve", lambda e: e.tensor_scalar(out=hb2[:, :], in0=vecs[:, V_BA:V_BA + 32], scalar1=0.5, scalar2=None,
                                                  op0=ALU.mult), rd=[R_vec], wr=[R_hb2])
            S.op("dve", lambda e: e.tensor_scalar(out=nsph[:, :], in0=nsp[:, :], scalar1=0.5, scalar2=None, op0=ALU.mult),
                 rd=[R_nsp], wr=[R_hb2])
            S.op("pool", lambda e: e.memset(quart[:, :], 0.25), wr=[R_hb2])
            SMAX = self.smax
            NB = 2
            NX = 3
            W = SMAX + 4
            mk = lambda nm, dt, n=NB: [S.sb(f"{nm}{i}", [128, W], dt) for i in range(n)]
            xrb, rb, ib, tb, hb, hfin = (mk(n_, F32) for n_ in ("xrb", "rb", "ib", "tb", "hb", "hfin"))
            xc = mk("xc", F32, NX)
            gy, ob = (mk(n_, BF16) for n_ in ("gy", "ob"))
            mkr = lambda n=NB: [Res() for _ in range(n)]
            R_xrb, R_rb, R_ib, R_tb, R_hb, R_hfin, R_gy, R_ob = (mkr() for _ in range(8))
            R_xc = mkr(NX)
            d_xrb, d_hb, d_hfin, d_gy, d_ob = ([S.dsem() for _ in range(NB)] for _ in range(5))
            carry = [S.sb(f"carry{i}", [128, 1], F32) for i in range(2)]
            R_carry = [Res(), Res()]
            rps = [S.ps(f"r{i}", [128, 512]) for i in range(2)]
            ips = [S.ps(f"i{i}", [128, 512]) for i in range(2)]
            R_rp, R_ip = [Res(), Res()], [Res(), Res()]
            XRv, HFv, GYv, LTv = self.fm(self.XR), self.fm(self.HF), self.fm(self.GY), self.fm(self.LT)
            pc = [0]
            xci = [0]
            jobs = []

            XCv = self.fm(self.XC)
            d_xcs = [S.dsem() for _ in range(NX)]
            d_xcl = [S.dsem() for _ in range(NX)]

            def make_job(ji, base, L, c, s0, s1, d, has_carry, reuse_xc, hfres, xcres=None, xmode="compute"):
                k = ji % NB
                Sg = s1 - s0
                cidx = d
                if not reuse_xc:
                    xci[0] += 1
                kx = xci[0] % NX
                cw = lambda j: vecs[:, V_CW + j * 8 + c:V_CW + j * 8 + c + 1]
                hba = hb2[:, d * 8 + c:d * 8 + c + 1]
                hbx = hb2[:, 16 + d * 8 + c:16 + d * 8 + c + 1]

                def ldf():
                    if xmode == "load":
                        S.dma("sp", xc[kx][:, :Sg], XCv[:, c, base + s0:base + s1], d_xcl[kx], rd=[xcres], wr=[R_xc[kx]])
                    elif not reuse_xc:
                        lo, hi = max(0, s0 - 2), min(L, s1 + 1)
                        if s0 - 2 < 0:
                            S.op("pool", lambda e: e.memset(xrb[k][:, 0:2], 0.0), wr=[R_xrb[k]])
                        if s1 + 1 > L:
                            S.op("pool", lambda e: e.memset(xrb[k][:, Sg + 2:Sg + 3], 0.0), wr=[R_xrb[k]])
                        S.dma("sp", xrb[k][:, lo - (s0 - 2):hi - (s0 - 2)], XRv[:, c, base + lo:base + hi], d_xrb[k],
                              wr=[R_xrb[k]])

                def s1f():
                    if not reuse_xc and xmode == "compute":
                        S.op("dve", lambda e: e.tensor_scalar(out=xc[kx][:, :Sg], in0=xrb[k][:, 0:Sg], scalar1=cw(0),
                                                              scalar2=vecs[:, V_CB + c:V_CB + c + 1], op0=ALU.mult, op1=ALU.add),
                             rd=[R_xrb[k], R_vec], wr=[R_xc[kx]])
                        for j in range(1, 4):
                            S.op("dve", lambda e, j=j: e.scalar_tensor_tensor(out=xc[kx][:, :Sg], in0=xrb[k][:, j:j + Sg],
                                                                              scalar=cw(j), in1=xc[kx][:, :Sg],
                                                                              op0=ALU.mult, op1=ALU.add),
                                 rd=[R_xrb[k], R_xc[kx], R_vec], wr=[R_xc[kx]])
                        if xcres is not None:
                            S.dma("pool", XCv[:, c, base + s0:base + s1], xc[kx][:, :Sg], d_xcs[kx], rd=[R_xc[kx]], wr=[xcres])
                    for a_, b_ in col_tiles(Sg, 512):
                        pb = pc[0] % 2
                        pc[0] += 1
                        S.op("pe", lambda e, a_=a_, b_=b_, pb=pb: e.matmul(rps[pb][:, :b_ - a_], wst[:, d * 2, c, :],
                                                                            xc[kx][:, a_:b_], start=True, stop=True),
                             rd=[R_wst, R_xc[kx]], wr=[R_rp[pb]])
                        S.op("pe", lambda e, a_=a_, b_=b_, pb=pb: e.matmul(ips[pb][:, :b_ - a_], wst[:, d * 2 + 1, c, :],
                                                                            xc[kx][:, a_:b_], start=True, stop=True),
                             rd=[R_wst, R_xc[kx]], wr=[R_ip[pb]])
                        S.op("act", lambda e, a_=a_, b_=b_, pb=pb: e.activation(out=rb[k][:, a_:b_], in_=rps[pb][:, :b_ - a_],
                                                                                 func=AF.Tanh, scale=0.5, bias=hba),
                             rd=[R_rp[pb], R_hb2], wr=[R_rb[k]])
                        S.op("act", lambda e, a_=a_, b_=b_, pb=pb: e.activation(out=ib[k][:, a_:b_], in_=ips[pb][:, :b_ - a_],
                                                                                 func=AF.Tanh, scale=0.5, bias=hbx),
                             rd=[R_ip[pb], R_hb2], wr=[R_ib[k]])
                    S.op("act", lambda e: e.activation(out=rb[k][:, :Sg], in_=rb[k][:, :Sg], func=AF.Exp,
                                                       scale=nsph[:, d * 8 + c:d * 8 + c + 1],
                                                       bias=nsph[:, d * 8 + c:d * 8 + c + 1]),
                         rd=[R_rb[k], R_hb2], wr=[R_rb[k]])
                    S.op("act", lambda e: e.activation(out=tb[k][:, :Sg], in_=rb[k][:, :Sg], func=AF.Square),
                         rd=[R_rb[k]], wr=[R_tb[k]])
                    S.op("act", lambda e: e.activation(out=tb[k][:, :Sg], in_=tb[k][:, :Sg], func=AF.Sqrt, scale=-0.25,
                                                       bias=quart[:, 0:1]),
                         rd=[R_tb[k], R_hb2], wr=[R_tb[k]])

                def pref():
                    if d == 1:
                        S.dma("sp", hfin[k][:, :Sg], HFv[:, c, base + s0:base + s1], d_hfin[k], rd=[hfres], wr=[R_hfin[k]])
                        S.dma("sp", gy[k][:, :Sg], GYv[:, c, base + s0:base + s1], d_gy[k], wr=[R_gy[k]])

                def s2f():
                    S.op("dve", lambda e: e.scalar_tensor_tensor(out=ib[k][:, :Sg], in0=ib[k][:, :Sg], scalar=1.0,
                                                                 in1=xc[kx][:, :Sg], op0=ALU.add, op1=ALU.mult),
                         rd=[R_ib[k], R_xc[kx]], wr=[R_ib[k]])
                    S.op("dve", lambda e: e.tensor_tensor(out=ib[k][:, :Sg], in0=ib[k][:, :Sg], in1=tb[k][:, :Sg], op=ALU.mult),
                         rd=[R_ib[k], R_tb[k]], wr=[R_ib[k]])
                    init = carry[cidx][:, 0:1] if has_carry else 0.0
                    rdc = [R_carry[cidx]] if has_carry else []
                    if d == 0:
                        S.op("dve", lambda e: e.tensor_tensor_scan(out=hb[k][:, :Sg], data0=rb[k][:, :Sg], data1=ib[k][:, :Sg],
                                                                   initial=init, op0=ALU.mult, op1=ALU.add),
                             rd=[R_rb[k], R_ib[k]] + rdc, wr=[R_hb[k]])
                        S.op("dve", lambda e: e.tensor_copy(out=carry[cidx][:, 0:1], in_=hb[k][:, Sg - 1:Sg]),
                             rd=[R_hb[k]], wr=[R_carry[cidx]])
                        S.dma("pool", HFv[:, c, base + s0:base + s1], hb[k][:, :Sg], d_hb[k], rd=[R_hb[k]], wr=[hfres])
                    else:
                        S.op("dve", lambda e: e.tensor_tensor_scan(out=hb[k][:, :Sg][:, ::-1], data0=rb[k][:, :Sg][:, ::-1],
                                                                   data1=ib[k][:, :Sg][:, ::-1], initial=init,
                                                                   op0=ALU.mult, op1=ALU.add),
                             rd=[R_rb[k], R_ib[k]] + rdc, wr=[R_hb[k]])
                        S.op("dve", lambda e: e.tensor_copy(out=carry[cidx][:, 0:1], in_=hb[k][:, 0:1]),
                             rd=[R_hb[k]], wr=[R_carry[cidx]])
                        S.op("pool", lambda e: e.tensor_tensor(out=hfin[k][:, :Sg], in0=hfin[k][:, :Sg], in1=hb[k][:, :Sg],
                                                               op=ALU.add),
                             rd=[R_hfin[k], R_hb[k]], wr=[R_hfin[k]])
                        S.op("pool", lambda e: e.tensor_tensor(out=ob[k][:, :Sg], in0=hfin[k][:, :Sg], in1=gy[k][:, :Sg],
                                                               op=ALU.mult),
                             rd=[R_hfin[k], R_gy[k]], wr=[R_ob[k]])
                        S.dma("pool", LTv[:, c, base + s0:base + s1], ob[k][:, :Sg], d_ob[k], rd=[R_ob[k]])
                jobs.append(dict(ld=ldf, s1=s1f, s2=s2f, pre=pref))

            for kind, idx, T, base in self.seqs:
                L = NMETA + T
                nseg = (L + SMAX - 1) // SMAX
                assert L % nseg == 0
                Sg = L // nseg
                segs = [(i * Sg, (i + 1) * Sg) for i in range(nseg)]
                for c in range(8):
                    hfres = Res()
                    xcres = Res() if nseg > 1 else None
                    for i, (s0, s1) in enumerate(segs):
                        make_job(len(jobs), base, L, c, s0, s1, 0, i != 0, False, hfres,
                                 xcres=xcres if i != nseg - 1 else None)
                    for i, (s0, s1) in reversed(list(enumerate(segs))):
                        if i == nseg - 1:
                            make_job(len(jobs), base, L, c, s0, s1, 1, False, True, hfres)
                        else:
                            make_job(len(jobs), base, L, c, s0, s1, 1, True, False, hfres, xcres=xcres, xmode="load")
            n = len(jobs)
            jobs[0]["ld"]()
            if n > 1:
                jobs[1]["ld"]()
            jobs[0]["s1"]()
            for i in range(n):
                if i + 2 < n:
                    jobs[i + 2]["ld"]()
                if i + 1 < n:
                    jobs[i + 1]["s1"]()
                if i == 0:
                    jobs[0]["pre"]()
                jobs[i]["s2"]()
                if i + 1 < n:
                    jobs[i + 1]["pre"]()
            S.replay()

    def phase_mixout(self, l):
        nc = self.nc
        with contextlib.ExitStack() as es:
            S = Sched(nc, es, f"mo{l}")
            wna = S.sb("wna", [128, 8, D], BF16)
            wlr = S.sb("wlr", [128, 8, D], BF16)
            wou = S.sb("wou", [128, 8, D], BF16)
            R_w, R_w2 = Res(), Res()
            dw, dw2 = S.dsem(), S.dsem()
            for dst, nm in ((wna, "wna"), (wlr, "wlru")):
                self.load_w(S, dst, self.w[nm][l], dw, R_w, 8, D)
            self.load_w(S, wou, self.w["wout"][l], dw2, R_w2, 8, D)
            NB = 2
            mk = lambda nm, dt: [S.sb(f"{nm}{i}", [128, 8, TT], dt) for i in range(NB)]
            x, nat, lt, sgn, sgl = mk("x", F32), mk("nat", BF16), mk("lt", BF16), mk("sgn", BF16), mk("sgl", BF16)
            R_x, R_nat, R_lt, R_sgn, R_sgl = ([Res() for _ in range(NB)] for _ in range(5))
            d_x, d_nat, d_lt, d_sgn, d_sgl, d_xs = ([S.dsem() for _ in range(NB)] for _ in range(6))
            mg = S.sb("mg", [128, 8, TT], BF16)
            R_mg = [Res() for _ in range(8)]
            t1 = [S.sb(f"t1{i}", [128, TT], F32) for i in range(2)]
            t2 = [S.sb(f"t2{i}", [128, TT], F32) for i in range(2)]
            R_t1, R_t2 = [Res(), Res()], [Res(), Res()]
            nps = [S.ps(f"n{i}", [128, TT]) for i in range(2)]
            lps = [S.ps(f"l{i}", [128, TT]) for i in range(2)]
            ops = [S.ps(f"o{i}", [128, TT]) for i in range(2)]
            R_n, R_l, R_o = [Res(), Res()], [Res(), Res()], [Res(), Res()]
            Hv = self.fm(self.H)
            tiles = self.tok_tiles()

            def loads(k):
                t0, t1_ = tiles[k]
                n = t1_ - t0
                b = k % NB
                for buf, R, ds, src in ((nat, R_nat, d_nat, self.NAT), (lt, R_lt, d_lt, self.LT),
                                        (sgn, R_sgn, d_sgn, self.SGN), (sgl, R_sgl, d_sgl, self.SGL), (x, R_x, d_x, self.H)):
                    S.dma("sp", buf[b][:, :, :n], self.fm(src)[:, :, t0:t1_], ds[b], wr=[R[b]])

            loads(0)
            for k, (t0, t1_) in enumerate(tiles):
                n = t1_ - t0
                b = k % NB
                if k + 1 < len(tiles):
                    loads(k + 1)
                for c in range(8):
                    pb = c % 2

                    def mmn(e, c=c, pb=pb, n=n, b=b):
                        ins = None
                        for kk in range(8):
                            ins = e.matmul(nps[pb][:, :n], wna[:, kk, c * 128:(c + 1) * 128], nat[b][:, kk, :n],
                                           start=(kk == 0), stop=(kk == 7))
                        return ins

                    def mml(e, c=c, pb=pb, n=n, b=b):
                        ins = None
                        for kk in range(8):
                            ins = e.matmul(lps[pb][:, :n], wlr[:, kk, c * 128:(c + 1) * 128], lt[b][:, kk, :n],
                                           start=(kk == 0), stop=(kk == 7))
                        return ins
                    S.op("pe", mmn, rd=[R_w, R_nat[b]], wr=[R_n[pb]])
                    S.op("pe", mml, rd=[R_w, R_lt[b]], wr=[R_l[pb]])
                    S.op("dve", lambda e, c=c, pb=pb, n=n, b=b: e.tensor_tensor(out=t1[pb][:, :n], in0=nps[pb][:, :n],
                                                                                in1=sgn[b][:, c, :n], op=ALU.mult),
                         rd=[R_n[pb], R_sgn[b]], wr=[R_t1[pb]])
                    S.op("dve", lambda e, c=c, pb=pb, n=n, b=b: e.tensor_tensor(out=t2[pb][:, :n], in0=lps[pb][:, :n],
                                                                                in1=sgl[b][:, c, :n], op=ALU.mult),
                         rd=[R_l[pb], R_sgl[b]], wr=[R_t2[pb]])
                    S.op("pool", lambda e, c=c, pb=pb, n=n: e.tensor_tensor(out=mg[:, c, :n], in0=t1[pb][:, :n],
                                                                            in1=t2[pb][:, :n], op=ALU.add),
                         rd=[R_t1[pb], R_t2[pb]], wr=[R_mg[c]])
                for c in range(8):
                    pb = c % 2

                    def mmo(e, c=c, pb=pb, n=n):
                        ins = None
                        for kk in range(8):
                            ins = e.matmul(ops[pb][:, :n], wou[:, kk, c * 128:(c + 1) * 128], mg[:, kk, :n],
                                           start=(kk == 0), stop=(kk == 7))
                        return ins
                    S.op("pe", mmo, rd=[R_w2] + R_mg, wr=[R_o[pb]])
                    S.op("dve", lambda e, c=c, pb=pb, n=n, b=b: e.tensor_tensor(out=x[b][:, c, :n], in0=ops[pb][:, :n],
                                                                                in1=x[b][:, c, :n], op=ALU.add),
                         rd=[R_o[pb], R_x[b]], wr=[R_x[b]])
                S.dma("pool", Hv[:, :, t0:t1_], x[b][:, :, :n], d_xs[b], rd=[R_x[b]])
            S.replay()

    def phase_final(self):
        nc = self.nc
        with contextlib.ExitStack() as es:
            S = Sched(nc, es, "fin")
            ones, R_ones, vecs, R_vec = self.consts(S, 0)
            idt = S.sb("idt", [128, 128], F32)
            R_id = Res()
            S.dma("sp", idt[:, :], self.ident, S.dsem(), wr=[R_id])
            x = [S.sb(f"x{i}", [128, 8, TT], F32) for i in range(2)]
            R_x, d_x = [Res(), Res()], [S.dsem(), S.dsem()]
            xn = [S.sb(f"xn{i}", [128, 8, TT], F32) for i in range(2)]
            R_xn = [Res(), Res()]
            yt = [S.sb(f"yt{i}", [128, 4, D], F32) for i in range(2)]
            R_yt, d_yt = [Res(), Res()], [S.dsem(), S.dsem()]
            sqs = [S.sb(f"sq{i}", [128, TT], BF16) for i in range(2)]
            R_sq = [Res(), Res()]
            sd = S.sb("sd", [128, TT], F32)
            rstd = S.sb("rstd", [128, TT], F32)
            R_sd, R_rstd = Res(), Res()
            ssq = S.ps("ssq", [128, TT])
            R_ssq = Res()
            tps = [S.ps(f"t{i}", [128, 512]) for i in range(4)]
            R_tp = [Res() for _ in range(4)]
            Hv = self.fm(self.H)
            jobs = []
            for kind, idx, T, base in self.seqs:
                dst = self.yp if kind == "p" else self.ys
                for a, b in col_tiles(T, TT):
                    jobs.append((dst[idx, a:b, :], base + NMETA + a, b - a))
            pc = 0
            pcc = [0]

            def ld(k):
                dst, t0, n = jobs[k]
                S.dma("sp", x[k % 2][:, :, :n], Hv[:, :, t0:t0 + n], d_x[k % 2], wr=[R_x[k % 2]])

            def st_a(k):
                dst, t0, n = jobs[k]
                b = k % 2
                self.norm(S, x[b], R_x[b], n, V_GF, vecs, R_vec, ones, sqs, R_sq, k, ssq, R_ssq, sd, R_sd,
                          rstd, R_rstd, xn[b], R_xn[b])

            def st_b(k):
                dst, t0, n = jobs[k]
                b = k % 2
                ng = n // 128
                for g in range(ng):
                    for hf in range(2):
                        pi = pcc[0] % 4
                        pcc[0] += 1
                        tp = tps[pi]

                        def tr(e, g=g, hf=hf, tp=tp, b=b):
                            ins = None
                            for j in range(4):
                                ins = e.transpose(tp[:, j * 128:(j + 1) * 128], xn[b][:, hf * 4 + j, g * 128:(g + 1) * 128], idt[:, :])
                            return ins
                        S.op("pe", tr, rd=[R_xn[b], R_id], wr=[R_tp[pi]])
                        if hf == 0:
                            S.op("dve", lambda e, g=g, hf=hf, tp=tp, b=b: e.tensor_copy(out=yt[b][:, g, hf * 512:(hf + 1) * 512], in_=tp[:, :]),
                                 rd=[R_tp[pi]], wr=[R_yt[b]])
                        else:
                            S.op("act", lambda e, g=g, hf=hf, tp=tp, b=b: e.activation(out=yt[b][:, g, hf * 512:(hf + 1) * 512], in_=tp[:, :], func=AF.Copy),
                                 rd=[R_tp[pi]], wr=[R_yt[b]])
                S.dma("pool", dst.rearrange("(g p) d -> p g d", p=128), yt[b][:, :ng, :], d_yt[b], rd=[R_yt[b]])

            nj = len(jobs)
            ld(0)
            st_a(0)
            for k in range(nj):
                if k + 1 < nj:
                    ld(k + 1)
                    st_a(k + 1)
                st_b(k)
            S.replay()

    def dump(self, name):
        if not self.debug:
            return
        nc = self.nc
        with contextlib.ExitStack() as es:
            S = Sched(nc, es, "dump" + name)
            S.dma("sp", self.dbg[name], self.H, S.dsem())
            S.replay()

    def build(self, stop_after=None):
        self.phase_init()
        self.dump("H0")
        for l in range(DEPTH):
            self.phase_ffn(l, 1)
            if l == 0:
                self.dump("H1")
            self.phase_mixin(l)
            self.phase_attn(l)
            self.phase_lru(l)
            self.phase_mixout(l)
            if l == 0:
                self.dump("H2")
            self.phase_ffn(l, 2)
            if l == 0:
                self.dump("H3")
        self.phase_final()
        return self.nc


def pack_vec(inp, l):
    v = np.zeros((128, NV), np.float32)
    pc = lambda a: np.ascontiguousarray(a.reshape(8, 128).T)
    v[:, V_G1:V_G1 + 8] = pc(inp["norm_ffn1"][l])
    v[:, V_GM:V_GM + 8] = pc(inp["norm_mix"][l])
    v[:, V_G2:V_G2 + 8] = pc(inp["norm_ffn2"][l])
    for j in range(4):
        v[:, V_CW + j * 8:V_CW + j * 8 + 8] = pc(inp["conv_w"][l, j])
    v[:, V_CB:V_CB + 8] = pc(inp["conv_b"][l])
    for d in range(2):
        v[:, V_BA + d * 8:V_BA + d * 8 + 8] = pc(inp["lru_ba"][l, d])
        v[:, V_BX + d * 8:V_BX + d * 8 + 8] = pc(inp["lru_bx"][l, d])
        v[:, V_LAM + d * 8:V_LAM + d * 8 + 8] = pc(inp["lru_lambda"][l, d])
    v[:, V_GF:V_GF + 8] = pc(inp["final_norm"])
    return v


def shared_inputs(inp):
    f = lambda a: np.ascontiguousarray(np.asarray(a, dtype=np.float32))
    m = dict(
        meta=f(inp["meta_tokens"]), ident=np.eye(128, dtype=np.float32),
        vec=np.stack([pack_vec(inp, l) for l in range(DEPTH)]),
        fbias=f(np.asarray(inp["na_rel_bias"])[:, :, ::-1, ::-1]),
        f1g=f(inp["ffn1_w_gate"]), f1u=f(inp["ffn1_w_up"]), f1d=f(inp["ffn1_w_down"]),
        win=f(inp["w_in"]), wna=f(inp["w_na_proj"]), wlru=f(inp["w_lru_proj"]), wout=f(inp["w_out"]),
        f2g=f(inp["ffn2_w_gate"]), f2u=f(inp["ffn2_w_up"]), f2d=f(inp["ffn2_w_down"]),
        lwa=f(inp["lru_wa"]), lwx=f(inp["lru_wx"]),
    )
    return m


def run(inp, n_cores, debug=False):
    inp = {k: np.asarray(v) for k, v in inp.items()}
    xp, xs = inp["x_prompt"], inp["x_sample"]
    n_p, t_p = xp.shape[0] // n_cores, xp.shape[1]
    n_s, t_s = xs.shape[0] // n_cores, xs.shape[1]
    bld = Builder(n_p, t_p, n_s, t_s, debug=debug)
    nc = bld.build()
    sh = shared_inputs(inp)
    in_maps = []
    for i in range(n_cores):
        m = dict(sh)
        m["xp"] = np.ascontiguousarray(xp[i * n_p:(i + 1) * n_p])
        m["xs"] = np.ascontiguousarray(xs[i * n_s:(i + 1) * n_s])
        in_maps.append(m)
    res = run_bass_kernel_spmd(nc, in_maps, core_ids=list(range(n_cores)))
    yp = np.concatenate([r["yp"] for r in res.results], axis=0)
    ys = np.concatenate([r["ys"] for r in res.results], axis=0)
    return (yp, ys), res, bld


def kernel(**inputs):
    (yp, ys), _, _ = run(inputs, NCORES)
    return (yp.astype(np.float32), ys.astype(np.float32))
```

```python
import contextlib
import numpy as np
import concourse.bass as bass
import concourse.mybir as mybir
from concourse.bass_utils import run_bass_kernel_spmd

F32 = mybir.dt.float32
BF16 = mybir.dt.bfloat16
AF = mybir.ActivationFunctionType
ALU = mybir.AluOpType

D = 1024
DFF = 2816
NFF = DFF // 128
DEPTH = 2
NMETA = 16
GW = 64
INW = 7168
EPS = 1e-6
NCORES = 8
TT = 512

V_G1, V_GM, V_G2, V_CW, V_CB, V_BA, V_BX, V_LAM, V_GF = 0, 8, 16, 24, 56, 64, 80, 96, 112
NV = 120


class Ev:
    __slots__ = ("ds", "val", "eng")

    def __init__(self, ds, val, eng):
        self.ds, self.val, self.eng = ds, val, eng

    def value(self):
        return self.val if self.val is not None else 16 * self.ds.cnt


class DSem:
    def __init__(self, sem):
        self.sem, self.cnt = sem, 0


class Res:
    def __init__(self, name=""):
        self.name = name
        self.w = None
        self.rs = {}


class Sched:
    ENGS = ("sp", "act", "dve", "pool", "pe")

    def __init__(self, nc, es, tag):
        self.nc, self.es, self.tag = nc, es, tag
        self.q = {e: [] for e in self.ENGS}
        self.sems = []
        self.esem = {e: DSem(self._sem(f"{tag}_s_{e}")) for e in ("act", "dve", "pool", "pe")}
        self.dsems = []
        self.nalloc = 0

    def _sem(self, name):
        h = self.nc.alloc_semaphore(name=name)
        self.sems.append(h)
        return h

    def dsem(self):
        self.nalloc += 1
        d = DSem(self._sem(f"{self.tag}_d{self.nalloc}"))
        self.dsems.append(d)
        return d

    def sb(self, name, shape, dt):
        return self.es.enter_context(self.nc.sbuf_tensor(f"{self.tag}_{name}", shape, dt))

    def ps(self, name, shape, dt=F32):
        return self.es.enter_context(self.nc.psum_tensor(f"{self.tag}_{name}", shape, dt))

    def _deps(self, eng, rd, wr):
        waits = []
        for r in rd:
            if r.w is not None:
                waits.append((r.w, True))
        for r in wr:
            if r.w is not None:
                waits.append((r.w, False))
            for ev in r.rs.values():
                waits.append((ev, False))
        return waits

    def _commit(self, ev, rd, wr):
        for r in rd:
            r.rs[id(ev.ds)] = ev
        for r in wr:
            r.w = ev
            r.rs = {}

    def op(self, eng, fn, rd=(), wr=()):
        waits = self._deps(eng, rd, wr)
        ds = self.esem[eng]
        ds.cnt += 1
        ev = Ev(ds, ds.cnt, eng)
        self.q[eng].append((fn, waits, ev, 1))
        self._commit(ev, rd, wr)
        return ev

    def dma(self, q, out, in_, ds, rd=(), wr=(), bulk=False, **kw):
        waits = self._deps(q, rd, wr)
        ds.cnt += 1
        ev = Ev(ds, None if bulk else 16 * ds.cnt, "dma")
        self.q[q].append((lambda e: e.dma_start(out=out, in_=in_, **kw), waits, ev, 16))
        self._commit(ev, rd, wr)
        return ev

    def replay(self):
        nc = self.nc
        finals = [(d.sem, 16 * d.cnt) for d in self.dsems if d.cnt]
        with nc.Block() as block:
            for eng, deco in (("sp", block.sync), ("act", block.scalar), ("dve", block.vector),
                              ("pool", block.gpsimd), ("pe", block.tensor)):
                q = self.q[eng]

                def body(e, q=q, eng=eng):
                    mw = {}
                    for fn, waits, ev, inc in q:
                        for wev, raw in waits:
                            key = id(wev.ds)
                            v = wev.value()
                            if mw.get(key, 0) >= v:
                                continue
                            e.wait_ge(wev.ds.sem, v)
                            mw[key] = v
                        ins = fn(e)
                        ins.then_inc(ev.ds.sem, inc)
                    if eng == "sp":
                        for sem, v in finals:
                            e.wait_ge(sem, v)
                        for en2 in ("act", "dve", "pool", "pe"):
                            d = self.esem[en2]
                            if d.cnt:
                                e.wait_ge(d.sem, d.cnt)

                deco(body)
        nc.all_engine_barrier()
        nc.clear_and_free_semaphores(self.sems)
        nc.all_engine_barrier()


def col_tiles(n, step=512):
    return [(a, min(a + step, n)) for a in range(0, n, step)]


class Builder:
    def __init__(self, n_p, t_p, n_s, t_s, debug=False):
        self.cfg = (n_p, t_p, n_s, t_s)
        self.debug = debug
        self.smax = 2064
        nc = self.nc = bass.Bass("TRN2", target_bir_lowering=False)
        self.seqs = []
        base = 0
        for i in range(n_p):
            self.seqs.append(("p", i, t_p, base))
            base += NMETA + t_p
        for i in range(n_s):
            self.seqs.append(("s", i, t_s, base))
            base += NMETA + t_s
        self.NT = NT = base
        di = lambda name, shape, dt=F32: nc.dram_tensor(name, shape, dt, kind="ExternalInput").ap()
        do = lambda name, shape, dt=F32: nc.dram_tensor(name, shape, dt, kind="ExternalOutput").ap()
        sk = "ExternalOutput" if debug else "Internal"
        dsr = lambda name, shape, dt: nc.dram_tensor(name, shape, dt, kind=sk).ap()
        self.xp = di("xp", [n_p, t_p, D])
        self.xs = di("xs", [n_s, t_s, D])
        self.meta = di("meta", [NMETA, D])
        self.ident = di("ident", [128, 128])
        self.vec = di("vec", [DEPTH, 128, NV])
        self.fbias = di("fbias", [DEPTH, 16, 15, 31])
        self.w = {}
        for nm, shp in (("f1g", [D, DFF]), ("f1u", [D, DFF]), ("f1d", [DFF, D]), ("win", [D, INW]),
                        ("wna", [D, D]), ("wlru", [D, D]), ("wout", [D, D]),
                        ("f2g", [D, DFF]), ("f2u", [D, DFF]), ("f2d", [DFF, D]),
                        ("lwa", [2, 16, 64, 64]), ("lwx", [2, 16, 64, 64])):
            self.w[nm] = di(nm, [DEPTH] + shp)
        self.yp = do("yp", [n_p, t_p, D])
        self.ys = do("ys", [n_s, t_s, D])
        self.H = dsr("H", [D, NT], F32)
        self.QT = dsr("QT", [D, NT], BF16)
        self.KT = dsr("KT", [D, NT], BF16)
        self.V = dsr("V", [NT, D], BF16)
        self.XR = dsr("XR", [D, NT], F32)
        self.GY = dsr("GY", [D, NT], BF16)
        self.SGN = dsr("SGN", [D, NT], BF16)
        self.SGL = dsr("SGL", [D, NT], BF16)
        self.NAT = dsr("NAT", [D, NT], BF16)
        self.LT = dsr("LT", [D, NT], BF16)
        self.HF = dsr("HF", [D, NT], F32)
        self.XC = nc.dram_tensor("XC", [D, NT], F32, kind="Internal").ap()
        self.MSK = nc.dram_tensor("MSK", [DEPTH, 2, 128, 16, 1024], BF16, kind="Internal").ap()
        self.dbg = {}
        if debug:
            for nm in ("H0", "H1", "H2", "H3"):
                self.dbg[nm] = do("dbg_" + nm, [D, NT])

    @staticmethod
    def fm(ap):
        return ap.rearrange("(c p) t -> p c t", p=128)

    def tok_tiles(self):
        return col_tiles(self.NT, TT)

    def norm(self, S, x, R_x, n, gcol, vecs, R_vec, ones, sqs, R_sq, k, ssq, R_ssq, sd, R_sd, rstd, R_rstd, xn, R_xn):
        for c in range(8):
            sq, Rq = sqs[(k * 8 + c) % 2], R_sq[(k * 8 + c) % 2]
            S.op("act", lambda e, c=c, sq=sq: e.activation(out=sq[:, :n], in_=x[:, c, :n], func=AF.Square),
                 rd=[R_x], wr=[Rq])
            S.op("pe", lambda e, c=c, sq=sq: e.matmul(ssq[:, :n], ones[:, :], sq[:, :n], start=(c == 0), stop=(c == 7)),
                 rd=[Rq, self.R_ones], wr=[R_ssq])
        S.op("act", lambda e: e.activation(out=sd[:, :n], in_=ssq[:, :n], func=AF.Sqrt, scale=1.0 / D, bias=self.eps_ap),
             rd=[R_ssq, self.R_ones], wr=[R_sd])
        S.op("dve", lambda e: e.reciprocal(out=rstd[:, :n], in_=sd[:, :n]), rd=[R_sd], wr=[R_rstd])
        for c in range(8):
            S.op("dve", lambda e, c=c: e.scalar_tensor_tensor(out=xn[:, c, :n], in0=x[:, c, :n],
                                                              scalar=vecs[:, gcol + c:gcol + c + 1], in1=rstd[:, :n],
                                                              op0=ALU.mult, op1=ALU.mult),
                 rd=[R_x, R_rstd, R_vec], wr=[R_xn])

    def consts(self, S, l):
        ones = S.sb("ones", [128, 128], BF16)
        vecs = S.sb("vecs", [128, NV], F32)
        eps = S.sb("eps", [128, 1], F32)
        R_ones, R_vec = Res(), Res()
        S.op("pool", lambda e: e.memset(ones[:, :], 1.0), wr=[R_ones])
        S.op("pool", lambda e: e.memset(eps[:, :], EPS), wr=[R_ones])
        S.dma("sp", vecs[:, :], self.vec[l], S.dsem(), wr=[R_vec])
        self.eps_ap = eps[:, 0:1]
        self.R_ones = R_ones
        return ones, R_ones, vecs, R_vec

    def load_w(self, S, dst, src, ds, R, nk, ncols, cstep=1024):
        srcv = src.rearrange("(c p) f -> p c f", p=128)
        for c in range(nk):
            for a, b in col_tiles(ncols, cstep):
                S.dma("pool", dst[:, c, a:b], srcv[:, c, a:b], ds, wr=[], bulk=True)
        R.w = Ev(ds, None, "dma")

    def build_masks(self, S, q):
        rf32 = S.sb("rf32", [128, 16, 15, 64], F32)
        rfull4 = S.sb("rfull", [128, 16, 16, 64], BF16)
        rint4 = S.sb("rint", [128, 16, 16, 64], BF16)
        rfull = rfull4.rearrange("p h j q -> p h (j q)")
        rint = rint4.rearrange("p h j q -> p h (j q)")
        R_rf32, R_rfull, R_rint = Res(), Res(), Res()
        d_st = [S.dsem(), S.dsem()]
        qs = np.arange(GW)
        cs = np.clip(qs - 8, 0, GW - 16)
        rfv = rf32.rearrange("p h j q -> p (h j) q")
        for l in range(DEPTH):
            dmk = S.dsem()
            S.op("pool", lambda e: e.memset(rf32[:, :, :, :], -30000.0), wr=[R_rf32])
            S.op("pool", lambda e: e.memset(rfull4[:, :, :, :], 0.0), wr=[R_rfull])
            first = True
            for kp in range(2):
                for kcol in range(GW):
                    valid = np.nonzero((cs <= kcol) & (kcol < cs + 16))[0]
                    qlo, qhi = int(valid[0]), int(valid[-1])
                    assert len(valid) == qhi - qlo + 1
                    nq = qhi - qlo + 1
                    b0 = 15 - kcol + qlo
                    assert 0 <= b0 and b0 + nq <= 31
                    pp = kp * 64 + kcol
                    S.dma(q, rfv[pp:pp + 1, :, qlo:qhi + 1],
                          self.fbias[l].rearrange("h a b -> (h a) b")[:, b0:b0 + nq].rearrange("(o r) b -> o r b", o=1),
                          dmk, rd=[R_rf32] if first else [], bulk=True)
                    first = False
            R_rf32.w = Ev(dmk, None, "dma")
            S.op("act", lambda e: e.activation(out=rfull4[0:64, :, 0:15, :], in_=rf32[0:64, :, :, :], func=AF.Exp),
                 rd=[R_rf32, R_rfull], wr=[R_rfull])
            S.op("act", lambda e: e.activation(out=rfull4[64:128, :, 1:16, :], in_=rf32[64:128, :, :, :], func=AF.Exp),
                 rd=[R_rf32, R_rfull], wr=[R_rfull])
            S.op("pool", lambda e: e.tensor_copy(out=rint[:, :, :], in_=rfull[:, :, :]), rd=[R_rfull], wr=[R_rint])
            S.op("pool", lambda e: e.memset(rint[0:64, :, 0:4 * 64], 0.0), wr=[R_rint])
            S.op("pool", lambda e: e.memset(rint[0:64, :, 12 * 64:16 * 64], 0.0), wr=[R_rint])
            S.op("pool", lambda e: e.memset(rint[64:128, :, 0:5 * 64], 0.0), wr=[R_rint])
            S.op("pool", lambda e: e.memset(rint[64:128, :, 13 * 64:16 * 64], 0.0), wr=[R_rint])
            S.dma("pool", self.MSK[l, 0], rfull[:, :, :], d_st[0], rd=[R_rfull])
            S.dma("pool", self.MSK[l, 1], rint[:, :, :], d_st[1], rd=[R_rint])

    def phase_init(self):
        nc = self.nc
        with contextlib.ExitStack() as es:
            S = Sched(nc, es, "in")
            self.build_masks(S, "act")
            idt = S.sb("idt", [128, 128], F32)
            R_id = Res()
            S.dma("sp", idt[:, :], self.ident, S.dsem(), wr=[R_id])
            xin = [S.sb(f"xin{i}", [128, 4, D], F32) for i in range(2)]
            xfm = [S.sb(f"xfm{i}", [128, 8, TT], F32) for i in range(2)]
            R_xin = [Res(), Res()]
            R_xfm = [Res(), Res()]
            d_xin = [S.dsem(), S.dsem()]
            d_xfm = [S.dsem(), S.dsem()]
            pst = [S.ps(f"pt{i}", [128, TT]) for i in range(4)]
            R_ps = [Res() for _ in range(4)]
            jobs = []
            for kind, idx, T, base in self.seqs:
                jobs.append((self.meta, base, NMETA))
                src = self.xp if kind == "p" else self.xs
                for a, b in col_tiles(T, TT):
                    jobs.append((src[idx, a:b, :], base + NMETA + a, b - a))
            Hv = self.fm(self.H)
            pc = 0
            for k, (src, t0, n) in enumerate(jobs):
                b = k % 2
                ng = (n + 127) // 128
                if n >= 128:
                    S.dma("sp", xin[b][:, :ng, :], src.rearrange("(g p) d -> p g d", p=128), d_xin[b], wr=[R_xin[b]])
                else:
                    S.dma("sp", xin[b][:n, 0, :], src, d_xin[b], wr=[R_xin[b]])
                for c in range(8):
                    pi = pc % 4
                    pc += 1
                    ps = pst[pi]

                    def tr(e, c=c, ps=ps, b=b, n=n, ng=ng):
                        ins = None
                        for g in range(ng):
                            m = min(128, n - g * 128)
                            ins = e.transpose(ps[:, g * 128:g * 128 + m], xin[b][:m, g, c * 128:(c + 1) * 128], idt[:m, :m])
                        return ins
                    S.op("pe", tr, rd=[R_xin[b], R_id], wr=[R_ps[pi]])
                    eng = "dve"
                    if eng == "dve":
                        S.op("dve", lambda e, c=c, ps=ps, b=b, n=n: e.tensor_copy(out=xfm[b][:, c, :n], in_=ps[:, :n]),
                             rd=[R_ps[pi]], wr=[R_xfm[b]])
                    else:
                        S.op("act", lambda e, c=c, ps=ps, b=b, n=n: e.activation(out=xfm[b][:, c, :n], in_=ps[:, :n], func=AF.Copy),
                             rd=[R_ps[pi]], wr=[R_xfm[b]])
                S.dma("sp", Hv[:, :, t0:t0 + n], xfm[b][:, :, :n], d_xfm[b], rd=[R_xfm[b]])
            S.replay()

    def phase_ffn(self, l, which):
        nc = self.nc
        pre = "f1" if which == 1 else "f2"
        gcol = V_G1 if which == 1 else V_G2
        with contextlib.ExitStack() as es:
            S = Sched(nc, es, f"{pre}{l}")
            ones, R_ones, vecs, R_vec = self.consts(S, l)
            wg = S.sb("wg", [128, 8, DFF], BF16)
            wu = S.sb("wu", [128, 8, DFF], BF16)
            wd = S.sb("wd", [128, NFF, D], BF16)
            R_wd = Res()
            dd = S.dsem()
            blocks = col_tiles(DFF, 1024)
            R_wgu = [Res() for _ in blocks]
            dgu = [S.dsem() for _ in blocks]
            gv = self.w[pre + "g"][l].rearrange("(c p) f -> p c f", p=128)
            uv = self.w[pre + "u"][l].rearrange("(c p) f -> p c f", p=128)
            for bi, (a, b) in enumerate(blocks):
                for c in range(8):
                    S.dma("pool", wg[:, c, a:b], gv[:, c, a:b], dgu[bi], bulk=True)
                    S.dma("pool", wu[:, c, a:b], uv[:, c, a:b], dgu[bi], bulk=True)
                R_wgu[bi].w = Ev(dgu[bi], None, "dma")
            self.load_w(S, wd, self.w[pre + "d"][l], dd, R_wd, NFF, D)
            x = [S.sb(f"x{i}", [128, 8, TT], F32) for i in range(2)]
            R_x = [Res(), Res()]
            d_x = [S.dsem(), S.dsem()]
            d_xs = [S.dsem(), S.dsem()]
            xn = S.sb("xn", [128, 8, TT], BF16)
            R_xn = Res()
            hT = S.sb("hT", [128, NFF, TT], BF16)
            R_h = [Res() for _ in range(NFF)]
            sqs = [S.sb(f"sq{i}", [128, TT], BF16) for i in range(2)]
            R_sq = [Res(), Res()]
            sg = [S.sb(f"sg{i}", [128, TT], F32) for i in range(2)]
            R_sg = [Res(), Res()]
            sd = S.sb("sd", [128, TT], F32)
            rstd = S.sb("rstd", [128, TT], F32)
            R_sd, R_rstd = Res(), Res()
            ssq = S.ps("ssq", [128, TT])
            R_ssq = Res()
            gps = [S.ps(f"g{i}", [128, TT]) for i in range(2)]
            ups = [S.ps(f"u{i}", [128, TT]) for i in range(2)]
            ops = [S.ps(f"o{i}", [128, TT]) for i in range(2)]
            R_g, R_u, R_o = [Res(), Res()], [Res(), Res()], [Res(), Res()]
            Hv = self.fm(self.H)
            tiles = self.tok_tiles()

            def stage_a(k):
                t0, t1 = tiles[k]
                n = t1 - t0
                b = k % 2
                S.dma("sp", x[b][:, :, :n], Hv[:, :, t0:t1], d_x[b], wr=[R_x[b]])
                self.norm(S, x[b], R_x[b], n, gcol, vecs, R_vec, ones, sqs, R_sq, k, ssq, R_ssq, sd, R_sd,
                          rstd, R_rstd, xn, R_xn)

            stage_a(0)
            for k, (t0, t1) in enumerate(tiles):
                n = t1 - t0
                b = k % 2
                for f in range(NFF):
                    pb = f % 2

                    def mmg(e, f=f, pb=pb, n=n):
                        ins = None
                        for c in range(8):
                            ins = e.matmul(gps[pb][:, :n], wg[:, c, f * 128:(f + 1) * 128], xn[:, c, :n],
                                           start=(c == 0), stop=(c == 7))
                        return ins

                    def mmu(e, f=f, pb=pb, n=n):
                        ins = None
                        for c in range(8):
                            ins = e.matmul(ups[pb][:, :n], wu[:, c, f * 128:(f + 1) * 128], xn[:, c, :n],
                                           start=(c == 0), stop=(c == 7))
                        return ins
                    S.op("pe", mmg, rd=[R_wgu[f * 128 // 1024], R_xn], wr=[R_g[pb]])
                    S.op("pe", mmu, rd=[R_wgu[f * 128 // 1024], R_xn], wr=[R_u[pb]])
                    S.op("act", lambda e, pb=pb, n=n: e.activation(out=sg[pb][:, :n], in_=gps[pb][:, :n], func=AF.Silu),
                         rd=[R_g[pb]], wr=[R_sg[pb]])
                    S.op("dve", lambda e, f=f, pb=pb, n=n: e.tensor_tensor(out=hT[:, f, :n], in0=ups[pb][:, :n],
                                                                          in1=sg[pb][:, :n], op=ALU.mult),
                         rd=[R_u[pb], R_sg[pb]], wr=[R_h[f]])
                if k + 1 < len(tiles):
                    stage_a(k + 1)
                for c in range(8):
                    pb = c % 2

                    def mmd(e, c=c, pb=pb, n=n):
                        ins = None
                        for f in range(NFF):
                            ins = e.matmul(ops[pb][:, :n], wd[:, f, c * 128:(c + 1) * 128], hT[:, f, :n],
                                           start=(f == 0), stop=(f == NFF - 1))
                        return ins
                    S.op("pe", mmd, rd=[R_wd] + R_h, wr=[R_o[pb]])
                    S.op("dve", lambda e, c=c, pb=pb, n=n, b=b: e.scalar_tensor_tensor(
                        out=x[b][:, c, :n], in0=ops[pb][:, :n], scalar=0.5, in1=x[b][:, c, :n],
                        op0=ALU.mult, op1=ALU.add), rd=[R_o[pb], R_x[b]], wr=[R_x[b]])
                S.dma("pool", Hv[:, :, t0:t1], x[b][:, :, :n], d_xs[b], rd=[R_x[b]])
            S.replay()

    def phase_mixin(self, l):
        nc = self.nc
        with contextlib.ExitStack() as es:
            S = Sched(nc, es, f"mi{l}")
            ones, R_ones, vecs, R_vec = self.consts(S, l)
            win = S.sb("win", [128, 8, INW], BF16)
            R_wb = [Res() for _ in range(7)]
            winv = self.w["win"][l].rearrange("(c p) f -> p c f", p=128)
            for bi in range(7):
                dwb = S.dsem()
                for c in range(8):
                    S.dma("pool", win[:, c, bi * 1024:(bi + 1) * 1024], winv[:, c, bi * 1024:(bi + 1) * 1024], dwb, bulk=True)
                R_wb[bi].w = Ev(dwb, None, "dma")
            x = S.sb("x", [128, 8, TT], F32)
            R_x, d_x = Res(), S.dsem()
            xn = [S.sb(f"xn{i}", [128, 8, TT], BF16) for i in range(2)]
            R_xn = [Res(), Res()]
            sqs = [S.sb(f"sq{i}", [128, TT], BF16) for i in range(2)]
            R_sq = [Res(), Res()]
            sd = S.sb("sd", [128, TT], F32)
            rstd = S.sb("rstd", [128, TT], F32)
            R_sd, R_rstd = Res(), Res()
            stg = [S.sb(f"stg{i}", [128, 8, TT], BF16) for i in range(3)]
            R_stg = [Res() for _ in range(3)]
            d_stg = [S.dsem() for _ in range(3)]
            vst = S.sb("vst", [128, 4, D], BF16)
            R_vst, d_vst = Res(), S.dsem()
            xrs = S.sb("xrs", [128, 8, TT], F32)
            R_xrs, d_xrs = Res(), S.dsem()
            ssq = S.ps("ssq", [128, TT])
            R_ssq = Res()
            zps = [S.ps(f"z{i}", [128, TT]) for i in range(6)]
            R_z = [Res() for _ in range(6)]
            Hv = self.fm(self.H)
            tiles = self.tok_tiles()

            def load_x(k):
                t0, t1 = tiles[k]
                S.dma("sp", x[:, :, :t1 - t0], Hv[:, :, t0:t1], d_x, wr=[R_x])

            def stage_a(k):
                t0, t1 = tiles[k]
                n = t1 - t0
                self.norm(S, x, R_x, n, V_GM, vecs, R_vec, ones, sqs, R_sq, k, ssq, R_ssq, sd, R_sd,
                          rstd, R_rstd, xn[k % 2], R_xn[k % 2])

            load_x(0)
            stage_a(0)
            zc = 0
            sc = 0
            kinds = [("q", 0, self.QT), ("k", 1024, self.KT), ("xr", 3072, self.XR), ("yr", 4096, self.GY),
                     ("gn", 5120, self.SGN), ("gl", 6144, self.SGL)]
            for k, (t0, t1) in enumerate(tiles):
                n = t1 - t0
                xb, Rb = xn[k % 2], R_xn[k % 2]
                if k + 1 < len(tiles):
                    load_x(k + 1)
                for name, off, dst in kinds:
                    if name == "xr":
                        buf, Rbuf, dbuf = xrs, R_xrs, d_xrs
                    else:
                        si = sc % 3
                        sc += 1
                        buf, Rbuf, dbuf = stg[si], R_stg[si], d_stg[si]
                    for c in range(8):
                        zi = zc % 6
                        zc += 1
                        zp = zps[zi]

                        def mm(e, c=c, zp=zp, n=n, off=off, xb=xb):
                            ins = None
                            for kk in range(8):
                                ins = e.matmul(zp[:, :n], win[:, kk, off + c * 128:off + (c + 1) * 128], xb[:, kk, :n],
                                               start=(kk == 0), stop=(kk == 7))
                            return ins
                        S.op("pe", mm, rd=[R_wb[off // 1024], Rb], wr=[R_z[zi]])
                        if name == "q":
                            S.op("dve", lambda e, c=c, zp=zp, n=n, buf=buf: e.tensor_scalar(
                                out=buf[:, c, :n], in0=zp[:, :n], scalar1=0.125, scalar2=None, op0=ALU.mult),
                                rd=[R_z[zi]], wr=[Rbuf])
                        elif name in ("k", "xr"):
                            S.op("dve", lambda e, c=c, zp=zp, n=n, buf=buf: e.tensor_copy(out=buf[:, c, :n], in_=zp[:, :n]),
                                 rd=[R_z[zi]], wr=[Rbuf])
                        else:
                            fn = AF.Gelu if name == "yr" else AF.Sigmoid
                            S.op("act", lambda e, c=c, zp=zp, n=n, buf=buf, fn=fn: e.activation(
                                out=buf[:, c, :n], in_=zp[:, :n], func=fn), rd=[R_z[zi]], wr=[Rbuf])
                    S.dma("pool", self.fm(dst)[:, :, t0:t1], buf[:, :, :n], dbuf, rd=[Rbuf])
                    if name == "k" and k + 1 < len(tiles):
                        stage_a(k + 1)
                ng = (n + 127) // 128
                for g in range(ng):
                    m = min(128, n - g * 128)
                    for hf in range(2):
                        zi = zc % 6
                        zc += 1
                        zp = zps[zi]

                        def mmv(e, g=g, m=m, hf=hf, zp=zp, xb=xb):
                            ins = None
                            for kk in range(8):
                                ins = e.matmul(zp[:m, :], xb[:, kk, g * 128:g * 128 + m],
                                               win[:, kk, 2048 + hf * 512:2048 + (hf + 1) * 512],
                                               start=(kk == 0), stop=(kk == 7))
                            return ins
                        S.op("pe", mmv, rd=[R_wb[2], Rb], wr=[R_z[zi]])
                        S.op("dve", lambda e, g=g, m=m, hf=hf, zp=zp: e.tensor_copy(
                            out=vst[:m, g, hf * 512:(hf + 1) * 512], in_=zp[:m, :]), rd=[R_z[zi]], wr=[R_vst])
                if n % 128 == 0:
                    S.dma("pool", self.V[t0:t1, :].rearrange("(g p) d -> p g d", p=128), vst[:, :ng, :], d_vst, rd=[R_vst])
                else:
                    S.dma("pool", self.V[t0:t1, :], vst[:n, 0, :], d_vst, rd=[R_vst])
            S.replay()

    def phase_attn(self, l):
        nc = self.nc
        with contextlib.ExitStack() as es:
            S = Sched(nc, es, f"at{l}")
            ones = S.sb("ones", [128, 128], BF16)
            R_ones = Res()
            S.op("pool", lambda e: e.memset(ones[:, :], 1.0), wr=[R_ones])
            rfull = S.sb("rfull", [128, 16, 1024], BF16)
            rint = S.sb("rint", [128, 16, 1024], BF16)
            R_rfull, R_rint = Res(), Res()
            S.dma("sp", rfull[:, :, :], self.MSK[l, 0], S.dsem(), wr=[R_rfull])
            S.dma("sp", rint[:, :, :], self.MSK[l, 1], S.dsem(), wr=[R_rint])

            NS = 8
            kt = [S.sb(f"kt{i}", [128, 8, 128], BF16) for i in range(NS)]
            vt = [S.sb(f"vt{i}", [128, D], BF16) for i in range(NS)]
            R_kv = [Res() for _ in range(NS)]
            d_kv = [S.dsem() for _ in range(NS)]
            ktm = S.sb("ktm", [128, 8, NMETA], BF16)
            vm = S.sb("vm", [NMETA, D], BF16)
            R_m, d_m = Res(), S.dsem()
            qt = [S.sb(f"qt{i}", [128, 8, 256], BF16) for i in range(2)]
            R_q = [Res(), Res()]
            d_q = [S.dsem(), S.dsem()]
            for i in range(2):
                S.op("pool", lambda e, i=i: e.memset(qt[i][:, :, :], 0.0), wr=[R_q[i]])
            ot = [S.sb(f"ot{i}", [128, 8, 128], BF16) for i in range(2)]
            R_ot = [Res(), Res()]
            d_ot = [S.dsem(), S.dsem()]
            eb = [S.sb(f"eb{i}", [128, 6, 256], BF16) for i in range(3)]
            R_eb = [Res() for _ in range(3)]
            rc = [S.sb(f"rc{i}", [128, 256], F32) for i in range(2)]
            R_rc = [Res(), Res()]
            stp = [S.ps(f"st{i}", [128, 6, 256]) for i in range(2)]
            R_st = [Res(), Res()]
            odp = [S.ps(f"od{i}", [128, 512]) for i in range(2)]
            R_od = [Res(), Res()]
            QTv, KTv, NATv = self.fm(self.QT), self.fm(self.KT), self.fm(self.NAT)
            cnt = dict(q=0, h=0)

            jobs = []
            load_meta_cur = [None]

            def do_queries(qtok0, nq, kcs, p, slot_of, table, R_table, pre=None):
                qi = cnt["q"] % 2
                cnt["q"] += 1
                nk = len(kcs)
                ma = mb = 0
                if nk:
                    d_hi, d_lo = kcs[0] - p, kcs[-1] - p
                    ma, mb = (7 - 2 * d_hi) * 64, (9 - 2 * d_lo) * 64
                    assert mb - ma == nk * 128 and ma >= 0 and mb <= 1024
                for hp in range(8):
                    hi = cnt["h"]
                    cnt["h"] += 1
                    st, Rst = stp[hi % 2], R_st[hi % 2]
                    e_, Re = eb[hi % 3], R_eb[hi % 3]
                    od, Rod = odp[hi % 2], R_od[hi % 2]
                    rcb, Rrc = rc[hi % 2], R_rc[hi % 2]

                    def stage_a(hp=hp, st=st, Rst=Rst, e_=e_, Re=Re):
                        if hp == 0:
                            if pre is not None:
                                pre()
                            S.dma("sp", qt[qi][0:64, :, 0:nq], QTv[0:64, :, qtok0:qtok0 + nq], d_q[qi], wr=[R_q[qi]])
                            S.dma("sp", qt[qi][64:128, :, 128:128 + nq], QTv[64:128, :, qtok0:qtok0 + nq], d_q[qi], wr=[R_q[qi]])

                        def qk(e):
                            for i, kc in enumerate(kcs):
                                e.matmul(st[:, i, :], kt[slot_of(kc)][:, hp, :], qt[qi][:, hp, :], start=True, stop=True)
                            return e.matmul(st[:NMETA, nk, :], ktm[:, hp, :], qt[qi][:, hp, :], start=True, stop=True)
                        S.op("pe", qk, rd=[R_q[qi], R_m] + [R_kv[slot_of(kc)] for kc in kcs], wr=[Rst])
                        if nk:
                            S.op("act", lambda e: e.activation(out=e_[:, :nk, :], in_=st[:, :nk, :], func=AF.Exp),
                                 rd=[Rst], wr=[Re])
                        S.op("act", lambda e: e.activation(out=e_[:NMETA, nk, :], in_=st[:NMETA, nk, :], func=AF.Exp),
                             rd=[Rst], wr=[Re])
                        if nk:
                            ev = e_[:, :nk, :].rearrange("p i (h q) -> p i h q", h=2)
                            tv = table[:, 2 * hp:2 * hp + 2, ma:mb].rearrange("p h (i q) -> p i h q", q=128)
                            S.op("dve", lambda e: e.tensor_tensor(out=ev, in0=ev, in1=tv, op=ALU.mult),
                                 rd=[Re, R_table], wr=[Re])

                    def stage_b(hp=hp, e_=e_, Re=Re, od=od, Rod=Rod, rcb=rcb, Rrc=Rrc):
                        def pv(e):
                            for i, kc in enumerate(kcs):
                                e.matmul(od[:, 0:256], vt[slot_of(kc)][:, hp * 128:(hp + 1) * 128], e_[:, i, :],
                                         start=(i == 0), stop=False)
                            e.matmul(od[:, 0:256], vm[:, hp * 128:(hp + 1) * 128], e_[:NMETA, nk, :],
                                     start=(nk == 0), stop=True)
                            for i, kc in enumerate(kcs):
                                e.matmul(od[:, 256:512], ones[:, :], e_[:, i, :], start=(i == 0), stop=False)
                            return e.matmul(od[:, 256:512], ones[:NMETA, :], e_[:NMETA, nk, :], start=(nk == 0), stop=True)
                        S.op("pe", pv, rd=[Re, R_m, R_ones] + [R_kv[slot_of(kc)] for kc in kcs], wr=[Rod])
                        S.op("act", lambda e: e.activation(out=rcb[:, :], in_=od[:, 256:512], func=AF.Ln),
                             rd=[Rod], wr=[Rrc])
                        S.op("act", lambda e: e.activation(out=rcb[:, :], in_=rcb[:, :], func=AF.Exp, scale=-1.0),
                             rd=[Rrc], wr=[Rrc])
                        S.op("dve", lambda e: e.tensor_tensor(out=ot[qi][0:64, hp, :nq], in0=od[0:64, 0:nq], in1=rcb[0:64, 0:nq],
                                                              op=ALU.mult),
                             rd=[Rod, Rrc], wr=[R_ot[qi]])
                        S.op("dve", lambda e: e.tensor_tensor(out=ot[qi][64:128, hp, :nq], in0=od[64:128, 128:128 + nq],
                                                              in1=rcb[64:128, 128:128 + nq], op=ALU.mult),
                             rd=[Rod, Rrc], wr=[R_ot[qi]])
                        if hp == 7:
                            S.dma("pool", NATv[:, :, qtok0:qtok0 + nq], ot[qi][:, :, :nq], d_ot[qi], rd=[R_ot[qi]])
                    jobs.append((stage_a, stage_b, pre is load_meta_cur[0] and hp == 0))

            gslot = [0]
            for kind, idx, T, base in self.seqs:
                P = T // 128
                assert P >= 4
                s0 = gslot[0]
                slot_of = lambda kc, s0=s0: (s0 + kc) % NS
                gslot[0] += P

                def load_meta(base=base):
                    S.dma("sp", ktm[:, :, :], KTv[:, :, base:base + NMETA], d_m, wr=[R_m])
                    S.dma("sp", vm[:, :], self.V[base:base + NMETA, :], d_m, wr=[R_m])

                def load_kv(kc, base=base, slot_of=slot_of):
                    sl = slot_of(kc)
                    tk = base + NMETA + kc * 128
                    S.dma("sp", kt[sl][:, :, :], KTv[:, :, tk:tk + 128], d_kv[sl], wr=[R_kv[sl]])
                    S.dma("sp", vt[sl][:, :], self.V[tk:tk + 128, :], d_kv[sl], wr=[R_kv[sl]])
                load_meta_cur[0] = load_meta
                do_queries(base, NMETA, [], 0, slot_of, rfull, R_rfull, pre=load_meta)
                loaded = 0
                for p in range(P):
                    if p < 2:
                        kcs = [3, 2, 1, 0]
                        table, Rt = rfull, R_rfull
                    elif p >= P - 2:
                        kcs = [P - 1, P - 2, P - 3, P - 4]
                        table, Rt = rfull, R_rfull
                    else:
                        kcs = [p + 2, p + 1, p, p - 1, p - 2]
                        table, Rt = rint, R_rint
                    need = min(P, max(kcs) + 3)
                    lst = list(range(loaded, need))
                    loaded = max(loaded, need)
                    pre = (lambda lst=lst, load_kv=load_kv: [load_kv(kc) for kc in lst]) if lst else None
                    do_queries(base + NMETA + p * 128, 128, kcs, p, slot_of, table, Rt, pre=pre)
            pend = None
            for i in range(len(jobs)):
                if jobs[i][2] and pend is not None:
                    pend()
                    pend = None
                jobs[i][0]()
                if pend is not None:
                    pend()
                pend = jobs[i][1]
            pend()
            S.replay()

    def phase_lru(self, l):
        nc = self.nc
        with contextlib.ExitStack() as es:
            S = Sched(nc, es, f"lr{l}")
            vecs = S.sb("vecs", [128, NV], F32)
            R_vec = Res()
            S.dma("sp", vecs[:, :], self.vec[l], S.dsem(), wr=[R_vec])
            wst = S.sb("wst", [128, 4, 8, 128], F32)
            R_wst = Res()
            dws = S.dsem()
            S.op("pool", lambda e: e.memset(wst[:, :, :, :], 0.0), wr=[R_wst])
            first = True
            for d in range(2):
                for kd, nm in enumerate(("lwa", "lwx")):
                    src = self.w[nm][l, d].rearrange("(c h) i j -> h i c j", h=2)
                    for half in range(2):
                        S.dma("sp", wst[half * 64:(half + 1) * 64, d * 2 + kd, :, half * 64:(half + 1) * 64], src[half],
                              dws, rd=[R_wst] if first else [], bulk=True)
                        first = False
            R_wst.w = Ev(dws, None, "dma")
            one = S.sb("one", [128, 1], F32)
            S.op("pool", lambda e: e.memset(one[:, :], 1.0), wr=[R_vec])
            self.one_ap = one[:, 0:1]
            nsp = S.sb("nsp", [128, 16], F32)
            R_nsp = Res()
            S.op("act", lambda e: e.activation(out=nsp[:, :], in_=vecs[:, V_LAM:V_LAM + 16], func=AF.Exp, scale=-1.0),
                 rd=[R_vec], wr=[R_nsp])
            S.op("act", lambda e: e.activation(out=nsp[:, :], in_=nsp[:, :], func=AF.Ln, bias=self.one_ap),
                 rd=[R_nsp, R_vec], wr=[R_nsp])
            S.op("dve", lambda e: e.tensor_scalar(out=nsp[:, :], in0=nsp[:, :], scalar1=-8.0, scalar2=None, op0=ALU.mult),
                 rd=[R_nsp], wr=[R_nsp])

            hb2 = S.sb("hb2", [128, 32], F32)
            nsph = S.sb("nsph", [128, 16], F32)
            quart = S.sb("quart", [128, 1], F32)
            R_hb2 = Res()
            S.op("dve", lambda e: e.tensor_scalar(out=hb2[:, :], in0=vecs[:, V_BA:V_BA + 32], scalar1=0.5, scalar2=None,
                                                  op0=ALU.mult), rd=[R_vec], wr=[R_hb2])
            S.op("dve", lambda e: e.tensor_scalar(out=nsph[:, :], in0=nsp[:, :], scalar1=0.5, scalar2=None, op0=ALU.mult),
                 rd=[R_nsp], wr=[R_hb2])
            S.op("pool", lambda e: e.memset(quart[:, :], 0.25), wr=[R_hb2])
            SMAX = self.smax
            NB = 2
            NX = 3
            W = SMAX + 4
            mk = lambda nm, dt, n=NB: [S.sb(f"{nm}{i}", [128, W], dt) for i in range(n)]
            xrb, rb, ib, tb, hb, hfin = (mk(n_, F32) for n_ in ("xrb", "rb", "ib", "tb", "hb", "hfin"))
            xc = mk("xc", F32, NX)
            gy, ob = (mk(n_, BF16) for n_ in ("gy", "ob"))
            mkr = lambda n=NB: [Res() for _ in range(n)]
            R_xrb, R_rb, R_ib, R_tb, R_hb, R_hfin, R_gy, R_ob = (mkr() for _ in range(8))
            R_xc = mkr(NX)
            d_xrb, d_hb, d_hfin, d_gy, d_ob = ([S.dsem() for _ in range(NB)] for _ in range(5))
            carry = [S.sb(f"carry{i}", [128, 1], F32) for i in range(2)]
            R_carry = [Res(), Res()]
            rps = [S.ps(f"r{i}", [128, 512]) for i in range(2)]
            ips = [S.ps(f"i{i}", [128, 512]) for i in range(2)]
            R_rp, R_ip = [Res(), Res()], [Res(), Res()]
            XRv, HFv, GYv, LTv = self.fm(self.XR), self.fm(self.HF), self.fm(self.GY), self.fm(self.LT)
            pc = [0]
            xci = [0]
            jobs = []

            XCv = self.fm(self.XC)
            d_xcs = [S.dsem() for _ in range(NX)]
            d_xcl = [S.dsem() for _ in range(NX)]

            def make_job(ji, base, L, c, s0, s1, d, has_carry, reuse_xc, hfres, xcres=None, xmode="compute"):
                k = ji % NB
                Sg = s1 - s0
                cidx = d
                if not reuse_xc:
                    xci[0] += 1
                kx = xci[0] % NX
                cw = lambda j: vecs[:, V_CW + j * 8 + c:V_CW + j * 8 + c + 1]
                hba = hb2[:, d * 8 + c:d * 8 + c + 1]
                hbx = hb2[:, 16 + d * 8 + c:16 + d * 8 + c + 1]

                def ldf():
                    if xmode == "load":
                        S.dma("sp", xc[kx][:, :Sg], XCv[:, c, base + s0:base + s1], d_xcl[kx], rd=[xcres], wr=[R_xc[kx]])
                    elif not reuse_xc:
                        lo, hi = max(0, s0 - 2), min(L, s1 + 1)
                        if s0 - 2 < 0:
                            S.op("pool", lambda e: e.memset(xrb[k][:, 0:2], 0.0), wr=[R_xrb[k]])
                        if s1 + 1 > L:
                            S.op("pool", lambda e: e.memset(xrb[k][:, Sg + 2:Sg + 3], 0.0), wr=[R_xrb[k]])
                        S.dma("sp", xrb[k][:, lo - (s0 - 2):hi - (s0 - 2)], XRv[:, c, base + lo:base + hi], d_xrb[k],
                              wr=[R_xrb[k]])

                def s1f():
                    if not reuse_xc and xmode == "compute":
                        S.op("dve", lambda e: e.tensor_scalar(out=xc[kx][:, :Sg], in0=xrb[k][:, 0:Sg], scalar1=cw(0),
                                                              scalar2=vecs[:, V_CB + c:V_CB + c + 1], op0=ALU.mult, op1=ALU.add),
                             rd=[R_xrb[k], R_vec], wr=[R_xc[kx]])
                        for j in range(1, 4):
                            S.op("dve", lambda e, j=j: e.scalar_tensor_tensor(out=xc[kx][:, :Sg], in0=xrb[k][:, j:j + Sg],
                                                                              scalar=cw(j), in1=xc[kx][:, :Sg],
                                                                              op0=ALU.mult, op1=ALU.add),
                                 rd=[R_xrb[k], R_xc[kx], R_vec], wr=[R_xc[kx]])
                        if xcres is not None:
                            S.dma("pool", XCv[:, c, base + s0:base + s1], xc[kx][:, :Sg], d_xcs[kx], rd=[R_xc[kx]], wr=[xcres])
                    for a_, b_ in col_tiles(Sg, 512):
                        pb = pc[0] % 2
                        pc[0] += 1
                        S.op("pe", lambda e, a_=a_, b_=b_, pb=pb: e.matmul(rps[pb][:, :b_ - a_], wst[:, d * 2, c, :],
                                                                            xc[kx][:, a_:b_], start=True, stop=True),
                             rd=[R_wst, R_xc[kx]], wr=[R_rp[pb]])
                        S.op("pe", lambda e, a_=a_, b_=b_, pb=pb: e.matmul(ips[pb][:, :b_ - a_], wst[:, d * 2 + 1, c, :],
                                                                            xc[kx][:, a_:b_], start=True, stop=True),
                             rd=[R_wst, R_xc[kx]], wr=[R_ip[pb]])
                        S.op("act", lambda e, a_=a_, b_=b_, pb=pb: e.activation(out=rb[k][:, a_:b_], in_=rps[pb][:, :b_ - a_],
                                                                                 func=AF.Tanh, scale=0.5, bias=hba),
                             rd=[R_rp[pb], R_hb2], wr=[R_rb[k]])
                        S.op("act", lambda e, a_=a_, b_=b_, pb=pb: e.activation(out=ib[k][:, a_:b_], in_=ips[pb][:, :b_ - a_],
                                                                                 func=AF.Tanh, scale=0.5, bias=hbx),
                             rd=[R_ip[pb], R_hb2], wr=[R_ib[k]])
                    S.op("act", lambda e: e.activation(out=rb[k][:, :Sg], in_=rb[k][:, :Sg], func=AF.Exp,
                                                       scale=nsph[:, d * 8 + c:d * 8 + c + 1],
                                                       bias=nsph[:, d * 8 + c:d * 8 + c + 1]),
                         rd=[R_rb[k], R_hb2], wr=[R_rb[k]])
                    S.op("act", lambda e: e.activation(out=tb[k][:, :Sg], in_=rb[k][:, :Sg], func=AF.Square),
                         rd=[R_rb[k]], wr=[R_tb[k]])
                    S.op("act", lambda e: e.activation(out=tb[k][:, :Sg], in_=tb[k][:, :Sg], func=AF.Sqrt, scale=-0.25,
                                                       bias=quart[:, 0:1]),
                         rd=[R_tb[k], R_hb2], wr=[R_tb[k]])

                def pref():
                    if d == 1:
                        S.dma("sp", hfin[k][:, :Sg], HFv[:, c, base + s0:base + s1], d_hfin[k], rd=[hfres], wr=[R_hfin[k]])
                        S.dma("sp", gy[k][:, :Sg], GYv[:, c, base + s0:base + s1], d_gy[k], wr=[R_gy[k]])

                def s2f():
                    S.op("dve", lambda e: e.scalar_tensor_tensor(out=ib[k][:, :Sg], in0=ib[k][:, :Sg], scalar=1.0,
                                                                 in1=xc[kx][:, :Sg], op0=ALU.add, op1=ALU.mult),
                         rd=[R_ib[k], R_xc[kx]], wr=[R_ib[k]])
                    S.op("dve", lambda e: e.tensor_tensor(out=ib[k][:, :Sg], in0=ib[k][:, :Sg], in1=tb[k][:, :Sg], op=ALU.mult),
                         rd=[R_ib[k], R_tb[k]], wr=[R_ib[k]])
                    init = carry[cidx][:, 0:1] if has_carry else 0.0
                    rdc = [R_carry[cidx]] if has_carry else []
                    if d == 0:
                        S.op("dve", lambda e: e.tensor_tensor_scan(out=hb[k][:, :Sg], data0=rb[k][:, :Sg], data1=ib[k][:, :Sg],
                                                                   initial=init, op0=ALU.mult, op1=ALU.add),
                             rd=[R_rb[k], R_ib[k]] + rdc, wr=[R_hb[k]])
                        S.op("dve", lambda e: e.tensor_copy(out=carry[cidx][:, 0:1], in_=hb[k][:, Sg - 1:Sg]),
                             rd=[R_hb[k]], wr=[R_carry[cidx]])
                        S.dma("pool", HFv[:, c, base + s0:base + s1], hb[k][:, :Sg], d_hb[k], rd=[R_hb[k]], wr=[hfres])
                    else:
                        S.op("dve", lambda e: e.tensor_tensor_scan(out=hb[k][:, :Sg][:, ::-1], data0=rb[k][:, :Sg][:, ::-1],
                                                                   data1=ib[k][:, :Sg][:, ::-1], initial=init,
                                                                   op0=ALU.mult, op1=ALU.add),
                             rd=[R_rb[k], R_ib[k]] + rdc, wr=[R_hb[k]])
                        S.op("dve", lambda e: e.tensor_copy(out=carry[cidx][:, 0:1], in_=hb[k][:, 0:1]),
                             rd=[R_hb[k]], wr=[R_carry[cidx]])
                        S.op("pool", lambda e: e.tensor_tensor(out=hfin[k][:, :Sg], in0=hfin[k][:, :Sg], in1=hb[k][:, :Sg],
                                                               op=ALU.add),
                             rd=[R_hfin[k], R_hb[k]], wr=[R_hfin[k]])
                        S.op("pool", lambda e: e.tensor_tensor(out=ob[k][:, :Sg], in0=hfin[k][:, :Sg], in1=gy[k][:, :Sg],
                                                               op=ALU.mult),
                             rd=[R_hfin[k], R_gy[k]], wr=[R_ob[k]])
                        S.dma("pool", LTv[:, c, base + s0:base + s1], ob[k][:, :Sg], d_ob[k], rd=[R_ob[k]])
                jobs.append(dict(ld=ldf, s1=s1f, s2=s2f, pre=pref))

            for kind, idx, T, base in self.seqs:
                L = NMETA + T
                nseg = (L + SMAX - 1) // SMAX
                assert L % nseg == 0
                Sg = L // nseg
                segs = [(i * Sg, (i + 1) * Sg) for i in range(nseg)]
                for c in range(8):
                    hfres = Res()
                    xcres = Res() if nseg > 1 else None
                    for i, (s0, s1) in enumerate(segs):
                        make_job(len(jobs), base, L, c, s0, s1, 0, i != 0, False, hfres,
                                 xcres=xcres if i != nseg - 1 else None)
                    for i, (s0, s1) in reversed(list(enumerate(segs))):
                        if i == nseg - 1:
                            make_job(len(jobs), base, L, c, s0, s1, 1, False, True, hfres)
                        else:
                            make_job(len(jobs), base, L, c, s0, s1, 1, True, False, hfres, xcres=xcres, xmode="load")
            n = len(jobs)
            jobs[0]["ld"]()
            if n > 1:
                jobs[1]["ld"]()
            jobs[0]["s1"]()
            for i in range(n):
                if i + 2 < n:
                    jobs[i + 2]["ld"]()
                if i + 1 < n:
                    jobs[i + 1]["s1"]()
                if i == 0:
                    jobs[0]["pre"]()
                jobs[i]["s2"]()
                if i + 1 < n:
                    jobs[i + 1]["pre"]()
            S.replay()

    def phase_mixout(self, l):
        nc = self.nc
        with contextlib.ExitStack() as es:
            S = Sched(nc, es, f"mo{l}")
            wna = S.sb("wna", [128, 8, D], BF16)
            wlr = S.sb("wlr", [128, 8, D], BF16)
            wou = S.sb("wou", [128, 8, D], BF16)
            R_w, R_w2 = Res(), Res()
            dw, dw2 = S.dsem(), S.dsem()
            for dst, nm in ((wna, "wna"), (wlr, "wlru")):
                self.load_w(S, dst, self.w[nm][l], dw, R_w, 8, D)
            self.load_w(S, wou, self.w["wout"][l], dw2, R_w2, 8, D)
            NB = 2
            mk = lambda nm, dt: [S.sb(f"{nm}{i}", [128, 8, TT], dt) for i in range(NB)]
            x, nat, lt, sgn, sgl = mk("x", F32), mk("nat", BF16), mk("lt", BF16), mk("sgn", BF16), mk("sgl", BF16)
            R_x, R_nat, R_lt, R_sgn, R_sgl = ([Res() for _ in range(NB)] for _ in range(5))
            d_x, d_nat, d_lt, d_sgn, d_sgl, d_xs = ([S.dsem() for _ in range(NB)] for _ in range(6))
            mg = [S.sb(f"mg{i}", [128, 8, TT], BF16) for i in range(2)]
            R_mg = [[Res() for _ in range(8)] for _ in range(2)]
            t1 = [S.sb(f"t1{i}", [128, TT], F32) for i in range(2)]
            t2 = [S.sb(f"t2{i}", [128, TT], F32) for i in range(2)]
            R_t1, R_t2 = [Res(), Res()], [Res(), Res()]
            nps = [S.ps(f"n{i}", [128, TT]) for i in range(2)]
            lps = [S.ps(f"l{i}", [128, TT]) for i in range(2)]
            ops = [S.ps(f"o{i}", [128, TT]) for i in range(2)]
            R_n, R_l, R_o = [Res(), Res()], [Res(), Res()], [Res(), Res()]
            Hv = self.fm(self.H)
            tiles = self.tok_tiles()

            def loads(k):
                t0, t1_ = tiles[k]
                n = t1_ - t0
                b = k % NB
                for buf, R, ds, src in ((nat, R_nat, d_nat, self.NAT), (lt, R_lt, d_lt, self.LT),
                                        (sgn, R_sgn, d_sgn, self.SGN), (sgl, R_sgl, d_sgl, self.SGL), (x, R_x, d_x, self.H)):
                    S.dma("sp", buf[b][:, :, :n], self.fm(src)[:, :, t0:t1_], ds[b], wr=[R[b]])

            def stage_nl(k):
                t0, t1_ = tiles[k]
                n = t1_ - t0
                b = k % NB
                for c in range(8):
                    pb = c % 2

                    def mmn(e, c=c, pb=pb):
                        ins = None
                        for kk in range(8):
                            ins = e.matmul(nps[pb][:, :n], wna[:, kk, c * 128:(c + 1) * 128], nat[b][:, kk, :n],
                                           start=(kk == 0), stop=(kk == 7))
                        return ins

                    def mml(e, c=c, pb=pb):
                        ins = None
                        for kk in range(8):
                            ins = e.matmul(lps[pb][:, :n], wlr[:, kk, c * 128:(c + 1) * 128], lt[b][:, kk, :n],
                                           start=(kk == 0), stop=(kk == 7))
                        return ins
                    S.op("pe", mmn, rd=[R_w, R_nat[b]], wr=[R_n[pb]])
                    S.op("pe", mml, rd=[R_w, R_lt[b]], wr=[R_l[pb]])
                    S.op("dve", lambda e, c=c, pb=pb: e.tensor_tensor(out=t1[pb][:, :n], in0=nps[pb][:, :n],
                                                                      in1=sgn[b][:, c, :n], op=ALU.mult),
                         rd=[R_n[pb], R_sgn[b]], wr=[R_t1[pb]])
                    S.op("dve", lambda e, c=c, pb=pb: e.tensor_tensor(out=t2[pb][:, :n], in0=lps[pb][:, :n],
                                                                      in1=sgl[b][:, c, :n], op=ALU.mult),
                         rd=[R_l[pb], R_sgl[b]], wr=[R_t2[pb]])
                    S.op("pool", lambda e, c=c, pb=pb: e.tensor_tensor(out=mg[b][:, c, :n], in0=t1[pb][:, :n],
                                                                       in1=t2[pb][:, :n], op=ALU.add),
                         rd=[R_t1[pb], R_t2[pb]], wr=[R_mg[b][c]])

            def stage_o(k):
                t0, t1_ = tiles[k]
                n = t1_ - t0
                b = k % NB
                for c in range(8):
                    pb = c % 2

                    def mmo(e, c=c, pb=pb):
                        ins = None
                        for kk in range(8):
                            ins = e.matmul(ops[pb][:, :n], wou[:, kk, c * 128:(c + 1) * 128], mg[b][:, kk, :n],
                                           start=(kk == 0), stop=(kk == 7))
                        return ins
                    S.op("pe", mmo, rd=[R_w2] + R_mg[b], wr=[R_o[pb]])
                    S.op("dve", lambda e, c=c, pb=pb: e.tensor_tensor(out=x[b][:, c, :n], in0=ops[pb][:, :n],
                                                                      in1=x[b][:, c, :n], op=ALU.add),
                         rd=[R_o[pb], R_x[b]], wr=[R_x[b]])
                S.dma("pool", Hv[:, :, t0:t1_], x[b][:, :, :n], d_xs[b], rd=[R_x[b]])

            nt = len(tiles)
            loads(0)
            if nt > 1:
                loads(1)
            stage_nl(0)
            for k in range(nt):
                if k + 1 < nt:
                    stage_nl(k + 1)
                stage_o(k)
                if k + 2 < nt:
                    loads(k + 2)
            S.replay()

    def phase_final(self):
        nc = self.nc
        with contextlib.ExitStack() as es:
            S = Sched(nc, es, "fin")
            ones, R_ones, vecs, R_vec = self.consts(S, 0)
            idt = S.sb("idt", [128, 128], F32)
            R_id = Res()
            S.dma("sp", idt[:, :], self.ident, S.dsem(), wr=[R_id])
            x = [S.sb(f"x{i}", [128, 8, TT], F32) for i in range(2)]
            R_x, d_x = [Res(), Res()], [S.dsem(), S.dsem()]
            xn = [S.sb(f"xn{i}", [128, 8, TT], F32) for i in range(2)]
            R_xn = [Res(), Res()]
            yt = [S.sb(f"yt{i}", [128, 4, D], F32) for i in range(2)]
            R_yt, d_yt = [Res(), Res()], [S.dsem(), S.dsem()]
            sqs = [S.sb(f"sq{i}", [128, TT], BF16) for i in range(2)]
            R_sq = [Res(), Res()]
            sd = S.sb("sd", [128, TT], F32)
            rstd = S.sb("rstd", [128, TT], F32)
            R_sd, R_rstd = Res(), Res()
            ssq = S.ps("ssq", [128, TT])
            R_ssq = Res()
            tps = [S.ps(f"t{i}", [128, 512]) for i in range(4)]
            R_tp = [Res() for _ in range(4)]
            Hv = self.fm(self.H)
            jobs = []
            for kind, idx, T, base in self.seqs:
                dst = self.yp if kind == "p" else self.ys
                for a, b in col_tiles(T, TT):
                    jobs.append((dst[idx, a:b, :], base + NMETA + a, b - a))
            pc = 0
            pcc = [0]

            def ld(k):
                dst, t0, n = jobs[k]
                S.dma("sp", x[k % 2][:, :, :n], Hv[:, :, t0:t0 + n], d_x[k % 2], wr=[R_x[k % 2]])

            def st_a(k):
                dst, t0, n = jobs[k]
                b = k % 2
                self.norm(S, x[b], R_x[b], n, V_GF, vecs, R_vec, ones, sqs, R_sq, k, ssq, R_ssq, sd, R_sd,
                          rstd, R_rstd, xn[b], R_xn[b])

            def st_b(k):
                dst, t0, n = jobs[k]
                b = k % 2
                ng = n // 128
                for g in range(ng):
                    for hf in range(2):
                        pi = pcc[0] % 4
                        pcc[0] += 1
                        tp = tps[pi]

                        def tr(e, g=g, hf=hf, tp=tp, b=b):
                            ins = None
                            for j in range(4):
                                ins = e.transpose(tp[:, j * 128:(j + 1) * 128], xn[b][:, hf * 4 + j, g * 128:(g + 1) * 128], idt[:, :])
                            return ins
                        S.op("pe", tr, rd=[R_xn[b], R_id], wr=[R_tp[pi]])
                        if hf == 0:
                            S.op("dve", lambda e, g=g, hf=hf, tp=tp, b=b: e.tensor_copy(out=yt[b][:, g, hf * 512:(hf + 1) * 512], in_=tp[:, :]),
                                 rd=[R_tp[pi]], wr=[R_yt[b]])
                        else:
                            S.op("act", lambda e, g=g, hf=hf, tp=tp, b=b: e.activation(out=yt[b][:, g, hf * 512:(hf + 1) * 512], in_=tp[:, :], func=AF.Copy),
                                 rd=[R_tp[pi]], wr=[R_yt[b]])
                S.dma("pool", dst.rearrange("(g p) d -> p g d", p=128), yt[b][:, :ng, :], d_yt[b], rd=[R_yt[b]])

            nj = len(jobs)
            ld(0)
            st_a(0)
            for k in range(nj):
                if k + 1 < nj:
                    ld(k + 1)
                    st_a(k + 1)
                st_b(k)
            S.replay()

    def dump(self, name):
        if not self.debug:
            return
        nc = self.nc
        with contextlib.ExitStack() as es:
            S = Sched(nc, es, "dump" + name)
            S.dma("sp", self.dbg[name], self.H, S.dsem())
            S.replay()

    def build(self, stop_after=None):
        self.phase_init()
        self.dump("H0")
        for l in range(DEPTH):
            self.phase_ffn(l, 1)
            if l == 0:
                self.dump("H1")
            self.phase_mixin(l)
            self.phase_attn(l)
            self.phase_lru(l)
            self.phase_mixout(l)
            if l == 0:
                self.dump("H2")
            self.phase_ffn(l, 2)
            if l == 0:
                self.dump("H3")
        self.phase_final()
        return self.nc


def pack_vec(inp, l):
    v = np.zeros((128, NV), np.float32)
    pc = lambda a: np.ascontiguousarray(a.reshape(8, 128).T)
    v[:, V_G1:V_G1 + 8] = pc(inp["norm_ffn1"][l])
    v[:, V_GM:V_GM + 8] = pc(inp["norm_mix"][l])
    v[:, V_G2:V_G2 + 8] = pc(inp["norm_ffn2"][l])
    for j in range(4):
        v[:, V_CW + j * 8:V_CW + j * 8 + 8] = pc(inp["conv_w"][l, j])
    v[:, V_CB:V_CB + 8] = pc(inp["conv_b"][l])
    for d in range(2):
        v[:, V_BA + d * 8:V_BA + d * 8 + 8] = pc(inp["lru_ba"][l, d])
        v[:, V_BX + d * 8:V_BX + d * 8 + 8] = pc(inp["lru_bx"][l, d])
        v[:, V_LAM + d * 8:V_LAM + d * 8 + 8] = pc(inp["lru_lambda"][l, d])
    v[:, V_GF:V_GF + 8] = pc(inp["final_norm"])
    return v


def shared_inputs(inp):
    f = lambda a: np.ascontiguousarray(np.asarray(a, dtype=np.float32))
    m = dict(
        meta=f(inp["meta_tokens"]), ident=np.eye(128, dtype=np.float32),
        vec=np.stack([pack_vec(inp, l) for l in range(DEPTH)]),
        fbias=f(np.asarray(inp["na_rel_bias"])[:, :, ::-1, ::-1]),
        f1g=f(inp["ffn1_w_gate"]), f1u=f(inp["ffn1_w_up"]), f1d=f(inp["ffn1_w_down"]),
        win=f(inp["w_in"]), wna=f(inp["w_na_proj"]), wlru=f(inp["w_lru_proj"]), wout=f(inp["w_out"]),
        f2g=f(inp["ffn2_w_gate"]), f2u=f(inp["ffn2_w_up"]), f2d=f(inp["ffn2_w_down"]),
        lwa=f(inp["lru_wa"]), lwx=f(inp["lru_wx"]),
    )
    return m


def run(inp, n_cores, debug=False):
    inp = {k: np.asarray(v) for k, v in inp.items()}
    xp, xs = inp["x_prompt"], inp["x_sample"]
    n_p, t_p = xp.shape[0] // n_cores, xp.shape[1]
    n_s, t_s = xs.shape[0] // n_cores, xs.shape[1]
    bld = Builder(n_p, t_p, n_s, t_s, debug=debug)
    nc = bld.build()
    sh = shared_inputs(inp)
    in_maps = []
    for i in range(n_cores):
        m = dict(sh)
        m["xp"] = np.ascontiguousarray(xp[i * n_p:(i + 1) * n_p])
        m["xs"] = np.ascontiguousarray(xs[i * n_s:(i + 1) * n_s])
        in_maps.append(m)
    res = run_bass_kernel_spmd(nc, in_maps, core_ids=list(range(n_cores)))
    yp = np.concatenate([r["yp"] for r in res.results], axis=0)
    ys = np.concatenate([r["ys"] for r in res.results], axis=0)
    return (yp, ys), res, bld


def kernel(**inputs):
    (yp, ys), _, _ = run(inputs, NCORES)
    return (yp.astype(np.float32), ys.astype(np.float32))
```
